# Optimizing a Trainium2 kernel written in Bass

```python
import math
import jax
import jax.numpy as jnp
from jax import lax
import numpy as np

D_MODEL = 1024
BATCH = 8
SEQ = 4096
DEPTH = 2

CTX_LEN = 256
GRID_W = 64
N_BRANCH = 4
BR_W = 512
EPS = 1e-6
CONV_W = 4
DT_MIN = 1e-3
DT_MAX = 1e-1

HG_HEADS = 4
HG_DK = BR_W // HG_HEADS
HG_CHUNK = 64
S5_GROUP = 16
S5_GROUPS = BR_W // S5_GROUP
S5_STATE = 64
LRU_BLOCKS = 8
LRU_BW = BR_W // LRU_BLOCKS
LRU_C = 8.0
M2_HEADDIM = 64
M2_HEADS = BR_W // M2_HEADDIM
M2_GROUPS = 2
M2_HPG = M2_HEADS // M2_GROUPS
M2_STATE = 64
M2_CHUNK = 64
M2_XBC = BR_W + 2 * M2_GROUPS * M2_STATE

IN_SIZES = (BR_W,) * 9 + (M2_XBC, 2 * M2_HEADS, BR_W)
IN_SPLITS = tuple(sum(IN_SIZES[:i + 1]) for i in range(len(IN_SIZES) - 1))
IN_COLS = sum(IN_SIZES)

kernel_name = 'hybrid_gated_recurrent_flow_block'


def _rms(x, w):
    xf = x.astype(jnp.float32)
    y = xf * lax.rsqrt(jnp.mean(xf * xf, axis=-1, keepdims=True) + EPS)
    return (y * w.astype(jnp.float32)).astype(x.dtype)


def _lin_comb(e1, e2):
    a1, b1 = e1
    a2, b2 = e2
    return a1 * a2, a2 * b1 + b2


def _cat_f(c, l):
    return jnp.concatenate([c, l], axis=1)


def _cat_b(c, l):
    return jnp.flip(jnp.concatenate([l, c], axis=1), axis=1)


def _uncat_b(y, n_ctx):
    y = jnp.flip(y, axis=1)
    n_lat = y.shape[1] - n_ctx
    return jnp.concatenate([y[:, n_lat:], y[:, :n_lat]], axis=1)


def _f_to_b(y, n_ctx):
    return _cat_b(y[:, :n_ctx], y[:, n_ctx:])


def _dwconv(x, w, b):
    n = x.shape[-2]
    lo = (CONV_W - 1) // 2
    pad = [(0, 0)] * (x.ndim - 2) + [(lo, CONV_W - 1 - lo), (0, 0)]
    xp = jnp.pad(x, pad)
    out = b
    for k in range(CONV_W):
        out = out + w[k] * xp[..., k:k + n, :]
    return out


def _short_conv(xc, xl, w, b):
    bsz, n_lat, ch = xl.shape
    rows = n_lat // GRID_W
    yl = _dwconv(xl.reshape(bsz, rows, GRID_W, ch), w, b).reshape(bsz, n_lat, ch)
    return _dwconv(xc, w, b), yl


def _gla_chunked(q, k, v, logf):
    bsz, T, H, _ = q.shape
    n = T // HG_CHUNK

    def r(a):
        return a.reshape(bsz, n, HG_CHUNK, H, a.shape[-1])
    q, k, v, logf = r(q), r(k), r(v), r(logf)
    b = jnp.cumsum(logf, axis=2)
    b_end = b[:, :, -1:]
    mid = 0.5 * b_end
    att = jnp.einsum('bnihk,bnjhk->bnhij', q * jnp.exp(b - mid), k * jnp.exp(mid - b))
    mask = jnp.tril(jnp.ones((HG_CHUNK, HG_CHUNK), bool))
    att = jnp.where(mask, att, 0.0)
    o_intra = jnp.einsum('bnhij,bnjhv->bnihv', att, v)
    chunk_kv = jnp.einsum('bnjhk,bnjhv->bnhkv', k * jnp.exp(b_end - b), v)
    decay = jnp.exp(b_end[:, :, 0])

    def step(s, inp):
        kv, dec = inp
        return dec[..., None] * s + kv, s
    s0 = jnp.zeros((bsz, H, k.shape[-1], v.shape[-1]), jnp.float32)
    _, s_prev = lax.scan(step, s0, (jnp.moveaxis(chunk_kv, 1, 0), jnp.moveaxis(decay, 1, 0)))
    s_prev = jnp.moveaxis(s_prev, 0, 1)
    o_inter = jnp.einsum('bnihk,bnhkv->bnihv', q * jnp.exp(b), s_prev)
    return (o_intra + o_inter).reshape(bsz, T, H, v.shape[-1])


def _hgrn2(ctx_in, lat_in, lb, norm_w):
    (cq, ci, cff, cfb, cz), (lq, li, lff, lfb, lz) = ctx_in, lat_in
    n_ctx = cq.shape[1]
    f32 = jnp.float32

    def heads(a):
        return a.reshape(a.shape[0], a.shape[1], HG_HEADS, HG_DK)

    def run(q, i, f_raw, lbd):
        f = lbd + (1.0 - lbd) * jax.nn.sigmoid(f_raw.astype(f32))
        return _gla_chunked(heads(jax.nn.silu(q.astype(f32))), heads(1.0 - f),
                            heads(i.astype(f32)), heads(jnp.log(f)))
    o = (run(_cat_f(cq, lq), _cat_f(ci, li), _cat_f(cff, lff), lb[0])
         + _uncat_b(run(_cat_b(cq, lq), _cat_b(ci, li), _cat_b(cfb, lfb), lb[1]), n_ctx))
    o = _rms(o, norm_w.reshape(HG_HEADS, HG_DK))
    o = o.reshape(o.shape[0], o.shape[1], BR_W)
    return o * jax.nn.silu(_cat_f(cz, lz).astype(f32))


def _s5(ctx_in, lat_in, a_re, a_im, log_step, b_re, b_im, c_re, c_im, d_skip, w_glu, b_glu):
    (cu, cz), (lu, lz) = ctx_in, lat_in
    n_ctx = cu.shape[1]
    f32 = jnp.float32
    u = _cat_f(cu, lu).astype(f32)
    bsz, T, _ = u.shape
    ug = u.reshape(bsz, T, S5_GROUPS, S5_GROUP).astype(jnp.complex64)
    b_mat = lax.complex(b_re.astype(f32), b_im.astype(f32))
    c_mat = lax.complex(c_re.astype(f32), c_im.astype(f32))
    bu = jnp.einsum('gnp,btgp->btgn', b_mat, ug)

    def run(bu_seq, d):
        lam = lax.complex(a_re[d].astype(f32), a_im[d].astype(f32))
        step = jnp.exp(log_step[d].astype(f32))[:, None]
        a_bar = jnp.exp(lam * step)
        drive = ((a_bar - 1.0) / lam) * bu_seq
        a_seq = jnp.broadcast_to(a_bar, (1, T) + a_bar.shape)
        _, s = lax.associative_scan(_lin_comb, (a_seq, drive), axis=1)
        return jnp.real(jnp.einsum('gpn,btgn->btgp', c_mat, s))
    y = run(bu, 0) + _uncat_b(run(_f_to_b(bu, n_ctx), 1), n_ctx)
    y = y.reshape(bsz, T, BR_W) + d_skip.astype(f32) * u
    g = jax.nn.gelu(y)
    y = g * jax.nn.sigmoid(g @ w_glu.astype(f32) + b_glu.astype(f32))
    return y * jax.nn.silu(_cat_f(cz, lz).astype(f32))


def _rglru(ctx_in, lat_in, conv_w, conv_b, gate_w, gate_b, lam):
    (cx, cz), (lx, lz) = ctx_in, lat_in
    n_ctx = cx.shape[1]
    f32 = jnp.float32
    cx, lx = _short_conv(cx, lx, conv_w, conv_b)

    def run(xs, d):
        xf = xs.astype(f32)
        bsz, T, _ = xf.shape
        xb = xf.reshape(bsz, T, LRU_BLOCKS, LRU_BW)
        gates = jax.nn.sigmoid(jnp.einsum('btnc,gncd->gbtnd', xb, gate_w[d].astype(f32))
                               + gate_b[d].astype(f32)[:, None, None])
        r = gates[0].reshape(bsz, T, BR_W)
        i = gates[1].reshape(bsz, T, BR_W)
        log_a = -LRU_C * r * jax.nn.softplus(-lam[d].astype(f32))
        b = jnp.sqrt(-jnp.expm1(2.0 * log_a)) * (i * xf)
        _, h = lax.associative_scan(_lin_comb, (jnp.exp(log_a), b), axis=1)
        return h
    h = run(_cat_f(cx, lx), 0) + _uncat_b(run(_cat_b(cx, lx), 1), n_ctx)
    return h * jax.nn.silu(_cat_f(cz, lz).astype(f32))


def _ssd_chunked(x, dt, a, bm, cm):
    bsz, T = x.shape[:2]
    n = T // M2_CHUNK

    def r(t):
        return t.reshape((bsz, n, M2_CHUNK) + t.shape[2:])
    x, dt, bm, cm = r(x), r(dt), r(bm), r(cm)
    cum = jnp.cumsum(dt * a, axis=2)
    seg = cum[:, :, :, None] - cum[:, :, None, :]
    mask = jnp.tril(jnp.ones((M2_CHUNK, M2_CHUNK), bool))[:, :, None, None]
    decay_ij = jnp.exp(jnp.where(mask, seg, -jnp.inf))
    scores = jnp.einsum('bcigs,bcjgs->bcijg', cm, bm)
    w = scores[..., None] * decay_ij * dt[:, :, None]
    y_intra = jnp.einsum('bcijgr,bcjgrp->bcigrp', w, x)
    cum_end = cum[:, :, -1]
    wx = (jnp.exp(cum_end[:, :, None] - cum) * dt)[..., None] * x
    chunk_state = jnp.einsum('bcjgs,bcjgrp->bcgrps', bm, wx)

    def step(s, inp):
        st, dec = inp
        return dec[..., None, None] * s + st, s
    s0 = jnp.zeros(chunk_state.shape[:1] + chunk_state.shape[2:], jnp.float32)
    _, s_prev = lax.scan(step, s0, (jnp.moveaxis(chunk_state, 1, 0), jnp.moveaxis(jnp.exp(cum_end), 1, 0)))
    s_prev = jnp.moveaxis(s_prev, 0, 1)
    y_inter = jnp.einsum('bcigs,bcgrps->bcigrp', cm, s_prev) * jnp.exp(cum)[..., None]
    return (y_intra + y_inter).reshape(bsz, T, M2_GROUPS, M2_HPG, M2_HEADDIM)


def _mamba2(ctx_in, lat_in, conv_w, conv_b, dt_bias, a_log, d_skip, norm_w):
    (cxbc, cdt, cz), (lxbc, ldt, lz) = ctx_in, lat_in
    n_ctx = cxbc.shape[1]
    f32 = jnp.float32
    cxbc, lxbc = _short_conv(cxbc, lxbc, conv_w, conv_b)
    cxbc, lxbc = jax.nn.silu(cxbc.astype(f32)), jax.nn.silu(lxbc.astype(f32))
    gn = M2_GROUPS * M2_STATE

    def run(xbc, dt_raw, d):
        bsz, T, _ = xbc.shape
        xs = xbc[..., :BR_W].reshape(bsz, T, M2_GROUPS, M2_HPG, M2_HEADDIM)
        bm = xbc[..., BR_W:BR_W + gn].reshape(bsz, T, M2_GROUPS, M2_STATE)
        cm = xbc[..., BR_W + gn:].reshape(bsz, T, M2_GROUPS, M2_STATE)
        dt = jax.nn.softplus(dt_raw.astype(f32) + dt_bias[d].astype(f32)).reshape(bsz, T, M2_GROUPS, M2_HPG)
        a = -jnp.exp(a_log[d].astype(f32)).reshape(M2_GROUPS, M2_HPG)
        return _ssd_chunked(xs, dt, a, bm, cm)
    xbc_f = _cat_f(cxbc, lxbc)
    y = (run(xbc_f, _cat_f(cdt[..., :M2_HEADS], ldt[..., :M2_HEADS]), 0)
         + _uncat_b(run(_cat_b(cxbc, lxbc), _cat_b(cdt[..., M2_HEADS:], ldt[..., M2_HEADS:]), 1), n_ctx))
    bsz, T, _ = xbc_f.shape
    skip = d_skip.astype(f32)[:, None] * xbc_f[..., :BR_W].reshape(bsz, T, M2_HEADS, M2_HEADDIM)
    y = y.reshape(bsz, T, BR_W) + skip.reshape(bsz, T, BR_W)
    return _rms(y * jax.nn.silu(_cat_f(cz, lz).astype(f32)), norm_w)


def _merge(h, ys, w_gate, b_gate, w_branch):
    out = jax.nn.sigmoid(h @ w_gate[0] + b_gate[0]) * (ys[0].astype(h.dtype) @ w_branch[0])
    for k in range(1, N_BRANCH):
        out = out + jax.nn.sigmoid(h @ w_gate[k] + b_gate[k]) * (ys[k].astype(h.dtype) @ w_branch[k])
    return out


def setup_inputs(seed: int = 0) -> dict:
    key = jax.random.key(seed)
    ks = iter(jax.random.split(key, 48))
    f32 = jnp.float32

    def nrm(shape, s=1.0):
        return s * jax.random.normal(next(ks), shape, f32)

    def uni(shape, lo, hi):
        return jax.random.uniform(next(ks), shape, f32, lo, hi)
    L = DEPTH
    dt_m2 = jnp.exp(uni((L, 2, M2_HEADS), math.log(DT_MIN), math.log(DT_MAX)))
    lru_a = uni((L, 2, BR_W), 0.9, 0.999) ** (1.0 / LRU_C)
    return {
        'x': nrm((BATCH, SEQ, D_MODEL)),
        'c': nrm((BATCH, D_MODEL)),
        'ctx': nrm((BATCH, CTX_LEN, D_MODEL)),
        'c_ctx': nrm((D_MODEL,)),
        'norm_w': 1.0 + nrm((L, D_MODEL), 0.02),
        'w_mod': nrm((L, D_MODEL, 3 * D_MODEL), 0.5 * D_MODEL ** -0.5),
        'b_mod': nrm((L, 3 * D_MODEL), 0.01),
        'w_in': nrm((L, D_MODEL, IN_COLS), D_MODEL ** -0.5),
        'hg_lb_logits': nrm((L + 1, 2, BR_W), 0.1),
        'hg_norm': 1.0 + nrm((L, BR_W), 0.02),
        's5_a_re': -0.5 + nrm((L, 2, S5_GROUPS, S5_STATE), 0.01),
        's5_a_im': math.pi * jnp.arange(S5_STATE, dtype=f32) + nrm((L, 2, S5_GROUPS, S5_STATE), 0.01),
        's5_log_step': uni((L, 2, S5_GROUPS), math.log(DT_MIN), math.log(DT_MAX)),
        's5_b_re': nrm((L, S5_GROUPS, S5_STATE, S5_GROUP), (2 * S5_GROUP) ** -0.5),
        's5_b_im': nrm((L, S5_GROUPS, S5_STATE, S5_GROUP), (2 * S5_GROUP) ** -0.5),
        's5_c_re': nrm((L, S5_GROUPS, S5_GROUP, S5_STATE), S5_STATE ** -0.5),
        's5_c_im': nrm((L, S5_GROUPS, S5_GROUP, S5_STATE), S5_STATE ** -0.5),
        's5_d': nrm((L, BR_W)),
        's5_w_glu': nrm((L, BR_W, BR_W), BR_W ** -0.5),
        's5_b_glu': nrm((L, BR_W), 0.01),
        'lru_conv_w': nrm((L, CONV_W, BR_W), CONV_W ** -0.5),
        'lru_conv_b': nrm((L, BR_W), 0.01),
        'lru_gate_w': nrm((L, 2, 2, LRU_BLOCKS, LRU_BW, LRU_BW), LRU_BW ** -0.5),
        'lru_gate_b': nrm((L, 2, 2, LRU_BLOCKS, LRU_BW), 0.01),
        'lru_lam': jnp.log(lru_a) - jnp.log1p(-lru_a),
        'm2_conv_w': nrm((L, CONV_W, M2_XBC), CONV_W ** -0.5),
        'm2_conv_b': nrm((L, M2_XBC), 0.01),
        'm2_dt_bias': dt_m2 + jnp.log(-jnp.expm1(-dt_m2)),
        'm2_a_log': jnp.log(uni((L, 2, M2_HEADS), 1.0, 16.0)),
        'm2_d': 1.0 + nrm((L, M2_HEADS), 0.02),
        'm2_norm': 1.0 + nrm((L, BR_W), 0.02),
        'w_branch': nrm((L, N_BRANCH, BR_W, D_MODEL), BR_W ** -0.5),
        'w_gate': nrm((L, N_BRANCH, D_MODEL, D_MODEL), D_MODEL ** -0.5),
        'b_gate': nrm((L, N_BRANCH, D_MODEL), 0.01),
        'w_out': nrm((L, D_MODEL, D_MODEL), D_MODEL ** -0.5),
        'final_norm': 1.0 + nrm((D_MODEL,), 0.02),
    }


def reference(x, c, ctx, c_ctx, norm_w, w_mod, b_mod, w_in, hg_lb_logits, hg_norm,
              s5_a_re, s5_a_im, s5_log_step, s5_b_re, s5_b_im, s5_c_re, s5_c_im, s5_d, s5_w_glu, s5_b_glu,
              lru_conv_w, lru_conv_b, lru_gate_w, lru_gate_b, lru_lam,
              m2_conv_w, m2_conv_b, m2_dt_bias, m2_a_log, m2_d, m2_norm,
              w_branch, w_gate, b_gate, w_out, final_norm):
    n_ctx = ctx.shape[1]
    lb_all = jnp.cumsum(jax.nn.softmax(hg_lb_logits.astype(jnp.float32), axis=0), axis=0)
    for l in range(DEPTH):
        mod = jax.nn.silu(c) @ w_mod[l] + b_mod[l]
        mod_c = jax.nn.silu(c_ctx) @ w_mod[l] + b_mod[l]
        sh, sc, gt = jnp.split(mod[:, None, :], 3, axis=-1)
        sh_c, sc_c, gt_c = jnp.split(mod_c, 3, axis=-1)
        h = _rms(x, norm_w[l]) * (1.0 + sc) + sh
        hc = _rms(ctx, norm_w[l]) * (1.0 + sc_c) + sh_c
        u = jnp.split(h @ w_in[l], IN_SPLITS, axis=-1)
        uc = jnp.split(hc @ w_in[l], IN_SPLITS, axis=-1)
        ys = (
            _hgrn2(uc[0:5], u[0:5], lb_all[l], hg_norm[l]),
            _s5(uc[5:7], u[5:7], s5_a_re[l], s5_a_im[l], s5_log_step[l], s5_b_re[l], s5_b_im[l],
                s5_c_re[l], s5_c_im[l], s5_d[l], s5_w_glu[l], s5_b_glu[l]),
            _rglru(uc[7:9], u[7:9], lru_conv_w[l], lru_conv_b[l], lru_gate_w[l], lru_gate_b[l], lru_lam[l]),
            _mamba2(uc[9:12], u[9:12], m2_conv_w[l], m2_conv_b[l], m2_dt_bias[l], m2_a_log[l], m2_d[l], m2_norm[l]),
        )
        x_new = x + gt * (_merge(h, [y[:, n_ctx:] for y in ys], w_gate[l], b_gate[l], w_branch[l]) @ w_out[l])
        if l < DEPTH - 1:
            ctx = ctx + gt_c * (_merge(hc, [y[:, :n_ctx] for y in ys], w_gate[l], b_gate[l], w_branch[l]) @ w_out[l])
        x = x_new
    return _rms(x, final_norm)
```

```python
import os
import contextlib
import numpy as np
import concourse.bass as bass
import concourse.mybir as mybir
from concourse.bass_utils import run_bass_kernel_spmd
from concourse.ap import AP

F32 = mybir.dt.float32
BF16 = mybir.dt.bfloat16
AF = mybir.ActivationFunctionType
ALU = mybir.AluOpType

T = 4352
NCTX = 256
NLAT = 4096
D = 1024
NT = 34
EPS = 1e-6
PI = float(np.pi)
C_Q, C_I, C_FF, C_FB, C_HZ, C_SU, C_SZ, C_LX, C_LZ, C_XBC, C_DT, C_MZ = (
    0, 512, 1024, 1536, 2048, 2560, 3072, 3584, 4096, 4608, 5376, 5392)
IN_COLS = 5904
SL = 256
SLABS = [(0, 256)] + [(256 + 512 * i, 512) for i in range(8)]
SLABS256 = [(256 * i, 256) for i in range(17)]


def rev(ap):
    pat = [list(p) for p in ap.ap]
    step, cnt = pat[-1]
    off = ap.offset + step * (cnt - 1)
    pat[-1] = [-step, cnt]
    return AP(ap.tensor, off, pat)


class S:
    def __init__(self, ap, sub):
        self.ap = ap
        self.sub = sub


def _ap(x):
    return x.ap if isinstance(x, S) else x


def _key(x):
    if isinstance(x, S):
        nm = getattr(x.ap, 'tensor', x.ap).name
        if nm.startswith('ps'):
            return (nm, None)
        return (nm, x.sub)
    return (getattr(x, 'tensor', x).name, None)


class Sched:
    def __init__(self, nc, stack):
        self.nc = nc
        self.eng = {'pe': nc.tensor, 'dve': nc.vector, 'act': nc.scalar, 'pool': nc.gpsimd, 'sp': nc.sync}
        self.cnt = {e: 0 for e in self.eng}
        self.EPOCH = 20000
        self.sems = {e: [] for e in self.eng}
        self.stack = stack
        self.seen = {e: {} for e in self.eng}
        self.lastw = {}
        self.readers = {}
        self.nsem = 0
        self.dma_slots = {}
        self.dma_next = {}
        for q in ('sp', 'pool', 'act'):
            n = 12 if q == 'sp' else 6
            self.dma_slots[q] = [[self._newsem(), 0] for _ in range(n)]
            self.dma_next[q] = 0
        self.ninst = 0
        self.nopool = False

    def _newsem(self):
        s = self.stack.enter_context(self.nc.semaphore("s%d" % self.nsem))
        self.nsem += 1
        return (self.nsem, s)

    def _token(self, e):
        n = self.cnt[e]
        ep = n // self.EPOCH
        while len(self.sems[e]) <= ep:
            self.sems[e].append(self._newsem())
        sid, sem = self.sems[e][ep]
        self.cnt[e] = n + 1
        return (sid, sem, n - ep * self.EPOCH + 1, e)

    def _wait(self, e, toks, pesync=False):
        best = {}
        for t in toks:
            if t is None:
                continue
            if e == 'pe' and t[3] == 'pe' and not pesync:
                continue
            if t[0] not in best or best[t[0]][2] < t[2]:
                best[t[0]] = t
        for sid, t in best.items():
            if self.seen[e].get(sid, 0) < t[2]:
                self.eng[e].wait_ge(t[1], t[2])
                self.seen[e][sid] = t[2]

    @staticmethod
    def _psplit(outs, ins):
        o2 = list(outs)
        i2 = []
        for x in ins:
            if _key(x)[0].startswith('ps'):
                o2.append(x)
            else:
                i2.append(x)
        return o2, i2

    def _deps(self, outs, ins):
        outs, ins = self._psplit(outs, ins)
        toks = []
        for x in ins:
            toks.append(self.lastw.get(_key(x)))
        for x in outs:
            k = _key(x)
            toks.append(self.lastw.get(k))
            toks.extend(self.readers.get(k, {}).values())
        return toks

    def _record(self, tok, outs, ins):
        outs, ins = self._psplit(outs, ins)
        for x in ins:
            self.readers.setdefault(_key(x), {})[tok[0]] = tok
        for x in outs:
            k = _key(x)
            self.lastw[k] = tok
            self.readers[k] = {}

    def op(self, e, outs, ins, fn, pesync=False):
        self._wait(e, self._deps(outs, ins), pesync)
        inst = fn(self.eng[e])
        tok = self._token(e)
        inst.then_inc(tok[1], 1)
        self._record(tok, outs, ins)
        self.ninst += 1
        return inst

    def dma(self, q, out, in_, **kw):
        slots = self.dma_slots[q]
        i = self.dma_next[q]
        self.dma_next[q] = (i + 1) % len(slots)
        (sid, sem), val = slots[i]
        toks = self._deps([out], [in_])
        if val > 0:
            toks.append((sid, sem, val, 'dma'))
        self._wait(q, toks)
        self.eng[q].dma_start(out=_ap(out), in_=_ap(in_), **kw).then_inc(sem, 16)
        slots[i][1] = val + 16
        tok = (sid, sem, val + 16, 'dma')
        self._record(tok, [out], [in_])
        self.ninst += 1

    def barrier(self):
        toks = []
        for e in self.eng:
            n = self.cnt[e]
            if n == 0:
                continue
            ep = (n - 1) // self.EPOCH
            sid, sem = self.sems[e][ep]
            toks.append((sid, sem, n - ep * self.EPOCH, e + '_b'))
        for q, slots in self.dma_slots.items():
            for (sid, sem), val in slots:
                if val > 0:
                    toks.append((sid, sem, val, 'dma'))
        for e in self.eng:
            self._wait(e, toks)
        self.lastw = {}
        self.readers = {}

    def tt(self, out, in0, in1, op, e='dve'):
        if e == 'pool' and self.nopool:
            e = 'dve'
        return self.op(e, [out], [in0, in1], lambda g: g.tensor_tensor(out=_ap(out), in0=_ap(in0), in1=_ap(in1), op=op))

    def ts(self, out, in0, s1, op0, s2=None, op1=None, e='dve'):
        ins = [in0] + [s for s in (s1, s2) if isinstance(s, (AP, S))]
        if op1 is None:
            return self.op(e, [out], ins, lambda g: g.tensor_scalar(out=_ap(out), in0=_ap(in0), scalar1=_ap(s1) if isinstance(s1, (AP, S)) else s1, scalar2=None, op0=op0))
        return self.op(e, [out], ins, lambda g: g.tensor_scalar(out=_ap(out), in0=_ap(in0), scalar1=_ap(s1) if isinstance(s1, (AP, S)) else s1,
                                                                 scalar2=_ap(s2) if isinstance(s2, (AP, S)) else s2, op0=op0, op1=op1))

    def stt(self, out, in0, sc, in1, op0, op1, e='dve'):
        ins = [in0, in1] + ([sc] if isinstance(sc, (AP, S)) else [])
        return self.op(e, [out], ins, lambda g: g.scalar_tensor_tensor(out=_ap(out), in0=_ap(in0), scalar=_ap(sc) if isinstance(sc, (AP, S)) else sc,
                                                                        in1=_ap(in1), op0=op0, op1=op1))

    def act(self, out, in_, func, bias=None, scale=None, accum=None):
        ins = [in_] + [s for s in (bias, scale) if isinstance(s, (AP, S))]
        outs = [out] + ([accum] if accum is not None else [])
        kw = {}
        if bias is not None:
            kw['bias'] = _ap(bias) if isinstance(bias, (AP, S)) else bias
        if scale is not None:
            kw['scale'] = _ap(scale) if isinstance(scale, (AP, S)) else scale
        if accum is not None:
            kw['accum_out'] = _ap(accum)
        return self.op('act', outs, ins, lambda g: g.activation(out=_ap(out), in_=_ap(in_), func=func, **kw))

    def rsqrt(self, out, in_, scale):
        self.act(out, in_, AF.Ln, bias=self.epsc, scale=scale)
        return self.act(out, out, AF.Exp, scale=-0.5)

    def sinred(self, out, x, shift, tmp, negpi):
        MAGIC = 12582912.0
        C1 = 6.28125
        C2 = float(2 * np.pi - 6.28125)
        xs = x
        if shift != 0.0:
            self.ts(out, x, float(shift), ALU.add)
            xs = out
        self.ts(tmp, xs, 1.0 / (2 * np.pi), ALU.mult, MAGIC, ALU.add)
        self.ts(tmp, tmp, -MAGIC, ALU.add)
        self.stt(out, tmp, -C1, xs, ALU.mult, ALU.add)
        self.stt(out, tmp, -C2, out, ALU.mult, ALU.add)
        self.ts(out, out, PI, ALU.min, -PI, ALU.max)
        return self.act(out, out, AF.Sin)

    def recip(self, out, in_):
        return self.op('dve', [out], [in_], lambda g: g.reciprocal(out=_ap(out), in_=_ap(in_)))

    def copy(self, out, in_, e='dve'):
        if e == 'act':
            return self.act(out, in_, AF.Copy)
        return self.op(e, [out], [in_], lambda g: g.tensor_copy(out=_ap(out), in_=_ap(in_)))

    def memset(self, out, val, e='dve'):
        return self.op(e, [out], [], lambda g: g.memset(_ap(out), val))

    def mm(self, out, lhsT, rhs, start=True, stop=True):
        return self.op('pe', [out], [lhsT, rhs], lambda g: g.matmul(_ap(out), lhsT=_ap(lhsT), rhs=_ap(rhs), start=start, stop=stop))

    def tr(self, out, in_, ident):
        return self.op('pe', [out], [in_, ident], lambda g: g.transpose(_ap(out), _ap(in_), _ap(ident)))

    def scan(self, out, d0, d1, init, op0=ALU.mult, op1=ALU.add):
        ins = [d0, d1] + ([init] if isinstance(init, (AP, S)) else [])
        return self.op('dve', [out], ins, lambda g: g.tensor_tensor_scan(_ap(out), _ap(d0), _ap(d1), _ap(init) if isinstance(init, (AP, S)) else init, op0, op1))


class ColPack:
    def __init__(self):
        self.parts = []
        self.off = {}
        self.n = 0

    def add(self, name, vec):
        v = np.asarray(vec, np.float32).reshape(-1, 128).T
        self.off[name] = (self.n, v.shape[1])
        self.parts.append(v)
        self.n += v.shape[1]

    def add_raw(self, name, arr):
        arr = np.asarray(arr, np.float32)
        self.off[name] = (self.n, arr.shape[1])
        self.parts.append(arr)
        self.n += arr.shape[1]

    def array(self):
        return np.ascontiguousarray(np.concatenate(self.parts, axis=1))


class RowPack:
    def __init__(self):
        self.parts = []
        self.off = {}
        self.n = 0

    def add(self, name, vec):
        v = np.asarray(vec, np.float32).reshape(-1)
        self.off[name] = (self.n, v.size)
        self.parts.append(v)
        self.n += v.size

    def array(self):
        v = np.concatenate(self.parts)
        return np.ascontiguousarray(np.broadcast_to(v[None, :], (128, v.size)))


def const_mats():
    j = np.arange(128)[:, None]
    i = np.arange(128)[None, :]
    same = (j // 64) == (i // 64)
    ident = (j == i).astype(np.float32)
    tri_f = ((j <= i) & same).astype(np.float32)
    tri_b = ((j >= i) & same).astype(np.float32)
    su_f = ((j > i) & same).astype(np.float32)
    su_b = ((j < i) & same).astype(np.float32)
    ones = same.astype(np.float32)
    e2m_f = tri_f - 0.5 * ones
    e2m_b = tri_b - 0.5 * ones
    allone = np.ones((128, 128), np.float32)
    iota = np.broadcast_to(np.arange(1, 129, dtype=np.float32)[None, :], (128, 128))
    iota2 = np.broadcast_to(np.arange(129, 257, dtype=np.float32)[None, :], (128, 128))
    m = np.stack([ident, tri_f, tri_b, su_f, su_b, e2m_f, e2m_b, ones, allone, iota, iota2], axis=1)
    return np.ascontiguousarray(m.astype(np.float32))


CI_ID, CI_TRI, CI_SU, CI_E2M, CI_ONES, CI_ALL, CI_IOTA = 0, 1, 3, 5, 7, 8, 9


def prep_shared(inp):
    cp = ColPack()
    rp = RowPack()
    for l in range(2):
        cp.add(f"norm_w{l}", inp["norm_w"][l])
        cp.add(f"b_mod{l}", inp["b_mod"][l])
        cp.add(f"hg_norm{l}", inp["hg_norm"][l])
        for d in range(2):
            cp.add(f"s5_are{l}{d}", inp["s5_a_re"][l, d])
            cp.add(f"s5_aim{l}{d}", inp["s5_a_im"][l, d])
            cp.add(f"s5_ls{l}{d}", np.repeat(inp["s5_log_step"][l, d], 64))
            cp.add(f"lru_lam{l}{d}", inp["lru_lam"][l, d])
            for g in range(2):
                cp.add(f"lru_gb{l}{d}{g}", inp["lru_gate_b"][l, d, g])
        cp.add(f"s5_d{l}", inp["s5_d"][l])
        cp.add(f"s5_bglu{l}", inp["s5_b_glu"][l])
        for k in range(4):
            cp.add(f"lru_cw{l}{k}", inp["lru_conv_w"][l, k])
            cp.add(f"m2_cw{l}{k}", inp["m2_conv_w"][l, k])
            cp.add(f"b_gate{l}{k}", inp["b_gate"][l, k])
        cp.add(f"lru_cb{l}", inp["lru_conv_b"][l])
        cp.add(f"m2_cb{l}", inp["m2_conv_b"][l])
        rp.add(f"gtb{l}", inp["b_mod"][l][2048:3072])
        rp.add(f"m2_norm{l}", inp["m2_norm"][l])
        rp.add(f"m2_d{l}", inp["m2_d"][l])
        for d in range(2):
            rp.add(f"m2_dtb{l}{d}", inp["m2_dt_bias"][l, d])
            rp.add(f"m2_alog{l}{d}", inp["m2_a_log"][l, d])
    for l in range(3):
        for d in range(2):
            cp.add(f"hg_lg{l}{d}", inp["hg_lb_logits"][l, d])
            rp.add(f"hg_lg{l}{d}", inp["hg_lb_logits"][l, d])
    rp.add("final_norm", inp["final_norm"])
    bq = np.zeros((2, 2, 128, 16, 128), np.float32)
    cq = np.zeros((2, 2, 128, 16, 64), np.float32)
    for l in range(2):
        for ri, (bn, cn) in enumerate((("s5_b_re", "s5_c_re"), ("s5_b_im", "s5_c_im"))):
            B = inp[bn][l]
            C = inp[cn][l]
            for g in range(32):
                k = g // 2
                gl = g % 2
                c = k // 4
                kq = k % 4
                bq[l, ri, 32 * kq + 16 * gl:32 * kq + 16 * gl + 16, k, 64 * gl:64 * gl + 64] = B[g].T
                cq[l, ri, 64 * gl:64 * gl + 64, k, 32 * (kq % 2) + 16 * gl:32 * (kq % 2) + 16 * gl + 16] = C[g].T
    gw = np.zeros((2, 128, 16, 128), np.float32)
    for l in range(2):
        for d in range(2):
            for g in range(2):
                for j in range(4):
                    for bl in range(2):
                        gw[l, 64 * bl:64 * bl + 64, (d * 2 + g) * 4 + j, 64 * bl:64 * bl + 64] = inp["lru_gate_w"][l, d, g, 2 * j + bl]
    shared = {
        "cols": cp.array(), "rows": rp.array(), "cst": const_mats(),
        "s5bq": np.ascontiguousarray(bq), "s5cq": np.ascontiguousarray(cq), "lrugw": np.ascontiguousarray(gw),
        "w_mod": np.ascontiguousarray(inp["w_mod"], np.float32), "w_in": np.ascontiguousarray(inp["w_in"], np.float32),
        "w_gate": np.ascontiguousarray(inp["w_gate"], np.float32), "w_branch": np.ascontiguousarray(inp["w_branch"], np.float32),
        "w_out": np.ascontiguousarray(inp["w_out"], np.float32), "w_glu": np.ascontiguousarray(inp["s5_w_glu"], np.float32),
    }
    return shared, cp.off, rp.off


def build(shared, coff, roff, dbg=None):
    dbg = dbg or {}
    layers = dbg.get("layers", [0, 1])
    phases = dbg.get("phases", "HSLMG")
    nc = bass.Bass("TRN2", target_bir_lowering=False)
    okind = "ExternalOutput" if dbg.get("dump") else "Internal"

    def din(name, shape, dt=F32):
        return nc.dram_tensor(name, list(shape), dt, kind="ExternalInput").ap()

    xc_d = din("xc", [T, D])
    ccol_d = din("ccol", [128, 8, 2])
    cols_d = din("cols", list(shared["cols"].shape))
    rows_d = din("rows", list(shared["rows"].shape))
    cst_d = din("cst", [128, 11, 128])
    s5bq_d = din("s5bq", [2, 2, 128, 16, 128])
    s5cq_d = din("s5cq", [2, 2, 128, 16, 64])
    lrugw_d = din("lrugw", [2, 128, 16, 128])
    wmod_d = din("w_mod", [2, D, 3072])
    win_d = din("w_in", [2, D, IN_COLS])
    wgate_d = din("w_gate", [2, 4, D, D])
    wbr_d = din("w_branch", [2, 4, 512, D])
    wout_d = din("w_out", [2, D, D])
    wglu_d = din("w_glu", [2, 512, 512])
    out_d = nc.dram_tensor("out", [NLAT, D], F32, kind="ExternalOutput").ap()
    x1_d = nc.dram_tensor("x1s", [T, D], F32, kind=okind).ap()
    hs_d = nc.dram_tensor("hs", [128, 8, T], BF16, kind="Internal").ap()
    ys_d = nc.dram_tensor("ys", [4, 128, 4, T], BF16, kind=okind).ap()
    gs_d = nc.dram_tensor("gs", [128, 4, T], BF16, kind=okind).ap()
    ym_d = nc.dram_tensor("ym", [T, 512], F32, kind="Internal").ap()
    ysd_d = nc.dram_tensor("ysd", [128, 4, T], F32, kind=okind).ap() if dbg.get("dump") else None

    NCOLS = shared["cols"].shape[1]

    with contextlib.ExitStack() as gstack:
        K = Sched(nc, gstack)
        K.nopool = dbg.get('nopool', True)

        uniq = [0]

        def sb(stack, name, shape, dt=F32):
            uniq[0] += 1
            return stack.enter_context(nc.sbuf_tensor("%s_u%d" % (name, uniq[0]), list(shape), dt))

        cst = sb(gstack, "cst", [128, 11, 128])
        cols = sb(gstack, "cols", [128, NCOLS])
        identb = sb(gstack, "identb", [128, 128], BF16)
        ccol = sb(gstack, "ccol", [128, 8, 2])
        PS = [gstack.enter_context(nc.psum_tensor("ps%d" % i, [128, 512], F32)) for i in range(8)]
        K.dma('sp', cst[:], cst_d)
        K.dma('sp', cols[:], cols_d)
        K.dma('sp', ccol[:], ccol_d)
        K.copy(identb[:], cst[:, CI_ID, :])
        epsc = sb(gstack, "epsc", [128, 1])
        K.memset(epsc[:], EPS)
        K.epsc = epsc[:, 0:1]

        def col(name, j=0, n=1):
            o, k = coff[name]
            return cols[:, o + j:o + j + n]

        def rowb(stack, name, tname):
            o, n = roff[name]
            t = sb(stack, tname, [128, n])
            K.dma('sp', t[:], rows_d[:, o:o + n])
            return t

        ident = cst[:, CI_ID, :]
        ONESBD = cst[:, CI_ONES, :]
        ALLONE = cst[:, CI_ALL, :]
        IOTA = cst[:, CI_IOTA, :]

        def TRI(d):
            return cst[:, CI_TRI + d, :]

        def SU(d):
            return cst[:, CI_SU + d, :]

        def E2M(d):
            return cst[:, CI_E2M + d, :]

        for l in layers:
            xin_d = xc_d if l == 0 else x1_d
            last = (l == 1)
            with contextlib.ExitStack() as lstack:
                modcol = sb(lstack, "modcol", [128, 24, 2])
                g1col = sb(lstack, "g1col", [128, 8, 2])
                silc = sb(lstack, "silc", [128, 8, 2])
                K.act(silc[:], ccol[:], AF.Silu)
                with contextlib.ExitStack() as st:
                    wm = [sb(st, "wm%d" % i, [128, 8, 512]) for i in range(2)]
                    for jb in range(6):
                        w = wm[jb % 2]
                        K.dma('sp', w[:], wmod_d[l, :, jb * 512:(jb + 1) * 512].rearrange("(k p) n -> p k n", p=128))
                        for jj in range(4):
                            j = jb * 4 + jj
                            for kt in range(8):
                                K.mm(PS[0][:, j * 2:j * 2 + 2], w[:, kt, jj * 128:(jj + 1) * 128], silc[:, kt, :], start=(kt == 0), stop=(kt == 7))
                    o, _ = coff[f"b_mod{l}"]
                    K.tt(modcol[:], PS[0][:, 0:48].rearrange("p (j c) -> p j c", c=2),
                         cols[:, o:o + 24].rearrange("p (j c) -> p j c", c=1).to_broadcast([128, 24, 2]), ALU.add)
                    o, _ = coff[f"norm_w{l}"]
                    K.ts(g1col[:], modcol[:, 8:16, :], 1.0, ALU.add)
                    K.tt(g1col[:], g1col[:], cols[:, o:o + 8].rearrange("p (j c) -> p j c", c=1).to_broadcast([128, 8, 2]), ALU.mult)
                shcol = modcol[:, 0:8, :]

                p15 = contextlib.ExitStack()
                hT = sb(p15, "hT", [128, 8, T], BF16)
                wst = [sb(p15, "wst%d" % i, [128, 8, 128]) for i in range(2)]
                wblk = sb(p15, "wblk", [128, 8, 768], BF16)
                wst_i = [0]

                def load_w(colranges):
                    pos = 0
                    for (c0, n) in colranges:
                        for s0 in range(0, n, 128):
                            m = min(128, n - s0)
                            w = wst[wst_i[0] % 2]
                            wst_i[0] += 1
                            K.dma('sp', w[:, :, 0:m], win_d[l, :, c0 + s0:c0 + s0 + m].rearrange("(k p) n -> p k n", p=128))
                            K.copy(wblk[:, :, pos:pos + m], w[:, :, 0:m], e='dve')
                            pos += m
                    return pos

                def proj_fm(ps, wc0, ncols, t0, n):
                    for kt in range(8):
                        K.mm(ps[0:ncols, 0:n], wblk[:, kt, wc0:wc0 + ncols], hT[:, kt, t0:t0 + n], start=(kt == 0), stop=(kt == 7))

                def proj_tm(ps, wc0, ncols, t0, pc0=0):
                    for kt in range(8):
                        K.mm(ps[:, pc0:pc0 + ncols], hT[:, kt, t0:t0 + 128], wblk[:, kt, wc0:wc0 + ncols], start=(kt == 0), stop=(kt == 7))

                with contextlib.ExitStack() as st:
                    xr = [sb(st, "xr%d" % i, [128, D]) for i in range(2)]
                    xn = [sb(st, "xn%d" % i, [128, D]) for i in range(2)]
                    junk = sb(st, "junk", [128, D])
                    ss = [sb(st, "ss%d" % i, [128, 1]) for i in range(2)]
                    for p in range(NT):
                        cc = 1 if p < 2 else 0
                        a, b, s_ = xr[p % 2], xn[p % 2], ss[p % 2]
                        K.dma('sp', a[:], xin_d[p * 128:(p + 1) * 128, :])
                        K.act(junk[:], a[:], AF.Square, accum=s_[:])
                        K.rsqrt(s_[:], s_[:], 1.0 / D)
                        K.ts(b[:], a[:], s_[:, 0:1], ALU.mult)
                        for half in range(2):
                            ps = PS[(p * 2 + half) % 4]
                            for q in range(4):
                                kt = half * 4 + q
                                K.tr(ps[:, q * 128:(q + 1) * 128], b[:, kt * 128:(kt + 1) * 128], ident)
                            for q in range(4):
                                kt = half * 4 + q
                                K.act(hT[:, kt, p * 128:(p + 1) * 128], ps[:, q * 128:(q + 1) * 128], AF.Identity,
                                      bias=shcol[:, kt, cc:cc + 1], scale=g1col[:, kt, cc:cc + 1])
                    K.dma('sp', hs_d, hT[:])

                if 'H' in phases:
                    with contextlib.ExitStack() as st:
                        stl = contextlib.ExitStack()
                        lbc = sb(st, "lbc", [128, 2, 4])
                        lbr = sb(st, "lbr", [128, 2, 512])
                        lgc = sb(stl, "lgc", [128, 3, 2, 4])
                        lgr = sb(stl, "lgr", [128, 3, 2, 512])
                        tmpc = sb(stl, "tmpc", [128, 2, 4])
                        tmpr = sb(stl, "tmpr", [128, 2, 512])
                        for ll in (range(3) if 'B' in dbg.get("hs", "BAF") else []):
                            for d in range(2):
                                o, _ = coff[f"hg_lg{ll}{d}"]
                                K.act(lgc[:, ll, d, :], cols[:, o:o + 4], AF.Exp)
                                o, n = roff[f"hg_lg{ll}{d}"]
                                K.dma('sp', lgr[:, ll, d, :], rows_d[:, o:o + n])
                        if 'B' in dbg.get("hs", "BAF"):
                            K.act(lgr[:], lgr[:], AF.Exp)
                        for (lg, lb_, tmp) in (((lgc, lbc, tmpc), (lgr, lbr, tmpr)) if 'B' in dbg.get("hs", "BAF") else []):
                            K.tt(tmp[:], lg[:, 0], lg[:, 1], ALU.add)
                            K.tt(tmp[:], tmp[:], lg[:, 2], ALU.add)
                            if l == 0:
                                K.tt(lb_[:], lg[:, 1], lg[:, 2], ALU.add)
                            else:
                                K.copy(lb_[:], lg[:, 2])
                            K.recip(tmp[:], tmp[:])
                            K.tt(lb_[:], lb_[:], tmp[:], ALU.mult)
                        K.barrier()
                        stl.close()
                        qs = sb(st, "qs", [128, T])
                        kkTs = [sb(st, "kkTs%d" % i, [128, T]) for i in range(2)]
                        vt = sb(st, "vt", [128, NT, 128], BF16)
                        osum = sb(st, "osum", [128, T])
                        SstD = [sb(st, "Sst%d" % i, [128, 128]) for i in range(2)]
                        SbfD = [[sb(st, "Sbf%d_%d" % (d_, i), [128, 128], BF16) for i in range(2)] for d_ in range(2)]

                        def tl(name, dt=F32, n=2, shape=(128, 128)):
                            return [sb(st, "%s%d" % (name, i), list(shape), dt) for i in range(n)]
                        def tl2(name, dt=F32):
                            return [tl(name + "d%d_" % d_, dt) for d_ in range(2)]
                        kkM, lfM = tl2("kkM"), tl2("lfM")
                        E1, E2, E3, EU = tl2("E1"), tl2("E2"), tl2("E3"), tl2("EU")
                        qb, qt, ktl, kh, attm = tl2("qb", BF16), tl2("qt", BF16), tl2("ktl", BF16), tl2("kh", BF16), tl2("attm", BF16)
                        o2 = tl("o2", F32, 1, (128, 512)) * 2
                        rs = tl("rs", F32, 2, (128, 512))
                        zz = tl("zz", F32, 2, (128, 512))
                        yo = tl("yo", BF16, 2, (128, 512))
                        for h in range(dbg.get("nheads", 4)):
                            hc = h * 128
                            load_w([(C_Q + hc, 128), (C_I + hc, 128), (C_FF + hc, 128), (C_FB + hc, 128), (C_HZ + hc, 128)])
                            for si_, (t0, n) in enumerate(SLABS):
                                ps = PS[si_ % 2]
                                proj_fm(ps, 0, 128, t0, n)
                                K.act(qs[:, t0:t0 + n], ps[:, 0:n], AF.Silu)
                            for p in range(NT):
                                ps = PS[2 + p % 2]
                                proj_tm(ps, 128, 128, p * 128, pc0=0)
                                K.copy(vt[:, p, :], ps[:, 0:128])
                            for d in range(2):
                                for si_, (t0, n) in enumerate(SLABS):
                                    ps = PS[4 + si_ % 2]
                                    proj_fm(ps, 256 + d * 128, 128, t0, n)
                                    kv_ = kkTs[d][:, t0:t0 + n]
                                    K.act(kv_, ps[:, 0:n], AF.Exp)
                                    K.act(kv_, kv_, AF.Ln, bias=1.0)
                                    K.act(kv_, kv_, AF.Exp, scale=-1.0)
                                    K.ts(kv_, kv_, lbc[:, d, h:h + 1], ALU.mult)
                            orders = [list(range(NT)), [1, 0] + list(range(NT - 1, 1, -1))]
                            sidx = [0, 0]
                            for d in range(2):
                                K.memset(SstD[d][:], 0.0)
                                K.memset(SbfD[d][0][:], 0.0)

                            def emit_X(it):
                                b_ = it % 2
                                dd = (0, 1)
                                p_ = [orders[d][it] for d in dd]
                                for d in dd:
                                    K.tr(PS[d][:, 0:128], kkTs[d][:, p_[d] * 128:(p_[d] + 1) * 128], ident)
                                for d in dd:
                                    K.copy(kkM[d][b_][:], PS[d][:, 0:128], e='act')
                                for d in dd:
                                    K.act(lfM[d][b_][:], kkM[d][b_][:], AF.Ln, bias=1.0, scale=-1.0)
                                for d in dd:
                                    psB = PS[2 + d]
                                    K.mm(psB[:, 0:128], lfM[d][b_][:], TRI(d))
                                    K.mm(psB[:, 128:256], lfM[d][b_][:], E2M(d))
                                    K.mm(psB[:, 256:384], SU(d), lfM[d][b_][:])
                                for d in dd:
                                    psB = PS[2 + d]
                                    K.act(E1[d][b_][:], psB[:, 0:128], AF.Exp)
                                    K.act(E2[d][b_][:], psB[:, 128:256], AF.Exp)
                                    K.act(E3[d][b_][:], psB[:, 128:256], AF.Exp, scale=-1.0)
                                    K.act(EU[d][b_][:], psB[:, 256:384], AF.Exp)
                                for d in dd:
                                    t0 = p_[d] * 128
                                    K.tt(qt[d][b_][:], qs[:, t0:t0 + 128], E2[d][b_][:], ALU.mult, e='pool')
                                    K.tt(ktl[d][b_][:], kkTs[d][:, t0:t0 + 128], E3[d][b_][:], ALU.mult)
                                for d in dd:
                                    t0 = p_[d] * 128
                                    K.tt(qb[d][b_][:], qs[:, t0:t0 + 128], E1[d][b_][:], ALU.mult, e='pool')
                                    K.tt(kh[d][b_][:], kkM[d][b_][:], EU[d][b_][:], ALU.mult)
                                for d in dd:
                                    K.mm(PS[d][:, 128:256], ktl[d][b_][:], qt[d][b_][:])
                                for d in dd:
                                    K.tt(attm[d][b_][:], PS[d][:, 128:256], TRI(d), ALU.mult)

                            ntl = min(NT, dbg.get("htiles", 99))
                            emit_X(0)
                            for it in range(ntl):
                                b_ = it % 2
                                if it + 1 < ntl:
                                    emit_X(it + 1)
                                for d in range(2):
                                    p = orders[d][it]
                                    K.mm(PS[4 + d][:, 0:128], vt[:, p, :], attm[d][b_][:], start=True, stop=False)
                                for ci in range(2):
                                    for d in range(2):
                                        p = orders[d][it]
                                        c0, lastcol = ([(0, 63), (64, 127)] if d == 0 else [(64, 64), (0, 0)])[ci]
                                        psO, psk = PS[4 + d], PS[6 + d]
                                        K.mm(psO[:, c0:c0 + 64], SbfD[d][sidx[d] % 2][:], qb[d][b_][:, c0:c0 + 64], start=False, stop=(ci == 1))
                                        K.mm(psk[:, 0:128], kh[d][b_][c0:c0 + 64, :], vt[c0:c0 + 64, p, :])
                                        K.stt(SstD[d][:], SstD[d][:], E1[d][b_][:, lastcol:lastcol + 1], psk[:, 0:128], ALU.mult, ALU.add)
                                        sidx[d] += 1
                                        K.copy(SbfD[d][sidx[d] % 2][:], SstD[d][:], e='act')
                                for d in range(2):
                                    p = orders[d][it]
                                    t0 = p * 128
                                    key = S(osum[:, t0:t0 + 128], p)
                                    first = (d == 0 and p >= 2) or (d == 1 and p < 2)
                                    stepf = p
                                    stepb = (1 - p) if p < 2 else (NT + 1 - p)
                                    is_first = (stepf < stepb) if d == 0 else (stepb < stepf)
                                    if stepf == stepb:
                                        is_first = (d == 0)
                                    if is_first:
                                        K.copy(key, PS[4 + d][:, 0:128])
                                    else:
                                        K.tt(key, key, PS[4 + d][:, 0:128], ALU.add)
                            o, _ = coff[f"hg_norm{l}"]
                            for si_, (t0, n) in (enumerate(SLABS) if 'F' in dbg.get("hs", "BAF") else []):
                                b_ = si_ % 2
                                ps = PS[si_ % 2]
                                ps2 = PS[2 + si_ % 2]
                                okeys = [S(osum[:, tt0:tt0 + 128], tt0 // 128) for tt0 in range(t0, t0 + n, 128)]
                                K.op('dve', [o2[b_][:]], okeys, lambda g, b_=b_, t0=t0, n=n: g.tensor_tensor(out=o2[b_][:, 0:n], in0=osum[:, t0:t0 + n], in1=osum[:, t0:t0 + n], op=ALU.mult))
                                K.mm(ps[:, 0:n], ALLONE, o2[b_][:, 0:n])
                                K.rsqrt(rs[b_][:, 0:n], ps[:, 0:n], 1.0 / 128)
                                proj_fm(ps2, 512, 128, t0, n)
                                K.act(zz[b_][:, 0:n], ps2[:, 0:n], AF.Exp, scale=-1.0)
                                K.act(zz[b_][:, 0:n], zz[b_][:, 0:n], AF.Ln, bias=1.0)
                                K.act(zz[b_][:, 0:n], zz[b_][:, 0:n], AF.Exp, scale=-1.0)
                                K.tt(zz[b_][:, 0:n], zz[b_][:, 0:n], ps2[:, 0:n], ALU.mult)
                                K.op('dve', [rs[b_][:]], okeys + [rs[b_][:], cols[:, o + h:o + h + 1]],
                                     lambda g, b_=b_, t0=t0, n=n, o=o, h=h: g.scalar_tensor_tensor(out=rs[b_][:, 0:n], in0=rs[b_][:, 0:n], scalar=cols[:, o + h:o + h + 1], in1=osum[:, t0:t0 + n], op0=ALU.mult, op1=ALU.mult))
                                K.tt(yo[b_][:, 0:n], rs[b_][:, 0:n], zz[b_][:, 0:n], ALU.mult)
                                K.dma('sp', S(ys_d[0, :, h, t0:t0 + n], (0, h, si_)), yo[b_][:, 0:n])
                    K.barrier()

                if 'S' in phases:
                    with contextlib.ExitStack() as st:
                        bqb = sb(st, "bqb", [128, 2, 16, 128], BF16)
                        cqb = sb(st, "cqb", [128, 2, 16, 64], BF16)
                        with contextlib.ExitStack() as st2:
                            bq32 = sb(st2, "bq32", [128, 2, 16, 128])
                            cq32 = sb(st2, "cq32", [128, 2, 16, 64])
                            for ri in range(2):
                                K.dma('sp', bq32[:, ri], s5bq_d[l, ri])
                                K.dma('sp', cq32[:, ri], s5cq_d[l, ri])
                            K.copy(bqb[:], bq32[:])
                            K.copy(cqb[:], cq32[:])
                            K.barrier()
                        def cst16(name):
                            return [sb(st, "%s%d" % (name, d), [128, 16]) for d in range(2)]
                        stp, rr, th, cr_, ci_, t_a, t_b, t_c = (cst16("stp"), cst16("rr"), cst16("th"), cst16("cr"), cst16("ci"),
                                                                cst16("ta"), cst16("tb"), cst16("tc"))
                        for d in range(2):
                            are = col(f"s5_are{l}{d}", 0, 16)
                            aim = col(f"s5_aim{l}{d}", 0, 16)
                            K.act(stp[d][:], col(f"s5_ls{l}{d}", 0, 16), AF.Exp)
                            K.tt(th[d][:], aim, stp[d][:], ALU.mult)
                            K.tt(t_a[d][:], are, stp[d][:], ALU.mult)
                            K.act(rr[d][:], t_a[d][:], AF.Exp)
                        negpi = sb(st, "negpi", [128, 1])
                        K.memset(negpi[:], -PI)
                        for d in range(2):
                            are = col(f"s5_are{l}{d}", 0, 16)
                            aim = col(f"s5_aim{l}{d}", 0, 16)
                            sn, cs = t_b[d], t_c[d]
                            K.sinred(sn[:], th[d][:], 0.0, t_a[d][:], None)
                            K.sinred(cs[:], th[d][:], 0.5 * PI, t_a[d][:], None)
                            K.tt(cs[:], cs[:], rr[d][:], ALU.mult)
                            K.ts(cs[:], cs[:], -1.0, ALU.add)
                            K.tt(sn[:], sn[:], rr[d][:], ALU.mult)
                            K.tt(t_a[d][:], are, are, ALU.mult)
                            K.tt(cr_[d][:], aim, aim, ALU.mult)
                            K.tt(t_a[d][:], t_a[d][:], cr_[d][:], ALU.add)
                            K.recip(t_a[d][:], t_a[d][:])
                            K.tt(cr_[d][:], cs[:], are, ALU.mult)
                            K.tt(ci_[d][:], sn[:], aim, ALU.mult)
                            K.tt(cr_[d][:], cr_[d][:], ci_[d][:], ALU.add)
                            K.tt(cr_[d][:], cr_[d][:], t_a[d][:], ALU.mult)
                            K.tt(ci_[d][:], sn[:], are, ALU.mult)
                            K.tt(cs[:], cs[:], aim, ALU.mult)
                            K.tt(ci_[d][:], ci_[d][:], cs[:], ALU.subtract)
                            K.tt(ci_[d][:], ci_[d][:], t_a[d][:], ALU.mult)
                        ub = sb(st, "ub", [128, T], BF16)
                        ysum = sb(st, "ysum", [128, T])
                        gt_ = [sb(st, "gt%d" % i, [128, 512]) for i in range(1)] * 2
                        gu_ = [sb(st, "gu%d" % i, [128, 512]) for i in range(2)]
                        gb_ = [sb(st, "gb%d" % i, [128, 512], BF16) for i in range(2)]
                        st3 = contextlib.ExitStack()
                        tabc = sb(st3, "tabc", [128, 4, SL])
                        tabm = sb(st3, "tabm", [128, 4, SL])
                        tabB = sb(st3, "tabB", [128, 4, 2, SL])
                        tabD = sb(st3, "tabD", [128, 4, 2, SL])
                        NB = 4

                        def tl(name, dt=F32, n=NB, shape=(128, 2, SL)):
                            return [sb(st3, "%s%d" % (name, i), list(shape), dt) for i in range(n)]
                        A_, B_, W_ = tl("sA"), tl("sB"), tl("sW")
                        D_, V_, U_ = A_, A_, B_
                        ph, tq1, tq2 = B_[0][:, 0, :], B_[1][:, 0, :], B_[2][:, 0, :]
                        slt = tl("slt", F32, NB, (128, 2))
                        Ub = [[sb(st3, "Ub%d_%d" % (k, i), [128, 2, SL], BF16) for i in range(2)] for k in range(4)]
                        Vb = [[sb(st3, "Vb%d_%d" % (k, i), [128, 2, SL], BF16) for i in range(2)] for k in range(4)]
                        sl_ = [[sb(st3, "sl%d_%d" % (k, i), [128, 2]) for i in range(2)] for k in range(4)]
                        IOTAL = cst[:, CI_IOTA:CI_IOTA + SL // 128, :].rearrange("p a b -> p (a b)")

                        def view(ap, swap=False, rv=False):
                            pat = [list(p) for p in ap.ap]
                            off = ap.offset
                            if swap:
                                st_, cn = pat[-2]
                                off = off + st_ * (cn - 1)
                                pat[-2] = [-st_, cn]
                            if rv:
                                st_, cn = pat[-1]
                                off = off + st_ * (cn - 1)
                                pat[-1] = [-st_, cn]
                            return AP(ap.tensor, off, pat)
                        nchunk = T // SL
                        S5E = dbg.get('s5e', 'dve')
                        for c in range(4):
                            load_w([(C_SU + c * 128, 128)])
                            for si_, (t0, n) in enumerate(SLABS):
                                ps = PS[si_ % 2]
                                proj_fm(ps, 0, 128, t0, n)
                                K.copy(ub[:, t0:t0 + n], ps[:, 0:n], e='act')
                            for d in dbg.get("s5dirs", [0, 1]):
                                for kq in range(4):
                                    k = 4 * c + kq
                                    K.ts(ph[:], IOTAL, th[d][:, k:k + 1], ALU.mult)
                                    K.sinred(tabD[:, kq, 0, :], ph[:], 0.0, tq1[:], None)
                                    K.sinred(tabc[:, kq, :], ph[:], 0.5 * PI, tq1[:], None)
                                    K.ts(tq2[:], tabD[:, kq, 0, :], ci_[d][:, k:k + 1], ALU.mult)
                                    K.stt(tabm[:, kq, :], tabc[:, kq, :], cr_[d][:, k:k + 1], tq2[:], ALU.mult, ALU.add)
                                    K.ts(tq2[:], tabD[:, kq, 0, :], cr_[d][:, k:k + 1], ALU.mult)
                                    K.stt(tabB[:, kq, 1, :], tabc[:, kq, :], ci_[d][:, k:k + 1], tq2[:], ALU.mult, ALU.subtract)
                                    K.ts(tabB[:, kq, 0, :], tabB[:, kq, 1, :], -1.0, ALU.mult)
                                    K.ts(tabD[:, kq, 1, :], tabD[:, kq, 0, :], -1.0, ALU.mult)
                                if d == 0:
                                    corder = list(range(nchunk))
                                else:
                                    nc_ctx = NCTX // SL
                                    corder = list(range(nc_ctx - 1, -1, -1)) + list(range(nchunk - 1, nc_ctx - 1, -1))
                                def emit_P(it, ch, kq):
                                    k = 4 * c + kq
                                    t0 = ch * SL
                                    psP = PS[4 + kq]
                                    pb = 64 * (kq // 2)
                                    K.mm(psP[:, 0:SL], bqb[pb:pb + 64, 0, k, :], ub[pb:pb + 64, t0:t0 + SL])
                                    K.mm(psP[:, SL:2 * SL], bqb[pb:pb + 64, 1, k, :], ub[pb:pb + 64, t0:t0 + SL])

                                for kq in range(4):
                                    emit_P(0, corder[0], kq)
                                for it, ch in enumerate(corder):
                                    t0 = ch * SL
                                    par = it % 2
                                    psY = PS[2 + it % 2]
                                    rv = (d == 1)
                                    for kq in range(4):
                                        P3 = PS[4 + kq][:, 0:2 * SL].rearrange("p (a t) -> p a t", a=2)
                                        K.tt(A_[kq][:], view(P3, False, rv), tabm[:, kq:kq + 1, :].to_broadcast([128, 2, SL]), ALU.mult)
                                        K.tt(B_[kq][:], view(P3, True, rv), tabB[:, kq], ALU.mult)
                                    if it + 1 < len(corder):
                                        for kq in range(4):
                                            emit_P(it + 1, corder[it + 1], kq)
                                    for kq in range(4):
                                        K.tt(D_[kq][:], A_[kq][:], B_[kq][:], ALU.add, e=S5E)
                                    for hf in range(2):
                                        for kq in range(4):
                                            k = 4 * c + kq
                                            rk = rr[d][:, k:k + 1].to_broadcast([128, SL])
                                            init = 0.0 if it == 0 else sl_[kq][1 - par][:, hf:hf + 1]
                                            K.scan(W_[kq][:, hf, :], rk, D_[kq][:, hf, :], init, ALU.mult, ALU.add if hf == 0 else ALU.subtract)
                                    for kq in range(4):
                                        K.ts(sl_[kq][par][:], W_[kq][:, :, SL - 1], tabc[:, kq, SL - 1:SL], ALU.mult)
                                    for kq in range(4):
                                        K.tt(slt[kq][:], view(W_[kq][:], True, False)[:, :, SL - 1], tabD[:, kq, :, SL - 1], ALU.mult)
                                    for kq in range(4):
                                        K.tt(sl_[kq][par][:], sl_[kq][par][:], slt[kq][:], ALU.add)
                                    for kq in range(4):
                                        K.tt(view(Ub[kq][par][:], False, rv), W_[kq][:], tabc[:, kq:kq + 1, :].to_broadcast([128, 2, SL]), ALU.mult, e=S5E)
                                    for kq in range(4):
                                        K.tt(view(Vb[kq][par][:], False, rv), view(W_[kq][:], True, False), tabD[:, kq], ALU.mult)
                                    for kq in range(4):
                                        k = 4 * c + kq
                                        pb = 64 * (kq // 2)
                                        u_, v_ = Ub[kq][par], Vb[kq][par]
                                        K.mm(psY[pb:pb + 64, 0:SL], cqb[:, 0, k, :], u_[:, 0, :], start=(kq % 2 == 0), stop=False)
                                        K.mm(psY[pb:pb + 64, 0:SL], cqb[:, 0, k, :], v_[:, 0, :], start=False, stop=False)
                                        K.mm(psY[pb:pb + 64, 0:SL], cqb[:, 1, k, :], u_[:, 1, :], start=False, stop=False)
                                        K.mm(psY[pb:pb + 64, 0:SL], cqb[:, 1, k, :], v_[:, 1, :], start=False, stop=(kq % 2 == 1))
                                    if d == 0:
                                        K.stt(S(ysum[:, t0:t0 + SL], ch), ub[:, t0:t0 + SL], col(f"s5_d{l}", c), psY[:, 0:SL], ALU.mult, ALU.add)
                                    else:
                                        K.tt(S(ysum[:, t0:t0 + SL], ch), S(ysum[:, t0:t0 + SL], ch), psY[:, 0:SL], ALU.add)
                            if dbg.get("dump"):
                                K.barrier()
                                K.dma('sp', ysd_d[:, c, :], ysum[:])
                                K.barrier()
                            for si_, (t0, n) in enumerate(SLABS):
                                b_ = si_ % 2
                                ysl = [S(ysum[:, tt0:tt0 + SL], tt0 // SL) for tt0 in range(t0, t0 + n, SL)]
                                yv = ysum[:, t0:t0 + n]
                                K.op('dve', [gt_[b_][:]], ysl, lambda g, b_=b_, yv=yv, n=n: g.tensor_tensor(out=gt_[b_][:, 0:n], in0=yv, in1=yv, op=ALU.mult))
                                K.ts(gt_[b_][:, 0:n], gt_[b_][:, 0:n], 0.044715, ALU.mult, 1.0, ALU.add)
                                K.op('dve', [gu_[b_][:]], ysl + [gt_[b_][:]], lambda g, b_=b_, yv=yv, n=n: g.tensor_tensor(out=gu_[b_][:, 0:n], in0=gt_[b_][:, 0:n], in1=yv, op=ALU.mult))
                                K.act(gu_[b_][:, 0:n], gu_[b_][:, 0:n], AF.Sigmoid, scale=1.5957691216)
                                K.op('dve', [gb_[b_][:]], ysl + [gu_[b_][:]], lambda g, b_=b_, yv=yv, n=n: g.tensor_tensor(out=gb_[b_][:, 0:n], in0=gu_[b_][:, 0:n], in1=yv, op=ALU.mult))
                                K.dma('sp', S(gs_d[:, c, t0:t0 + n], (c, si_)), gb_[b_][:, 0:n])
                        K.barrier()
                        st3.close()
                        wg32 = sb(st, "wg32", [128, 4, 512])
                        wgb = sb(st, "wgb", [128, 4, 512], BF16)
                        K.dma('sp', wg32[:], wglu_d[l].rearrange("(k p) n -> p k n", p=128))
                        K.copy(wgb[:], wg32[:], e='pool')
                        load_w([(C_SZ, 512)])
                        gin = [sb(st, "gin%d" % i, [128, 4, 512], BF16) for i in range(2)]
                        sg = [sb(st, "sg%d" % i, [128, 512]) for i in range(2)]
                        for si_, (t0, n) in enumerate(SLABS):
                            b_ = si_ % 2
                            for c in range(4):
                                K.dma('sp', gin[b_][:, c, 0:n], S(gs_d[:, c, t0:t0 + n], (c, si_)))
                            for c in range(4):
                                ps = PS[c % 2]
                                ps2 = PS[2 + c % 2]
                                for kc in range(4):
                                    K.mm(ps[:, 0:n], wgb[:, kc, c * 128:(c + 1) * 128], gin[b_][:, kc, 0:n], start=(kc == 0), stop=(kc == 3))
                                K.act(sg[c % 2][:, 0:n], ps[:, 0:n], AF.Sigmoid, bias=col(f"s5_bglu{l}", c))
                                proj_fm(ps2, c * 128, 128, t0, n)
                                K.act(gu_[c % 2][:, 0:n], ps2[:, 0:n], AF.Sigmoid)
                                K.tt(gu_[c % 2][:, 0:n], gu_[c % 2][:, 0:n], ps2[:, 0:n], ALU.mult)
                                K.tt(sg[c % 2][:, 0:n], sg[c % 2][:, 0:n], gin[b_][:, c, 0:n], ALU.mult)
                                K.tt(gb_[c % 2][:, 0:n], sg[c % 2][:, 0:n], gu_[c % 2][:, 0:n], ALU.mult)
                                K.dma('sp', S(ys_d[1, :, c, t0:t0 + n], (1, c, si_)), gb_[c % 2][:, 0:n])
                    K.barrier()

                if 'L' in phases:
                    with contextlib.ExitStack() as st:
                        gw32 = sb(st, "gw32", [128, 16, 128])
                        gwb = sb(st, "gwb", [128, 16, 128], BF16)
                        K.dma('sp', gw32[:], lrugw_d[l])
                        K.copy(gwb[:], gw32[:])
                        cd = sb(st, "cd", [128, 2, 4])
                        cd2 = sb(st, "cd2", [128, 2, 4])
                        for d in range(2):
                            K.act(cd[:, d, :], col(f"lru_lam{l}{d}", 0, 4), AF.Exp, scale=-1.0)
                        K.act(cd[:], cd[:], AF.Ln, bias=1.0)
                        K.ts(cd2[:], cd[:], -16.0, ALU.mult)
                        K.ts(cd[:], cd[:], -8.0, ALU.mult)
                        xraw = sb(st, "xraw", [128, T])
                        xcv = sb(st, "xcv", [128, T])
                        xcb = sb(st, "xcb", [128, T], BF16)
                        hsum = sb(st, "hsum", [128, T])

                        def tl(name, dt=F32, n=2, shape=(128, 512)):
                            return [sb(st, "%s%d" % (name, i), list(shape), dt) for i in range(n)]
                        rg, ig, aa, a2, bb, hb, zz, yo = tl("rg"), tl("ig"), tl("aa"), tl("a2"), tl("bb"), tl("hb"), tl("zz"), tl("yo", BF16)
                        for j in range(4):
                            load_w([(C_LX + j * 128, 128), (C_LZ + j * 128, 128)])
                            for si_, (t0, n) in enumerate(SLABS):
                                ps = PS[si_ % 2]
                                proj_fm(ps, 0, 128, t0, n)
                                K.copy(xraw[:, t0:t0 + n], ps[:, 0:n], e='act')
                            w = [col(f"lru_cw{l}{k}", j) for k in range(4)]
                            cb = col(f"lru_cb{l}", j)
                            K.ts(xcv[:], xraw[:], w[1], ALU.mult, cb, ALU.add)
                            for (a0, nr, rl) in ((0, 1, 256), (256, 64, 64)):
                                xv = xraw[:, a0:a0 + nr * rl].rearrange("p (r t) -> p r t", t=rl)
                                ov = xcv[:, a0:a0 + nr * rl].rearrange("p (r t) -> p r t", t=rl)
                                K.stt(ov[:, :, 1:rl], xv[:, :, 0:rl - 1], w[0], ov[:, :, 1:rl], ALU.mult, ALU.add)
                                K.stt(ov[:, :, 0:rl - 1], xv[:, :, 1:rl], w[2], ov[:, :, 0:rl - 1], ALU.mult, ALU.add)
                                K.stt(ov[:, :, 0:rl - 2], xv[:, :, 2:rl], w[3], ov[:, :, 0:rl - 2], ALU.mult, ALU.add)
                            K.copy(xcb[:], xcv[:], e='pool')
                            for d in range(2):
                                order = list(range(9)) if d == 0 else [0] + list(range(8, 0, -1))
                                prev = None
                                for pi in range(0, len(order), 2):
                                    grp = [(it, order[it]) for it in range(pi, min(pi + 2, len(order)))]
                                    for it, si_ in grp:
                                        t0, n = SLABS[si_]
                                        K.mm(PS[2 + it % 2][:, 0:n], gwb[:, (d * 2 + 0) * 4 + j, :], xcb[:, t0:t0 + n])
                                        K.mm(PS[4 + it % 2][:, 0:n], gwb[:, (d * 2 + 1) * 4 + j, :], xcb[:, t0:t0 + n])
                                    for it, si_ in grp:
                                        t0, n = SLABS[si_]
                                        b_ = it % 2
                                        K.act(rg[b_][:, 0:n], PS[2 + it % 2][:, 0:n], AF.Sigmoid, bias=col(f"lru_gb{l}{d}0", j))
                                        K.act(ig[b_][:, 0:n], PS[4 + it % 2][:, 0:n], AF.Sigmoid, bias=col(f"lru_gb{l}{d}1", j))
                                    for it, si_ in grp:
                                        t0, n = SLABS[si_]
                                        b_ = it % 2
                                        K.act(aa[b_][:, 0:n], rg[b_][:, 0:n], AF.Exp, scale=cd[:, d, j:j + 1])
                                        K.act(a2[b_][:, 0:n], rg[b_][:, 0:n], AF.Exp, scale=cd2[:, d, j:j + 1])
                                    for it, si_ in grp:
                                        t0, n = SLABS[si_]
                                        b_ = it % 2
                                        K.ts(a2[b_][:, 0:n], a2[b_][:, 0:n], -1.0, ALU.mult, 1.0, ALU.add)
                                        K.ts(a2[b_][:, 0:n], a2[b_][:, 0:n], 0.0, ALU.max)
                                        K.tt(bb[b_][:, 0:n], ig[b_][:, 0:n], xcv[:, t0:t0 + n], ALU.mult, e='pool')
                                    for it, si_ in grp:
                                        t0, n = SLABS[si_]
                                        b_ = it % 2
                                        K.act(a2[b_][:, 0:n], a2[b_][:, 0:n], AF.Sqrt)
                                    for it, si_ in grp:
                                        t0, n = SLABS[si_]
                                        b_ = it % 2
                                        K.tt(bb[b_][:, 0:n], bb[b_][:, 0:n], a2[b_][:, 0:n], ALU.mult, e='pool')
                                        init = 0.0 if prev is None else prev
                                        if d == 0:
                                            K.scan(hsum[:, t0:t0 + n], aa[b_][:, 0:n], bb[b_][:, 0:n], init)
                                            prev = hsum[:, t0 + n - 1:t0 + n]
                                        else:
                                            K.scan(rev(hb[b_][:, 0:n]), rev(aa[b_][:, 0:n]), rev(bb[b_][:, 0:n]), init)
                                            prev = hb[b_][:, 0:1]
                                            K.tt(hsum[:, t0:t0 + n], hsum[:, t0:t0 + n], hb[b_][:, 0:n], ALU.add, e='pool')
                            for si_, (t0, n) in enumerate(SLABS):
                                b_ = si_ % 2
                                ps = PS[si_ % 2]
                                proj_fm(ps, 128, 128, t0, n)
                                K.act(zz[b_][:, 0:n], ps[:, 0:n], AF.Silu)
                                K.tt(yo[b_][:, 0:n], zz[b_][:, 0:n], hsum[:, t0:t0 + n], ALU.mult)
                                K.dma('sp', S(ys_d[2, :, j, t0:t0 + n], (2, j, si_)), yo[b_][:, 0:n])
                    K.barrier()

                if 'M' in phases:
                    with contextlib.ExitStack() as st:
                        xbc = sb(st, "xbc", [128, 6, T], BF16)
                        st2 = contextlib.ExitStack()
                        xraw = sb(st2, "mxraw", [128, T])
                        xcv = sb(st2, "mxcv", [128, T])
                        for j in range(6):
                            load_w([(C_XBC + j * 128, 128)])
                            for si_, (t0, n) in enumerate(SLABS):
                                ps = PS[si_ % 2]
                                proj_fm(ps, 0, 128, t0, n)
                                K.copy(xraw[:, t0:t0 + n], ps[:, 0:n], e='act')
                            w = [col(f"m2_cw{l}{k}", j) for k in range(4)]
                            cb = col(f"m2_cb{l}", j)
                            K.ts(xcv[:], xraw[:], w[1], ALU.mult, cb, ALU.add)
                            for (a0, nr, rl) in ((0, 1, 256), (256, 64, 64)):
                                xv = xraw[:, a0:a0 + nr * rl].rearrange("p (r t) -> p r t", t=rl)
                                ov = xcv[:, a0:a0 + nr * rl].rearrange("p (r t) -> p r t", t=rl)
                                K.stt(ov[:, :, 1:rl], xv[:, :, 0:rl - 1], w[0], ov[:, :, 1:rl], ALU.mult, ALU.add)
                                K.stt(ov[:, :, 0:rl - 1], xv[:, :, 1:rl], w[2], ov[:, :, 0:rl - 1], ALU.mult, ALU.add)
                                K.stt(ov[:, :, 0:rl - 2], xv[:, :, 2:rl], w[3], ov[:, :, 0:rl - 2], ALU.mult, ALU.add)
                            K.act(S(xbc[:, j, :], j), xcv[:], AF.Silu)
                        K.barrier()
                        st2.close()
                        XBC = [S(xbc[:, j, :], j) for j in range(6)]
                        load_w([(C_DT, 16), (C_MZ, 512)])
                        dtb = [rowb(st, f"m2_dtb{l}{d}", "dtb%d" % d) for d in range(2)]
                        alg = [rowb(st, f"m2_alog{l}{d}", "alg%d" % d) for d in range(2)]
                        m2d = rowb(st, f"m2_d{l}", "m2d")
                        m2n = rowb(st, f"m2_norm{l}", "m2n")
                        for d in range(2):
                            K.act(alg[d][:], alg[d][:], AF.Exp)
                            K.ts(alg[d][:], alg[d][:], -1.0, ALU.mult)

                        def tl(name, shape, dt=F32, n=2):
                            return [sb(st, "%s%d" % (name, i), list(shape), dt) for i in range(n)]
                        dt_, dta, ecum, wdt = tl("dt", [128, 8]), tl("dta", [128, 8]), tl("ecum", [128, 8]), tl("wdt", [128, 8])
                        Rm = tl("Rm", [128, 8, 128])
                        exs = tl("exs", [128, 8, 128])
                        mdt = tl("mdt", [128, 8, 128])
                        Wt = tl("Wt", [128, 8, 128], BF16)
                        xtm = tl("xtm", [128, 640], BF16)
                        xw = tl("xw", [128, 512], BF16)
                        edec = tl("edec", [128, 4], F32, 4)
                        Sst = sb(st, "mS", [128, 256])
                        Sb = tl("mSb", [128, 256], BF16)
                        ya = tl("ya", [128, 512])
                        yb = tl("yb", [128, 512])
                        zt = tl("zt", [128, 512])
                        y2 = tl("y2", [128, 512])
                        ybf = tl("ybf", [128, 512], BF16)
                        yfm = tl("yfm", [128, 4, 128], BF16)
                        ssq = tl("ssq", [128, 1])
                        for d in range(2):
                            order = list(range(NT)) if d == 0 else [1, 0] + list(range(NT - 1, 1, -1))
                            K.memset(Sst[:], 0.0)
                            K.memset(Sb[0][:], 0.0)
                            si_m = [0]

                            def emit_MX(it, d=d, order=order):
                                p = order[it]
                                b_ = it % 2
                                t0 = p * 128
                                psA = PS[it % 2]
                                psS = [PS[2], PS[3]]
                                psC = PS[4 + it % 2]
                                psY = PS[6]
                                psZ = PS[7]
                                proj_tm(psA, d * 8, 8, t0, pc0=0)
                                K.tt(dt_[b_][:], psA[:, 0:8], dtb[d][:], ALU.add)
                                K.act(dt_[b_][:], dt_[b_][:], AF.Exp)
                                K.act(dt_[b_][:], dt_[b_][:], AF.Ln, bias=1.0)
                                K.tt(dta[b_][:], dt_[b_][:], alg[d][:], ALU.mult)
                                psT = psA[:, 128:512].bitcast(BF16)
                                for j in range(5):
                                    K.op('pe', [psA], [XBC[j], identb[:]], lambda g, j=j, t0=t0, psT=psT: g.transpose(psT[:, j * 128:(j + 1) * 128], xbc[:, j, t0:t0 + 128], identb[:]))
                                K.op('act', [xtm[b_][:]], [psA], lambda g, b_=b_, psT=psT: g.activation(out=xtm[b_][:], in_=psT[:, 0:640], func=AF.Copy))
                                K.tt(Rm[b_][:], TRI(d).rearrange("p (o i) -> p o i", o=1).to_broadcast([128, 8, 128]),
                                     dta[b_][:].rearrange("p (h o) -> p h o", o=1).to_broadcast([128, 8, 128]), ALU.mult, e='pool')
                                for hh in range(2):
                                    K.mm(psS[hh][:], SU(d), Rm[b_][:, hh * 4:(hh + 1) * 4, :].rearrange("p h i -> p (h i)"))
                                    K.act(exs[b_][:, hh * 4:(hh + 1) * 4, :].rearrange("p h i -> p (h i)"), psS[hh][:], AF.Exp)
                                K.mm(psA[:, 8:16], TRI(d), dta[b_][:])
                                K.act(ecum[b_][:], psA[:, 8:16], AF.Exp)
                                for g in range(2):
                                    K.op('pe', [S(psC[:, g * 128:(g + 1) * 128], 'sc')], [XBC[4], XBC[5]],
                                         lambda e_, g=g, t0=t0, psC=psC: e_.matmul(psC[:, g * 128:(g + 1) * 128], lhsT=xbc[64 * g:64 * g + 64, 4, t0:t0 + 128],
                                                                              rhs=xbc[64 * g:64 * g + 64, 5, t0:t0 + 128], start=True, stop=True), pesync=True)
                                K.tt(mdt[b_][:], TRI(d).rearrange("p (o i) -> p o i", o=1).to_broadcast([128, 8, 128]),
                                     dt_[b_][:].rearrange("p (h o) -> p h o", o=1).to_broadcast([128, 8, 128]), ALU.mult, e='pool')
                                K.tt(mdt[b_][:], mdt[b_][:], exs[b_][:], ALU.mult, e='pool')
                                for g in range(2):
                                    K.tt(Wt[b_][:, g * 4:(g + 1) * 4, :], mdt[b_][:, g * 4:(g + 1) * 4, :],
                                         S(psC[:, g * 128:(g + 1) * 128].rearrange("p (o i) -> p o i", o=1).to_broadcast([128, 4, 128]), 'sc'), ALU.mult)
                                chunks = [(0, 63), (64, 127)] if d == 0 else [(64, 64), (0, 0)]
                                for (c0, lastcol) in chunks:
                                    K.tt(wdt[b_][c0:c0 + 64, :], exs[b_][c0:c0 + 64, :, lastcol], dt_[b_][c0:c0 + 64, :], ALU.mult)
                                K.tt(xw[b_][:].rearrange("p (h q) -> p h q", q=64), xtm[b_][:, 0:512].rearrange("p (h q) -> p h q", q=64),
                                     wdt[b_][:].rearrange("p (h o) -> p h o", o=1).to_broadcast([128, 8, 64]), ALU.mult)

                            def emit_MY(it, d=d, order=order):
                                p = order[it]
                                b_ = it % 2
                                t0 = p * 128
                                psA = PS[it % 2]
                                psC = PS[4 + it % 2]
                                psY = PS[6]
                                psZ = PS[7]
                                si = si_m[0]
                                chunks = [(0, 63), (64, 127)] if d == 0 else [(64, 64), (0, 0)]
                                for hd in range(8):
                                    K.mm(psY[:, hd * 64:(hd + 1) * 64], Wt[b_][:, hd, :], xtm[b_][:, hd * 64:(hd + 1) * 64])
                                for ci, (c0, lastcol) in enumerate(chunks):
                                    for g in range(2):
                                        K.op('pe', [S(psZ[c0:c0 + 64, g * 256:(g + 1) * 256], 'z')], [XBC[5], Sb[si % 2][:]],
                                             lambda e_, g=g, c0=c0, t0=t0, s_=Sb[si % 2]: e_.matmul(psZ[c0:c0 + 64, g * 256:(g + 1) * 256], lhsT=xbc[64 * g:64 * g + 64, 5, t0 + c0:t0 + c0 + 64],
                                                                                                 rhs=s_[64 * g:64 * g + 64, :], start=True, stop=True), pesync=True)
                                    for g in range(2):
                                        K.op('pe', [S(psC[64 * g:64 * g + 64, 256:512], 'cs')], [xtm[b_][:], xw[b_][:]],
                                             lambda e_, g=g, c0=c0, b_=b_, psC=psC: e_.matmul(psC[64 * g:64 * g + 64, 256:512], lhsT=xtm[b_][c0:c0 + 64, 512 + 64 * g:512 + 64 * g + 64],
                                                                                            rhs=xw[b_][c0:c0 + 64, g * 256:(g + 1) * 256], start=True, stop=True), pesync=True)
                                        K.op('pe', [S(psA[64 * g:64 * g + 64, 16 + 4 * ci:20 + 4 * ci], 'ed')], [dta[b_][:]],
                                             lambda e_, g=g, c0=c0, ci=ci, b_=b_, psA=psA: e_.matmul(psA[64 * g:64 * g + 64, 16 + 4 * ci:20 + 4 * ci], lhsT=cst[c0:c0 + 64, CI_ALL, 0:64],
                                                                                                   rhs=dta[b_][c0:c0 + 64, 4 * g:4 * g + 4], start=True, stop=True), pesync=True)
                                    ed = edec[(it * 2 + ci) % 4]
                                    K.act(ed[:], S(psA[:, 16 + 4 * ci:20 + 4 * ci], 'ed'), AF.Exp)
                                    K.tt(Sst[:].rearrange("p (r q) -> p r q", q=64), Sst[:].rearrange("p (r q) -> p r q", q=64),
                                         ed[:].rearrange("p (r o) -> p r o", o=1).to_broadcast([128, 4, 64]), ALU.mult)
                                    K.tt(Sst[:], Sst[:], S(psC[:, 256:512], 'cs'), ALU.add)
                                    si += 1
                                    si_m[0] = si
                                    K.copy(Sb[si % 2][:], Sst[:], e='act')
                                K.tt(ya[b_][:].rearrange("p (h q) -> p h q", q=64), S(psZ[:].rearrange("p (h q) -> p h q", q=64), 'z'),
                                     ecum[b_][:].rearrange("p (h o) -> p h o", o=1).to_broadcast([128, 8, 64]), ALU.mult)
                                K.tt(ya[b_][:], ya[b_][:], psY[:], ALU.add)
                                if d == 0:
                                    K.tt(yb[b_][:].rearrange("p (h q) -> p h q", q=64), xtm[b_][:, 0:512].rearrange("p (h q) -> p h q", q=64),
                                         m2d[:].rearrange("p (h o) -> p h o", o=1).to_broadcast([128, 8, 64]), ALU.mult, e='pool')
                                    K.tt(ya[b_][:], ya[b_][:], yb[b_][:], ALU.add, e='pool')
                                    K.dma('sp', S(ym_d[t0:t0 + 128, :], p), ya[b_][:])
                                else:
                                    K.dma('sp', yb[b_][:], S(ym_d[t0:t0 + 128, :], p))
                                    K.tt(ya[b_][:], ya[b_][:], yb[b_][:], ALU.add, e='pool')
                                    proj_tm(psY, 16, 512, t0, pc0=0)
                                    K.act(zt[b_][:], psY[:], AF.Exp, scale=-1.0)
                                    K.act(zt[b_][:], zt[b_][:], AF.Ln, bias=1.0)
                                    K.act(zt[b_][:], zt[b_][:], AF.Exp, scale=-1.0)
                                    K.tt(zt[b_][:], zt[b_][:], psY[:], ALU.mult)
                                    K.tt(ya[b_][:], ya[b_][:], zt[b_][:], ALU.mult)
                                    K.act(y2[b_][:], ya[b_][:], AF.Square, accum=ssq[b_][:])
                                    K.rsqrt(ssq[b_][:], ssq[b_][:], 1.0 / 512)
                                    K.stt(ybf[b_][:], ya[b_][:], ssq[b_][:, 0:1], m2n[:], ALU.mult, ALU.mult)
                                    psT2 = psZ[:, 0:256].bitcast(BF16)
                                    for j in range(4):
                                        K.op('pe', [S(psZ[:], 'z')], [ybf[b_][:], identb[:]],
                                             lambda g, j=j, b_=b_, psT2=psT2: g.transpose(psT2[:, j * 128:(j + 1) * 128], ybf[b_][:, j * 128:(j + 1) * 128], identb[:]))
                                    K.op('act', [yfm[b_][:]], [S(psZ[:], 'z')], lambda g, b_=b_, psT2=psT2: g.activation(out=yfm[b_][:].rearrange("p j t -> p (j t)"), in_=psT2, func=AF.Copy))
                                    K.dma('sp', S(ys_d[3, :, :, t0:t0 + 128], (3, p)), yfm[b_][:])

                            emit_MX(0)
                            for it in range(NT):
                                if it + 1 < NT:
                                    emit_MX(it + 1)
                                emit_MY(it)
                    K.barrier()
                p15.close()
                K.barrier()

                if 'G' in phases:
                    with contextlib.ExitStack() as st:
                        wg = sb(st, "wg", [128, 4, 8, D], BF16)
                        wb = sb(st, "wb", [128, 4, 4, D], BF16)
                        wo = sb(st, "wo", [128, 8, D], BF16)
                        NSTG = 3
                        stg = [sb(st, "stg%d" % i, [128, 1024]) for i in range(NSTG)]
                        dq = ['sp', 'act', 'pool']
                        ce = ['dve', 'act', 'dve']
                        n_ = 0
                        for k in range(4):
                            for kt in range(8):
                                s_ = stg[n_ % NSTG]; q_ = dq[n_ % NSTG]; c_ = ce[n_ % NSTG]; n_ += 1
                                K.dma(q_, s_[:], wgate_d[l, k, kt * 128:(kt + 1) * 128, :])
                                K.copy(wg[:, k, kt, :], s_[:], e=c_)
                            for kt in range(4):
                                s_ = stg[n_ % NSTG]; q_ = dq[n_ % NSTG]; c_ = ce[n_ % NSTG]; n_ += 1
                                K.dma(q_, s_[:], wbr_d[l, k, kt * 128:(kt + 1) * 128, :])
                                K.copy(wb[:, k, kt, :], s_[:], e=c_)
                        for kt in range(8):
                            s_ = stg[n_ % NSTG]; q_ = dq[n_ % NSTG]; c_ = ce[n_ % NSTG]; n_ += 1
                            K.dma(q_, s_[:], wout_d[l, kt * 128:(kt + 1) * 128, :])
                            K.copy(wo[:, kt, :], s_[:], e=c_)
                        gtrow = sb(st, "gtrow", [128, 2, D])
                        gtb = rowb(st, f"gtb{l}", "gtbrow")
                        lb_ = sb(st, "lbc_", [128, 8, 128])
                        fnr = rowb(st, "final_norm", "fnr") if last else None
                        for cc in range(2):
                            for kt in range(8):
                                K.copy(lb_[:, kt, :], silc[:, kt, cc:cc + 1].to_broadcast([128, 128]))
                            for hf in range(2):
                                ps = PS[hf]
                                for kt in range(8):
                                    s_ = stg[n_ % NSTG]; q_ = dq[n_ % NSTG]; c_ = ce[n_ % NSTG]; n_ += 1
                                    K.dma(q_, s_[:, 0:512], wmod_d[l, kt * 128:(kt + 1) * 128, 2048 + hf * 512:2048 + (hf + 1) * 512])
                                    K.mm(ps[:], lb_[:, kt, :], s_[:, 0:512], start=(kt == 0), stop=(kt == 7))
                                K.tt(gtrow[:, cc, hf * 512:(hf + 1) * 512], ps[:], gtb[:, hf * 512:(hf + 1) * 512], ALU.add)

                        def tl(name, shape, dt=F32, n=2):
                            return [sb(st, "%s%d" % (name, i), list(shape), dt) for i in range(n)]
                        hsl = tl("hsl", [128, 8, 256], BF16)
                        ysl = tl("ysl", [128, 4, 4, 256], BF16)
                        macc = tl("macc", [128, 256], F32, 2)
                        gsg = tl("gsg", [128, 256], F32, 2)
                        mT = tl("mT", [128, 8, 256], BF16)
                        xrow = tl("xrow", [128, D], F32, 2)
                        orow = tl("orow", [128, D], F32, 2)
                        jk = stg[0]
                        ssq = tl("fss", [128, 1])
                        for si_, (t0, n) in enumerate(SLABS256):
                            if last and si_ == 0:
                                continue
                            b_ = si_ % 2
                            cc = 1 if si_ == 0 else 0
                            K.dma('sp', hsl[b_][:], hs_d[:, :, t0:t0 + n])
                            for k in range(4):
                                K.dma('pool', ysl[b_][:, k], ys_d[k, :, :, t0:t0 + n])
                            for oc in range(8):
                                for k in range(4):
                                    psg = PS[(oc * 4 + k) % 2]
                                    psb = PS[2 + (oc * 4 + k) % 2]
                                    for kt in range(8):
                                        K.mm(psg[:, 0:n], wg[:, k, kt, oc * 128:(oc + 1) * 128], hsl[b_][:, kt, :], start=(kt == 0), stop=(kt == 7))
                                    for kt in range(4):
                                        K.mm(psb[:, 0:n], wb[:, k, kt, oc * 128:(oc + 1) * 128], ysl[b_][:, k, kt, :], start=(kt == 0), stop=(kt == 3))
                                    g_ = gsg[(oc * 4 + k) % 2]
                                    K.act(g_[:], psg[:, 0:n], AF.Sigmoid, bias=col(f"b_gate{l}{k}", oc))
                                    m_ = macc[oc % 2]
                                    if k == 0:
                                        K.tt(m_[:], g_[:], psb[:, 0:n], ALU.mult)
                                    else:
                                        K.tt(g_[:], g_[:], psb[:, 0:n], ALU.mult)
                                        K.tt(m_[:], m_[:], g_[:], ALU.add, e='pool')
                                K.copy(mT[b_][:, oc, :], macc[oc % 2][:], e='act')
                            for tt_ in range(2):
                                r0 = t0 + tt_ * 128
                                xr_ = xrow[tt_]
                                K.dma('pool', xr_[:], xin_d[r0:r0 + 128, :])
                                for hf in range(2):
                                    ps = PS[4 + hf]
                                    for kt in range(8):
                                        K.mm(ps[:], mT[b_][:, kt, tt_ * 128:(tt_ + 1) * 128], wo[:, kt, hf * 512:(hf + 1) * 512], start=(kt == 0), stop=(kt == 7))
                                    K.tt(orow[tt_][:, hf * 512:(hf + 1) * 512], ps[:], gtrow[:, cc, hf * 512:(hf + 1) * 512], ALU.mult)
                                K.tt(orow[tt_][:], orow[tt_][:], xr_[:], ALU.add, e='pool')
                                if not last:
                                    K.dma('sp', x1_d[r0:r0 + 128, :], orow[tt_][:])
                                else:
                                    s2 = ssq[tt_]
                                    K.act(jk[:], orow[tt_][:], AF.Square, accum=s2[:])
                                    K.rsqrt(s2[:], s2[:], 1.0 / D)
                                    K.stt(orow[tt_][:], orow[tt_][:], s2[:, 0:1], fnr[:], ALU.mult, ALU.mult)
                                    K.dma('sp', out_d[r0 - NCTX:r0 - NCTX + 128, :], orow[tt_][:])
                    K.barrier()
        K.barrier()
        print("instructions:", K.ninst, {e: K.cnt[e] for e in K.cnt}, "sems", K.nsem)
    return nc


def make_inputs(inp):
    shared, coff, roff = prep_shared(inp)
    maps = []
    for b in range(8):
        m = dict(shared)
        m["xc"] = np.ascontiguousarray(np.concatenate([inp["ctx"][b], inp["x"][b]], axis=0), np.float32)
        cc = np.stack([np.asarray(inp["c"][b], np.float32).reshape(8, 128).T, np.asarray(inp["c_ctx"], np.float32).reshape(8, 128).T], axis=-1)
        m["ccol"] = np.ascontiguousarray(cc, np.float32)
        maps.append(m)
    return shared, coff, roff, maps


def kernel(**inputs):
    inp = {k: np.asarray(v) for k, v in inputs.items()}
    shared, coff, roff, maps = make_inputs(inp)
    nc = build(shared, coff, roff)
    res = run_bass_kernel_spmd(nc, maps, core_ids=list(range(8)))
    out = np.stack([np.asarray(r["out"], np.float32) for r in res.results], axis=0)
    return out
```

```python
import os
import contextlib
import numpy as np
import concourse.bass as bass
import concourse.mybir as mybir
from concourse.bass_utils import run_bass_kernel_spmd
from concourse.ap import AP

F32 = mybir.dt.float32
BF16 = mybir.dt.bfloat16
AF = mybir.ActivationFunctionType
ALU = mybir.AluOpType

T = 4352
NCTX = 256
NLAT = 4096
D = 1024
NT = 34
EPS = 1e-6
PI = float(np.pi)
C_Q, C_I, C_FF, C_FB, C_HZ, C_SU, C_SZ, C_LX, C_LZ, C_XBC, C_DT, C_MZ = (
    0, 512, 1024, 1536, 2048, 2560, 3072, 3584, 4096, 4608, 5376, 5392)
IN_COLS = 5904
SL = 256
SLABS = [(0, 256)] + [(256 + 512 * i, 512) for i in range(8)]
SLABS256 = [(256 * i, 256) for i in range(17)]


def rev(ap):
    pat = [list(p) for p in ap.ap]
    step, cnt = pat[-1]
    off = ap.offset + step * (cnt - 1)
    pat[-1] = [-step, cnt]
    return AP(ap.tensor, off, pat)


class S:
    def __init__(self, ap, sub):
        self.ap = ap
        self.sub = sub


def _ap(x):
    return x.ap if isinstance(x, S) else x


def _key(x):
    if isinstance(x, S):
        nm = getattr(x.ap, 'tensor', x.ap).name
        if nm.startswith('ps'):
            return (nm, None)
        return (nm, x.sub)
    return (getattr(x, 'tensor', x).name, None)


class Sched:
    def __init__(self, nc, stack):
        self.nc = nc
        self.eng = {'pe': nc.tensor, 'dve': nc.vector, 'act': nc.scalar, 'pool': nc.gpsimd, 'sp': nc.sync}
        self.cnt = {e: 0 for e in self.eng}
        self.EPOCH = 20000
        self.sems = {e: [] for e in self.eng}
        self.stack = stack
        self.seen = {e: {} for e in self.eng}
        self.lastw = {}
        self.readers = {}
        self.nsem = 0
        self.dma_slots = {}
        self.dma_next = {}
        for q in ('sp', 'pool', 'act'):
            n = 12 if q == 'sp' else 6
            self.dma_slots[q] = [[self._newsem(), 0] for _ in range(n)]
            self.dma_next[q] = 0
        self.ninst = 0
        self.nopool = False

    def _newsem(self):
        s = self.stack.enter_context(self.nc.semaphore("s%d" % self.nsem))
        self.nsem += 1
        return (self.nsem, s)

    def _token(self, e):
        n = self.cnt[e]
        ep = n // self.EPOCH
        while len(self.sems[e]) <= ep:
            self.sems[e].append(self._newsem())
        sid, sem = self.sems[e][ep]
        self.cnt[e] = n + 1
        return (sid, sem, n - ep * self.EPOCH + 1, e)

    def _wait(self, e, toks, pesync=False):
        best = {}
        for t in toks:
            if t is None:
                continue
            if e == 'pe' and t[3] == 'pe' and not pesync:
                continue
            if t[0] not in best or best[t[0]][2] < t[2]:
                best[t[0]] = t
        for sid, t in best.items():
            if self.seen[e].get(sid, 0) < t[2]:
                self.eng[e].wait_ge(t[1], t[2])
                self.seen[e][sid] = t[2]

    @staticmethod
    def _psplit(outs, ins):
        o2 = list(outs)
        i2 = []
        for x in ins:
            if _key(x)[0].startswith('ps'):
                o2.append(x)
            else:
                i2.append(x)
        return o2, i2

    def _deps(self, outs, ins):
        outs, ins = self._psplit(outs, ins)
        toks = []
        for x in ins:
            toks.append(self.lastw.get(_key(x)))
        for x in outs:
            k = _key(x)
            toks.append(self.lastw.get(k))
            toks.extend(self.readers.get(k, {}).values())
        return toks

    def _record(self, tok, outs, ins):
        outs, ins = self._psplit(outs, ins)
        for x in ins:
            self.readers.setdefault(_key(x), {})[tok[0]] = tok
        for x in outs:
            k = _key(x)
            self.lastw[k] = tok
            self.readers[k] = {}

    def op(self, e, outs, ins, fn, pesync=False):
        self._wait(e, self._deps(outs, ins), pesync)
        inst = fn(self.eng[e])
        tok = self._token(e)
        inst.then_inc(tok[1], 1)
        self._record(tok, outs, ins)
        self.ninst += 1
        return inst

    def dma(self, q, out, in_, **kw):
        slots = self.dma_slots[q]
        i = self.dma_next[q]
        self.dma_next[q] = (i + 1) % len(slots)
        (sid, sem), val = slots[i]
        toks = self._deps([out], [in_])
        if val > 0:
            toks.append((sid, sem, val, 'dma'))
        self._wait(q, toks)
        self.eng[q].dma_start(out=_ap(out), in_=_ap(in_), **kw).then_inc(sem, 16)
        slots[i][1] = val + 16
        tok = (sid, sem, val + 16, 'dma')
        self._record(tok, [out], [in_])
        self.ninst += 1

    def barrier(self):
        toks = []
        for e in self.eng:
            n = self.cnt[e]
            if n == 0:
                continue
            ep = (n - 1) // self.EPOCH
            sid, sem = self.sems[e][ep]
            toks.append((sid, sem, n - ep * self.EPOCH, e + '_b'))
        for q, slots in self.dma_slots.items():
            for (sid, sem), val in slots:
                if val > 0:
                    toks.append((sid, sem, val, 'dma'))
        for e in self.eng:
            self._wait(e, toks)
        self.lastw = {}
        self.readers = {}

    def tt(self, out, in0, in1, op, e='dve'):
        if e == 'pool' and self.nopool:
            e = 'dve'
        return self.op(e, [out], [in0, in1], lambda g: g.tensor_tensor(out=_ap(out), in0=_ap(in0), in1=_ap(in1), op=op))

    def ts(self, out, in0, s1, op0, s2=None, op1=None, e='dve'):
        ins = [in0] + [s for s in (s1, s2) if isinstance(s, (AP, S))]
        if op1 is None:
            return self.op(e, [out], ins, lambda g: g.tensor_scalar(out=_ap(out), in0=_ap(in0), scalar1=_ap(s1) if isinstance(s1, (AP, S)) else s1, scalar2=None, op0=op0))
        return self.op(e, [out], ins, lambda g: g.tensor_scalar(out=_ap(out), in0=_ap(in0), scalar1=_ap(s1) if isinstance(s1, (AP, S)) else s1,
                                                                 scalar2=_ap(s2) if isinstance(s2, (AP, S)) else s2, op0=op0, op1=op1))

    def stt(self, out, in0, sc, in1, op0, op1, e='dve'):
        ins = [in0, in1] + ([sc] if isinstance(sc, (AP, S)) else [])
        return self.op(e, [out], ins, lambda g: g.scalar_tensor_tensor(out=_ap(out), in0=_ap(in0), scalar=_ap(sc) if isinstance(sc, (AP, S)) else sc,
                                                                        in1=_ap(in1), op0=op0, op1=op1))

    def act(self, out, in_, func, bias=None, scale=None, accum=None):
        ins = [in_] + [s for s in (bias, scale) if isinstance(s, (AP, S))]
        outs = [out] + ([accum] if accum is not None else [])
        kw = {}
        if bias is not None:
            kw['bias'] = _ap(bias) if isinstance(bias, (AP, S)) else bias
        if scale is not None:
            kw['scale'] = _ap(scale) if isinstance(scale, (AP, S)) else scale
        if accum is not None:
            kw['accum_out'] = _ap(accum)
        return self.op('act', outs, ins, lambda g: g.activation(out=_ap(out), in_=_ap(in_), func=func, **kw))

    def rsqrt(self, out, in_, scale):
        self.act(out, in_, AF.Ln, bias=self.epsc, scale=scale)
        return self.act(out, out, AF.Exp, scale=-0.5)

    def sinred(self, out, x, shift, tmp, negpi):
        MAGIC = 12582912.0
        C1 = 6.28125
        C2 = float(2 * np.pi - 6.28125)
        xs = x
        if shift != 0.0:
            self.ts(out, x, float(shift), ALU.add)
            xs = out
        self.ts(tmp, xs, 1.0 / (2 * np.pi), ALU.mult, MAGIC, ALU.add)
        self.ts(tmp, tmp, -MAGIC, ALU.add)
        self.stt(out, tmp, -C1, xs, ALU.mult, ALU.add)
        self.stt(out, tmp, -C2, out, ALU.mult, ALU.add)
        self.ts(out, out, PI, ALU.min, -PI, ALU.max)
        return self.act(out, out, AF.Sin)

    def recip(self, out, in_):
        return self.op('dve', [out], [in_], lambda g: g.reciprocal(out=_ap(out), in_=_ap(in_)))

    def copy(self, out, in_, e='dve'):
        if e == 'act':
            return self.act(out, in_, AF.Copy)
        return self.op(e, [out], [in_], lambda g: g.tensor_copy(out=_ap(out), in_=_ap(in_)))

    def memset(self, out, val, e='dve'):
        return self.op(e, [out], [], lambda g: g.memset(_ap(out), val))

    def mm(self, out, lhsT, rhs, start=True, stop=True):
        return self.op('pe', [out], [lhsT, rhs], lambda g: g.matmul(_ap(out), lhsT=_ap(lhsT), rhs=_ap(rhs), start=start, stop=stop))

    def tr(self, out, in_, ident):
        return self.op('pe', [out], [in_, ident], lambda g: g.transpose(_ap(out), _ap(in_), _ap(ident)))

    def scan(self, out, d0, d1, init, op0=ALU.mult, op1=ALU.add):
        ins = [d0, d1] + ([init] if isinstance(init, (AP, S)) else [])
        return self.op('dve', [out], ins, lambda g: g.tensor_tensor_scan(_ap(out), _ap(d0), _ap(d1), _ap(init) if isinstance(init, (AP, S)) else init, op0, op1))


class ColPack:
    def __init__(self):
        self.parts = []
        self.off = {}
        self.n = 0

    def add(self, name, vec):
        v = np.asarray(vec, np.float32).reshape(-1, 128).T
        self.off[name] = (self.n, v.shape[1])
        self.parts.append(v)
        self.n += v.shape[1]

    def add_raw(self, name, arr):
        arr = np.asarray(arr, np.float32)
        self.off[name] = (self.n, arr.shape[1])
        self.parts.append(arr)
        self.n += arr.shape[1]

    def array(self):
        return np.ascontiguousarray(np.concatenate(self.parts, axis=1))


class RowPack:
    def __init__(self):
        self.parts = []
        self.off = {}
        self.n = 0

    def add(self, name, vec):
        v = np.asarray(vec, np.float32).reshape(-1)
        self.off[name] = (self.n, v.size)
        self.parts.append(v)
        self.n += v.size

    def array(self):
        v = np.concatenate(self.parts)
        return np.ascontiguousarray(np.broadcast_to(v[None, :], (128, v.size)))


def const_mats():
    j = np.arange(128)[:, None]
    i = np.arange(128)[None, :]
    same = (j // 64) == (i // 64)
    ident = (j == i).astype(np.float32)
    tri_f = ((j <= i) & same).astype(np.float32)
    tri_b = ((j >= i) & same).astype(np.float32)
    su_f = ((j > i) & same).astype(np.float32)
    su_b = ((j < i) & same).astype(np.float32)
    ones = same.astype(np.float32)
    e2m_f = tri_f - 0.5 * ones
    e2m_b = tri_b - 0.5 * ones
    allone = np.ones((128, 128), np.float32)
    iota = np.broadcast_to(np.arange(1, 129, dtype=np.float32)[None, :], (128, 128))
    iota2 = np.broadcast_to(np.arange(129, 257, dtype=np.float32)[None, :], (128, 128))
    m = np.stack([ident, tri_f, tri_b, su_f, su_b, e2m_f, e2m_b, ones, allone, iota, iota2], axis=1)
    return np.ascontiguousarray(m.astype(np.float32))


CI_ID, CI_TRI, CI_SU, CI_E2M, CI_ONES, CI_ALL, CI_IOTA = 0, 1, 3, 5, 7, 8, 9


def prep_shared(inp):
    cp = ColPack()
    rp = RowPack()
    for l in range(2):
        cp.add(f"norm_w{l}", inp["norm_w"][l])
        cp.add(f"b_mod{l}", inp["b_mod"][l])
        cp.add(f"hg_norm{l}", inp["hg_norm"][l])
        for d in range(2):
            cp.add(f"s5_are{l}{d}", inp["s5_a_re"][l, d])
            cp.add(f"s5_aim{l}{d}", inp["s5_a_im"][l, d])
            cp.add(f"s5_ls{l}{d}", np.repeat(inp["s5_log_step"][l, d], 64))
            cp.add(f"lru_lam{l}{d}", inp["lru_lam"][l, d])
            for g in range(2):
                cp.add(f"lru_gb{l}{d}{g}", inp["lru_gate_b"][l, d, g])
        cp.add(f"s5_d{l}", inp["s5_d"][l])
        cp.add(f"s5_bglu{l}", inp["s5_b_glu"][l])
        for k in range(4):
            cp.add(f"lru_cw{l}{k}", inp["lru_conv_w"][l, k])
            cp.add(f"m2_cw{l}{k}", inp["m2_conv_w"][l, k])
            cp.add(f"b_gate{l}{k}", inp["b_gate"][l, k])
        cp.add(f"lru_cb{l}", inp["lru_conv_b"][l])
        cp.add(f"m2_cb{l}", inp["m2_conv_b"][l])
        rp.add(f"gtb{l}", inp["b_mod"][l][2048:3072])
        rp.add(f"m2_norm{l}", inp["m2_norm"][l])
        rp.add(f"m2_d{l}", inp["m2_d"][l])
        for d in range(2):
            rp.add(f"m2_dtb{l}{d}", inp["m2_dt_bias"][l, d])
            rp.add(f"m2_alog{l}{d}", inp["m2_a_log"][l, d])
    for l in range(3):
        for d in range(2):
            cp.add(f"hg_lg{l}{d}", inp["hg_lb_logits"][l, d])
            rp.add(f"hg_lg{l}{d}", inp["hg_lb_logits"][l, d])
    rp.add("final_norm", inp["final_norm"])
    bq = np.zeros((2, 2, 128, 16, 128), np.float32)
    cq = np.zeros((2, 2, 128, 16, 64), np.float32)
    for l in range(2):
        for ri, (bn, cn) in enumerate((("s5_b_re", "s5_c_re"), ("s5_b_im", "s5_c_im"))):
            B = inp[bn][l]
            C = inp[cn][l]
            for g in range(32):
                k = g // 2
                gl = g % 2
                c = k // 4
                kq = k % 4
                bq[l, ri, 32 * kq + 16 * gl:32 * kq + 16 * gl + 16, k, 64 * gl:64 * gl + 64] = B[g].T
                cq[l, ri, 64 * gl:64 * gl + 64, k, 32 * (kq % 2) + 16 * gl:32 * (kq % 2) + 16 * gl + 16] = C[g].T
    gw = np.zeros((2, 128, 16, 128), np.float32)
    for l in range(2):
        for d in range(2):
            for g in range(2):
                for j in range(4):
                    for bl in range(2):
                        gw[l, 64 * bl:64 * bl + 64, (d * 2 + g) * 4 + j, 64 * bl:64 * bl + 64] = inp["lru_gate_w"][l, d, g, 2 * j + bl]
    shared = {
        "cols": cp.array(), "rows": rp.array(), "cst": const_mats(),
        "s5bq": np.ascontiguousarray(bq), "s5cq": np.ascontiguousarray(cq), "lrugw": np.ascontiguousarray(gw),
        "w_mod": np.ascontiguousarray(inp["w_mod"], np.float32), "w_in": np.ascontiguousarray(inp["w_in"], np.float32),
        "w_gate": np.ascontiguousarray(inp["w_gate"], np.float32), "w_branch": np.ascontiguousarray(inp["w_branch"], np.float32),
        "w_out": np.ascontiguousarray(inp["w_out"], np.float32), "w_glu": np.ascontiguousarray(inp["s5_w_glu"], np.float32),
    }
    return shared, cp.off, rp.off


def build(shared, coff, roff, dbg=None):
    dbg = dbg or {}
    layers = dbg.get("layers", [0, 1])
    phases = dbg.get("phases", "HSLMG")
    nc = bass.Bass("TRN2", target_bir_lowering=False)
    okind = "ExternalOutput" if dbg.get("dump") else "Internal"

    def din(name, shape, dt=F32):
        return nc.dram_tensor(name, list(shape), dt, kind="ExternalInput").ap()

    xc_d = din("xc", [T, D])
    ccol_d = din("ccol", [128, 8, 2])
    cols_d = din("cols", list(shared["cols"].shape))
    rows_d = din("rows", list(shared["rows"].shape))
    cst_d = din("cst", [128, 11, 128])
    s5bq_d = din("s5bq", [2, 2, 128, 16, 128])
    s5cq_d = din("s5cq", [2, 2, 128, 16, 64])
    lrugw_d = din("lrugw", [2, 128, 16, 128])
    wmod_d = din("w_mod", [2, D, 3072])
    win_d = din("w_in", [2, D, IN_COLS])
    wgate_d = din("w_gate", [2, 4, D, D])
    wbr_d = din("w_branch", [2, 4, 512, D])
    wout_d = din("w_out", [2, D, D])
    wglu_d = din("w_glu", [2, 512, 512])
    out_d = nc.dram_tensor("out", [NLAT, D], F32, kind="ExternalOutput").ap()
    x1_d = nc.dram_tensor("x1s", [T, D], F32, kind=okind).ap()
    hs_d = nc.dram_tensor("hs", [128, 8, T], BF16, kind="Internal").ap()
    ys_d = nc.dram_tensor("ys", [4, 128, 4, T], BF16, kind=okind).ap()
    gs_d = nc.dram_tensor("gs", [128, 4, T], BF16, kind=okind).ap()
    ym_d = nc.dram_tensor("ym", [T, 512], F32, kind="Internal").ap()
    ysd_d = nc.dram_tensor("ysd", [128, 4, T], F32, kind=okind).ap() if dbg.get("dump") else None

    NCOLS = shared["cols"].shape[1]

    with contextlib.ExitStack() as gstack:
        K = Sched(nc, gstack)
        K.nopool = dbg.get('nopool', True)

        uniq = [0]

        def sb(stack, name, shape, dt=F32):
            uniq[0] += 1
            return stack.enter_context(nc.sbuf_tensor("%s_u%d" % (name, uniq[0]), list(shape), dt))

        cst = sb(gstack, "cst", [128, 11, 128])
        cols = sb(gstack, "cols", [128, NCOLS])
        identb = sb(gstack, "identb", [128, 128], BF16)
        ccol = sb(gstack, "ccol", [128, 8, 2])
        PS = [gstack.enter_context(nc.psum_tensor("ps%d" % i, [128, 512], F32)) for i in range(8)]
        K.dma('sp', cst[:], cst_d)
        K.dma('sp', cols[:], cols_d)
        K.dma('sp', ccol[:], ccol_d)
        K.copy(identb[:], cst[:, CI_ID, :])
        epsc = sb(gstack, "epsc", [128, 1])
        K.memset(epsc[:], EPS)
        K.epsc = epsc[:, 0:1]

        def col(name, j=0, n=1):
            o, k = coff[name]
            return cols[:, o + j:o + j + n]

        def rowb(stack, name, tname):
            o, n = roff[name]
            t = sb(stack, tname, [128, n])
            K.dma('sp', t[:], rows_d[:, o:o + n])
            return t

        ident = cst[:, CI_ID, :]
        ONESBD = cst[:, CI_ONES, :]
        ALLONE = cst[:, CI_ALL, :]
        IOTA = cst[:, CI_IOTA, :]

        def TRI(d):
            return cst[:, CI_TRI + d, :]

        def SU(d):
            return cst[:, CI_SU + d, :]

        def E2M(d):
            return cst[:, CI_E2M + d, :]

        for l in layers:
            xin_d = xc_d if l == 0 else x1_d
            last = (l == 1)
            with contextlib.ExitStack() as lstack:
                modcol = sb(lstack, "modcol", [128, 24, 2])
                g1col = sb(lstack, "g1col", [128, 8, 2])
                silc = sb(lstack, "silc", [128, 8, 2])
                K.act(silc[:], ccol[:], AF.Silu)
                with contextlib.ExitStack() as st:
                    wm = [sb(st, "wm%d" % i, [128, 8, 512]) for i in range(2)]
                    for jb in range(6):
                        w = wm[jb % 2]
                        K.dma('sp', w[:], wmod_d[l, :, jb * 512:(jb + 1) * 512].rearrange("(k p) n -> p k n", p=128))
                        for jj in range(4):
                            j = jb * 4 + jj
                            for kt in range(8):
                                K.mm(PS[0][:, j * 2:j * 2 + 2], w[:, kt, jj * 128:(jj + 1) * 128], silc[:, kt, :], start=(kt == 0), stop=(kt == 7))
                    o, _ = coff[f"b_mod{l}"]
                    K.tt(modcol[:], PS[0][:, 0:48].rearrange("p (j c) -> p j c", c=2),
                         cols[:, o:o + 24].rearrange("p (j c) -> p j c", c=1).to_broadcast([128, 24, 2]), ALU.add)
                    o, _ = coff[f"norm_w{l}"]
                    K.ts(g1col[:], modcol[:, 8:16, :], 1.0, ALU.add)
                    K.tt(g1col[:], g1col[:], cols[:, o:o + 8].rearrange("p (j c) -> p j c", c=1).to_broadcast([128, 8, 2]), ALU.mult)
                shcol = modcol[:, 0:8, :]

                p15 = contextlib.ExitStack()
                hT = sb(p15, "hT", [128, 8, T], BF16)
                wst = [sb(p15, "wst%d" % i, [128, 8, 128]) for i in range(2)]
                wblk = sb(p15, "wblk", [128, 8, 768], BF16)
                wst_i = [0]

                def load_w(colranges):
                    pos = 0
                    for (c0, n) in colranges:
                        for s0 in range(0, n, 128):
                            m = min(128, n - s0)
                            w = wst[wst_i[0] % 2]
                            wst_i[0] += 1
                            K.dma('sp', w[:, :, 0:m], win_d[l, :, c0 + s0:c0 + s0 + m].rearrange("(k p) n -> p k n", p=128))
                            K.copy(wblk[:, :, pos:pos + m], w[:, :, 0:m], e='dve')
                            pos += m
                    return pos

                def proj_fm(ps, wc0, ncols, t0, n):
                    for kt in range(8):
                        K.mm(ps[0:ncols, 0:n], wblk[:, kt, wc0:wc0 + ncols], hT[:, kt, t0:t0 + n], start=(kt == 0), stop=(kt == 7))

                def proj_tm(ps, wc0, ncols, t0, pc0=0):
                    for kt in range(8):
                        K.mm(ps[:, pc0:pc0 + ncols], hT[:, kt, t0:t0 + 128], wblk[:, kt, wc0:wc0 + ncols], start=(kt == 0), stop=(kt == 7))

                with contextlib.ExitStack() as st:
                    xr = [sb(st, "xr%d" % i, [128, D]) for i in range(2)]
                    xn = [sb(st, "xn%d" % i, [128, D]) for i in range(2)]
                    junk = sb(st, "junk", [128, D])
                    ss = [sb(st, "ss%d" % i, [128, 1]) for i in range(2)]
                    for p in range(NT):
                        cc = 1 if p < 2 else 0
                        a, b, s_ = xr[p % 2], xn[p % 2], ss[p % 2]
                        K.dma('sp' if p % 2 == 0 else 'pool', a[:], xin_d[p * 128:(p + 1) * 128, :])
                        K.act(junk[:], a[:], AF.Square, accum=s_[:])
                        K.rsqrt(s_[:], s_[:], 1.0 / D)
                        K.ts(b[:], a[:], s_[:, 0:1], ALU.mult)
                        for half in range(2):
                            ps = PS[(p * 2 + half) % 4]
                            for q in range(4):
                                kt = half * 4 + q
                                K.tr(ps[:, q * 128:(q + 1) * 128], b[:, kt * 128:(kt + 1) * 128], ident)
                            for q in range(4):
                                kt = half * 4 + q
                                K.act(hT[:, kt, p * 128:(p + 1) * 128], ps[:, q * 128:(q + 1) * 128], AF.Identity,
                                      bias=shcol[:, kt, cc:cc + 1], scale=g1col[:, kt, cc:cc + 1])
                    K.dma('sp', hs_d, hT[:])

                if 'H' in phases:
                    with contextlib.ExitStack() as st:
                        stl = contextlib.ExitStack()
                        lbc = sb(st, "lbc", [128, 2, 4])
                        lbr = sb(st, "lbr", [128, 2, 512])
                        lgc = sb(stl, "lgc", [128, 3, 2, 4])
                        lgr = sb(stl, "lgr", [128, 3, 2, 512])
                        tmpc = sb(stl, "tmpc", [128, 2, 4])
                        tmpr = sb(stl, "tmpr", [128, 2, 512])
                        for ll in (range(3) if 'B' in dbg.get("hs", "BAF") else []):
                            for d in range(2):
                                o, _ = coff[f"hg_lg{ll}{d}"]
                                K.act(lgc[:, ll, d, :], cols[:, o:o + 4], AF.Exp)
                                o, n = roff[f"hg_lg{ll}{d}"]
                                K.dma('sp', lgr[:, ll, d, :], rows_d[:, o:o + n])
                        if 'B' in dbg.get("hs", "BAF"):
                            K.act(lgr[:], lgr[:], AF.Exp)
                        for (lg, lb_, tmp) in (((lgc, lbc, tmpc), (lgr, lbr, tmpr)) if 'B' in dbg.get("hs", "BAF") else []):
                            K.tt(tmp[:], lg[:, 0], lg[:, 1], ALU.add)
                            K.tt(tmp[:], tmp[:], lg[:, 2], ALU.add)
                            if l == 0:
                                K.tt(lb_[:], lg[:, 1], lg[:, 2], ALU.add)
                            else:
                                K.copy(lb_[:], lg[:, 2])
                            K.recip(tmp[:], tmp[:])
                            K.tt(lb_[:], lb_[:], tmp[:], ALU.mult)
                        K.barrier()
                        stl.close()
                        qs = sb(st, "qs", [128, T])
                        kkTs = [sb(st, "kkTs%d" % i, [128, T]) for i in range(2)]
                        vt = sb(st, "vt", [128, NT, 128], BF16)
                        osum = sb(st, "osum", [128, T])
                        SstD = [sb(st, "Sst%d" % i, [128, 128]) for i in range(2)]
                        SbfD = [[sb(st, "Sbf%d_%d" % (d_, i), [128, 128], BF16) for i in range(2)] for d_ in range(2)]

                        def tl(name, dt=F32, n=2, shape=(128, 128)):
                            return [sb(st, "%s%d" % (name, i), list(shape), dt) for i in range(n)]
                        def tl2(name, dt=F32):
                            return [tl(name + "d%d_" % d_, dt) for d_ in range(2)]
                        kkM, lfM = tl2("kkM"), tl2("lfM")
                        E1, E2, E3, EU = tl2("E1"), tl2("E2"), tl2("E3"), tl2("EU")
                        qb, qt, ktl, kh, attm = tl2("qb", BF16), tl2("qt", BF16), tl2("ktl", BF16), tl2("kh", BF16), tl2("attm", BF16)
                        o2 = tl("o2", F32, 1, (128, 512)) * 2
                        rs = tl("rs", F32, 2, (128, 512))
                        zz = tl("zz", F32, 2, (128, 512))
                        yo = tl("yo", BF16, 2, (128, 512))
                        for h in range(dbg.get("nheads", 4)):
                            hc = h * 128
                            load_w([(C_Q + hc, 128), (C_I + hc, 128), (C_FF + hc, 128), (C_FB + hc, 128), (C_HZ + hc, 128)])
                            for si_, (t0, n) in enumerate(SLABS):
                                ps = PS[si_ % 2]
                                proj_fm(ps, 0, 128, t0, n)
                                K.act(qs[:, t0:t0 + n], ps[:, 0:n], AF.Silu)
                            for p in range(NT):
                                ps = PS[2 + p % 2]
                                proj_tm(ps, 128, 128, p * 128, pc0=0)
                                K.copy(vt[:, p, :], ps[:, 0:128])
                            for d in range(2):
                                for si_, (t0, n) in enumerate(SLABS):
                                    ps = PS[4 + si_ % 2]
                                    proj_fm(ps, 256 + d * 128, 128, t0, n)
                                    kv_ = kkTs[d][:, t0:t0 + n]
                                    K.act(kv_, ps[:, 0:n], AF.Exp)
                                    K.act(kv_, kv_, AF.Ln, bias=1.0)
                                    K.act(kv_, kv_, AF.Exp, scale=-1.0)
                                    K.ts(kv_, kv_, lbc[:, d, h:h + 1], ALU.mult)
                            orders = [list(range(NT)), [1, 0] + list(range(NT - 1, 1, -1))]
                            sidx = [0, 0]
                            for d in range(2):
                                K.memset(SstD[d][:], 0.0)
                                K.memset(SbfD[d][0][:], 0.0)

                            def emit_X(it):
                                b_ = it % 2
                                dd = (0, 1)
                                p_ = [orders[d][it] for d in dd]
                                for d in dd:
                                    K.tr(PS[d][:, 0:128], kkTs[d][:, p_[d] * 128:(p_[d] + 1) * 128], ident)
                                for d in dd:
                                    K.copy(kkM[d][b_][:], PS[d][:, 0:128], e='act')
                                for d in dd:
                                    K.act(lfM[d][b_][:], kkM[d][b_][:], AF.Ln, bias=1.0, scale=-1.0)
                                for d in dd:
                                    psB = PS[2 + d]
                                    K.mm(psB[:, 0:128], lfM[d][b_][:], TRI(d))
                                    K.mm(psB[:, 128:256], lfM[d][b_][:], E2M(d))
                                    K.mm(psB[:, 256:384], SU(d), lfM[d][b_][:])
                                for d in dd:
                                    psB = PS[2 + d]
                                    K.act(E1[d][b_][:], psB[:, 0:128], AF.Exp)
                                    K.act(E2[d][b_][:], psB[:, 128:256], AF.Exp)
                                    K.act(E3[d][b_][:], psB[:, 128:256], AF.Exp, scale=-1.0)
                                    K.act(EU[d][b_][:], psB[:, 256:384], AF.Exp)
                                for d in dd:
                                    t0 = p_[d] * 128
                                    K.tt(qt[d][b_][:], qs[:, t0:t0 + 128], E2[d][b_][:], ALU.mult, e='pool')
                                    K.tt(ktl[d][b_][:], kkTs[d][:, t0:t0 + 128], E3[d][b_][:], ALU.mult)
                                for d in dd:
                                    t0 = p_[d] * 128
                                    K.tt(qb[d][b_][:], qs[:, t0:t0 + 128], E1[d][b_][:], ALU.mult, e='pool')
                                    K.tt(kh[d][b_][:], kkM[d][b_][:], EU[d][b_][:], ALU.mult)
                                for d in dd:
                                    K.mm(PS[d][:, 128:256], ktl[d][b_][:], qt[d][b_][:])
                                for d in dd:
                                    K.tt(attm[d][b_][:], PS[d][:, 128:256], TRI(d), ALU.mult)

                            ntl = min(NT, dbg.get("htiles", 99))
                            emit_X(0)
                            for it in range(ntl):
                                b_ = it % 2
                                if it + 1 < ntl:
                                    emit_X(it + 1)
                                for d in range(2):
                                    p = orders[d][it]
                                    K.mm(PS[4 + d][:, 0:128], vt[:, p, :], attm[d][b_][:], start=True, stop=False)
                                for ci in range(2):
                                    for d in range(2):
                                        p = orders[d][it]
                                        c0, lastcol = ([(0, 63), (64, 127)] if d == 0 else [(64, 64), (0, 0)])[ci]
                                        psO, psk = PS[4 + d], PS[6 + d]
                                        K.mm(psO[:, c0:c0 + 64], SbfD[d][sidx[d] % 2][:], qb[d][b_][:, c0:c0 + 64], start=False, stop=(ci == 1))
                                        K.mm(psk[:, 0:128], kh[d][b_][c0:c0 + 64, :], vt[c0:c0 + 64, p, :])
                                        K.stt(SstD[d][:], SstD[d][:], E1[d][b_][:, lastcol:lastcol + 1], psk[:, 0:128], ALU.mult, ALU.add)
                                        sidx[d] += 1
                                        K.copy(SbfD[d][sidx[d] % 2][:], SstD[d][:], e='act')
                                for d in range(2):
                                    p = orders[d][it]
                                    t0 = p * 128
                                    key = S(osum[:, t0:t0 + 128], p)
                                    first = (d == 0 and p >= 2) or (d == 1 and p < 2)
                                    stepf = p
                                    stepb = (1 - p) if p < 2 else (NT + 1 - p)
                                    is_first = (stepf < stepb) if d == 0 else (stepb < stepf)
                                    if stepf == stepb:
                                        is_first = (d == 0)
                                    if is_first:
                                        K.copy(key, PS[4 + d][:, 0:128])
                                    else:
                                        K.tt(key, key, PS[4 + d][:, 0:128], ALU.add)
                            o, _ = coff[f"hg_norm{l}"]
                            for si_, (t0, n) in (enumerate(SLABS) if 'F' in dbg.get("hs", "BAF") else []):
                                b_ = si_ % 2
                                ps = PS[si_ % 2]
                                ps2 = PS[2 + si_ % 2]
                                okeys = [S(osum[:, tt0:tt0 + 128], tt0 // 128) for tt0 in range(t0, t0 + n, 128)]
                                K.op('dve', [o2[b_][:]], okeys, lambda g, b_=b_, t0=t0, n=n: g.tensor_tensor(out=o2[b_][:, 0:n], in0=osum[:, t0:t0 + n], in1=osum[:, t0:t0 + n], op=ALU.mult))
                                K.mm(ps[:, 0:n], ALLONE, o2[b_][:, 0:n])
                                K.rsqrt(rs[b_][:, 0:n], ps[:, 0:n], 1.0 / 128)
                                proj_fm(ps2, 512, 128, t0, n)
                                K.act(zz[b_][:, 0:n], ps2[:, 0:n], AF.Silu)
                                K.op('dve', [rs[b_][:]], okeys + [rs[b_][:], cols[:, o + h:o + h + 1]],
                                     lambda g, b_=b_, t0=t0, n=n, o=o, h=h: g.scalar_tensor_tensor(out=rs[b_][:, 0:n], in0=rs[b_][:, 0:n], scalar=cols[:, o + h:o + h + 1], in1=osum[:, t0:t0 + n], op0=ALU.mult, op1=ALU.mult))
                                K.tt(yo[b_][:, 0:n], rs[b_][:, 0:n], zz[b_][:, 0:n], ALU.mult)
                                K.dma('sp', S(ys_d[0, :, h, t0:t0 + n], (0, h, si_)), yo[b_][:, 0:n])
                    K.barrier()

                if 'S' in phases:
                    with contextlib.ExitStack() as st:
                        bqb = sb(st, "bqb", [128, 2, 16, 128], BF16)
                        cqb = sb(st, "cqb", [128, 2, 16, 64], BF16)
                        with contextlib.ExitStack() as st2:
                            bq32 = sb(st2, "bq32", [128, 2, 16, 128])
                            cq32 = sb(st2, "cq32", [128, 2, 16, 64])
                            for ri in range(2):
                                K.dma('sp', bq32[:, ri], s5bq_d[l, ri])
                                K.dma('sp', cq32[:, ri], s5cq_d[l, ri])
                            K.copy(bqb[:], bq32[:])
                            K.copy(cqb[:], cq32[:])
                            K.barrier()
                        def cst16(name):
                            return [sb(st, "%s%d" % (name, d), [128, 16]) for d in range(2)]
                        stp, rr, th, cr_, ci_, t_a, t_b, t_c = (cst16("stp"), cst16("rr"), cst16("th"), cst16("cr"), cst16("ci"),
                                                                cst16("ta"), cst16("tb"), cst16("tc"))
                        for d in range(2):
                            are = col(f"s5_are{l}{d}", 0, 16)
                            aim = col(f"s5_aim{l}{d}", 0, 16)
                            K.act(stp[d][:], col(f"s5_ls{l}{d}", 0, 16), AF.Exp)
                            K.tt(th[d][:], aim, stp[d][:], ALU.mult)
                            K.tt(t_a[d][:], are, stp[d][:], ALU.mult)
                            K.act(rr[d][:], t_a[d][:], AF.Exp)
                        negpi = sb(st, "negpi", [128, 1])
                        K.memset(negpi[:], -PI)
                        for d in range(2):
                            are = col(f"s5_are{l}{d}", 0, 16)
                            aim = col(f"s5_aim{l}{d}", 0, 16)
                            sn, cs = t_b[d], t_c[d]
                            K.sinred(sn[:], th[d][:], 0.0, t_a[d][:], None)
                            K.sinred(cs[:], th[d][:], 0.5 * PI, t_a[d][:], None)
                            K.tt(cs[:], cs[:], rr[d][:], ALU.mult)
                            K.ts(cs[:], cs[:], -1.0, ALU.add)
                            K.tt(sn[:], sn[:], rr[d][:], ALU.mult)
                            K.tt(t_a[d][:], are, are, ALU.mult)
                            K.tt(cr_[d][:], aim, aim, ALU.mult)
                            K.tt(t_a[d][:], t_a[d][:], cr_[d][:], ALU.add)
                            K.recip(t_a[d][:], t_a[d][:])
                            K.tt(cr_[d][:], cs[:], are, ALU.mult)
                            K.tt(ci_[d][:], sn[:], aim, ALU.mult)
                            K.tt(cr_[d][:], cr_[d][:], ci_[d][:], ALU.add)
                            K.tt(cr_[d][:], cr_[d][:], t_a[d][:], ALU.mult)
                            K.tt(ci_[d][:], sn[:], are, ALU.mult)
                            K.tt(cs[:], cs[:], aim, ALU.mult)
                            K.tt(ci_[d][:], ci_[d][:], cs[:], ALU.subtract)
                            K.tt(ci_[d][:], ci_[d][:], t_a[d][:], ALU.mult)
                        ub = sb(st, "ub", [128, T], BF16)
                        ysum = sb(st, "ysum", [128, T])
                        gt_ = [sb(st, "gt%d" % i, [128, 512]) for i in range(1)] * 2
                        gu_ = [sb(st, "gu%d" % i, [128, 512]) for i in range(2)]
                        gb_ = [sb(st, "gb%d" % i, [128, 512], BF16) for i in range(2)]
                        st3 = contextlib.ExitStack()
                        tabc = sb(st3, "tabc", [128, 4, SL])
                        tabm = sb(st3, "tabm", [128, 4, SL])
                        tabB = sb(st3, "tabB", [128, 4, 2, SL])
                        tabD = sb(st3, "tabD", [128, 4, 2, SL])
                        NB = 4

                        def tl(name, dt=F32, n=NB, shape=(128, 2, SL)):
                            return [sb(st3, "%s%d" % (name, i), list(shape), dt) for i in range(n)]
                        A_, B_, W_ = tl("sA"), tl("sB"), tl("sW")
                        D_, V_, U_ = A_, A_, B_
                        ph, tq1, tq2 = B_[0][:, 0, :], B_[1][:, 0, :], B_[2][:, 0, :]
                        slt = tl("slt", F32, NB, (128, 2))
                        Ub = [[sb(st3, "Ub%d_%d" % (k, i), [128, 2, SL], BF16) for i in range(2)] for k in range(4)]
                        Vb = [[sb(st3, "Vb%d_%d" % (k, i), [128, 2, SL], BF16) for i in range(2)] for k in range(4)]
                        sl_ = [[sb(st3, "sl%d_%d" % (k, i), [128, 2]) for i in range(2)] for k in range(4)]
                        IOTAL = cst[:, CI_IOTA:CI_IOTA + SL // 128, :].rearrange("p a b -> p (a b)")

                        def view(ap, swap=False, rv=False):
                            pat = [list(p) for p in ap.ap]
                            off = ap.offset
                            if swap:
                                st_, cn = pat[-2]
                                off = off + st_ * (cn - 1)
                                pat[-2] = [-st_, cn]
                            if rv:
                                st_, cn = pat[-1]
                                off = off + st_ * (cn - 1)
                                pat[-1] = [-st_, cn]
                            return AP(ap.tensor, off, pat)
                        nchunk = T // SL
                        S5E = dbg.get('s5e', 'dve')
                        for c in range(4):
                            load_w([(C_SU + c * 128, 128)])
                            for si_, (t0, n) in enumerate(SLABS):
                                ps = PS[si_ % 2]
                                proj_fm(ps, 0, 128, t0, n)
                                K.copy(ub[:, t0:t0 + n], ps[:, 0:n], e='act')
                            for d in dbg.get("s5dirs", [0, 1]):
                                for kq in range(4):
                                    k = 4 * c + kq
                                    K.ts(ph[:], IOTAL, th[d][:, k:k + 1], ALU.mult)
                                    K.sinred(tabD[:, kq, 0, :], ph[:], 0.0, tq1[:], None)
                                    K.sinred(tabc[:, kq, :], ph[:], 0.5 * PI, tq1[:], None)
                                    K.ts(tq2[:], tabD[:, kq, 0, :], ci_[d][:, k:k + 1], ALU.mult)
                                    K.stt(tabm[:, kq, :], tabc[:, kq, :], cr_[d][:, k:k + 1], tq2[:], ALU.mult, ALU.add)
                                    K.ts(tq2[:], tabD[:, kq, 0, :], cr_[d][:, k:k + 1], ALU.mult)
                                    K.stt(tabB[:, kq, 1, :], tabc[:, kq, :], ci_[d][:, k:k + 1], tq2[:], ALU.mult, ALU.subtract)
                                    K.ts(tabB[:, kq, 0, :], tabB[:, kq, 1, :], -1.0, ALU.mult)
                                    K.ts(tabD[:, kq, 1, :], tabD[:, kq, 0, :], -1.0, ALU.mult)
                                if d == 0:
                                    corder = list(range(nchunk))
                                else:
                                    nc_ctx = NCTX // SL
                                    corder = list(range(nc_ctx - 1, -1, -1)) + list(range(nchunk - 1, nc_ctx - 1, -1))
                                def emit_P(it, ch, kq):
                                    k = 4 * c + kq
                                    t0 = ch * SL
                                    psP = PS[4 + kq]
                                    pb = 64 * (kq // 2)
                                    K.mm(psP[:, 0:SL], bqb[pb:pb + 64, 0, k, :], ub[pb:pb + 64, t0:t0 + SL])
                                    K.mm(psP[:, SL:2 * SL], bqb[pb:pb + 64, 1, k, :], ub[pb:pb + 64, t0:t0 + SL])

                                for kq in range(4):
                                    emit_P(0, corder[0], kq)
                                for it, ch in enumerate(corder):
                                    t0 = ch * SL
                                    par = it % 2
                                    psY = PS[2 + it % 2]
                                    rv = (d == 1)
                                    for kq in range(4):
                                        P3 = PS[4 + kq][:, 0:2 * SL].rearrange("p (a t) -> p a t", a=2)
                                        K.tt(A_[kq][:], view(P3, False, rv), tabm[:, kq:kq + 1, :].to_broadcast([128, 2, SL]), ALU.mult)
                                        K.tt(B_[kq][:], view(P3, True, rv), tabB[:, kq], ALU.mult)
                                    if it + 1 < len(corder):
                                        for kq in range(4):
                                            emit_P(it + 1, corder[it + 1], kq)
                                    for kq in range(4):
                                        K.tt(D_[kq][:], A_[kq][:], B_[kq][:], ALU.add, e=S5E)
                                    for hf in range(2):
                                        for kq in range(4):
                                            k = 4 * c + kq
                                            rk = rr[d][:, k:k + 1].to_broadcast([128, SL])
                                            init = 0.0 if it == 0 else sl_[kq][1 - par][:, hf:hf + 1]
                                            K.scan(W_[kq][:, hf, :], rk, D_[kq][:, hf, :], init, ALU.mult, ALU.add if hf == 0 else ALU.subtract)
                                    for kq in range(4):
                                        K.ts(sl_[kq][par][:], W_[kq][:, :, SL - 1], tabc[:, kq, SL - 1:SL], ALU.mult)
                                    for kq in range(4):
                                        K.tt(slt[kq][:], view(W_[kq][:], True, False)[:, :, SL - 1], tabD[:, kq, :, SL - 1], ALU.mult)
                                    for kq in range(4):
                                        K.tt(sl_[kq][par][:], sl_[kq][par][:], slt[kq][:], ALU.add)
                                    for kq in range(4):
                                        K.tt(view(Ub[kq][par][:], False, rv), W_[kq][:], tabc[:, kq:kq + 1, :].to_broadcast([128, 2, SL]), ALU.mult, e=S5E)
                                    for kq in range(4):
                                        K.tt(view(Vb[kq][par][:], False, rv), view(W_[kq][:], True, False), tabD[:, kq], ALU.mult)
                                    for kq in range(4):
                                        k = 4 * c + kq
                                        pb = 64 * (kq // 2)
                                        u_, v_ = Ub[kq][par], Vb[kq][par]
                                        K.mm(psY[pb:pb + 64, 0:SL], cqb[:, 0, k, :], u_[:, 0, :], start=(kq % 2 == 0), stop=False)
                                        K.mm(psY[pb:pb + 64, 0:SL], cqb[:, 0, k, :], v_[:, 0, :], start=False, stop=False)
                                        K.mm(psY[pb:pb + 64, 0:SL], cqb[:, 1, k, :], u_[:, 1, :], start=False, stop=False)
                                        K.mm(psY[pb:pb + 64, 0:SL], cqb[:, 1, k, :], v_[:, 1, :], start=False, stop=(kq % 2 == 1))
                                    if d == 0:
                                        K.stt(S(ysum[:, t0:t0 + SL], ch), ub[:, t0:t0 + SL], col(f"s5_d{l}", c), psY[:, 0:SL], ALU.mult, ALU.add)
                                    else:
                                        K.tt(S(ysum[:, t0:t0 + SL], ch), S(ysum[:, t0:t0 + SL], ch), psY[:, 0:SL], ALU.add)
                            if dbg.get("dump"):
                                K.barrier()
                                K.dma('sp', ysd_d[:, c, :], ysum[:])
                                K.barrier()
                            for si_, (t0, n) in enumerate(SLABS):
                                b_ = si_ % 2
                                ysl = [S(ysum[:, tt0:tt0 + SL], tt0 // SL) for tt0 in range(t0, t0 + n, SL)]
                                yv = ysum[:, t0:t0 + n]
                                K.op('dve', [gt_[b_][:]], ysl, lambda g, b_=b_, yv=yv, n=n: g.tensor_tensor(out=gt_[b_][:, 0:n], in0=yv, in1=yv, op=ALU.mult))
                                K.ts(gt_[b_][:, 0:n], gt_[b_][:, 0:n], 0.044715, ALU.mult, 1.0, ALU.add)
                                K.op('dve', [gu_[b_][:]], ysl + [gt_[b_][:]], lambda g, b_=b_, yv=yv, n=n: g.tensor_tensor(out=gu_[b_][:, 0:n], in0=gt_[b_][:, 0:n], in1=yv, op=ALU.mult))
                                K.act(gu_[b_][:, 0:n], gu_[b_][:, 0:n], AF.Sigmoid, scale=1.5957691216)
                                K.op('dve', [gb_[b_][:]], ysl + [gu_[b_][:]], lambda g, b_=b_, yv=yv, n=n: g.tensor_tensor(out=gb_[b_][:, 0:n], in0=gu_[b_][:, 0:n], in1=yv, op=ALU.mult))
                                K.dma('sp', S(gs_d[:, c, t0:t0 + n], (c, si_)), gb_[b_][:, 0:n])
                        K.barrier()
                        st3.close()
                        wg32 = sb(st, "wg32", [128, 4, 512])
                        wgb = sb(st, "wgb", [128, 4, 512], BF16)
                        K.dma('sp', wg32[:], wglu_d[l].rearrange("(k p) n -> p k n", p=128))
                        K.copy(wgb[:], wg32[:], e='pool')
                        load_w([(C_SZ, 512)])
                        gin = [sb(st, "gin%d" % i, [128, 4, 512], BF16) for i in range(2)]
                        sg = [sb(st, "sg%d" % i, [128, 512]) for i in range(2)]
                        for si_, (t0, n) in enumerate(SLABS):
                            b_ = si_ % 2
                            for c in range(4):
                                K.dma('sp', gin[b_][:, c, 0:n], S(gs_d[:, c, t0:t0 + n], (c, si_)))
                            for c in range(4):
                                ps = PS[c % 2]
                                ps2 = PS[2 + c % 2]
                                for kc in range(4):
                                    K.mm(ps[:, 0:n], wgb[:, kc, c * 128:(c + 1) * 128], gin[b_][:, kc, 0:n], start=(kc == 0), stop=(kc == 3))
                                K.act(sg[c % 2][:, 0:n], ps[:, 0:n], AF.Sigmoid, bias=col(f"s5_bglu{l}", c))
                                proj_fm(ps2, c * 128, 128, t0, n)
                                K.act(gu_[c % 2][:, 0:n], ps2[:, 0:n], AF.Sigmoid)
                                K.tt(gu_[c % 2][:, 0:n], gu_[c % 2][:, 0:n], ps2[:, 0:n], ALU.mult)
                                K.tt(sg[c % 2][:, 0:n], sg[c % 2][:, 0:n], gin[b_][:, c, 0:n], ALU.mult)
                                K.tt(gb_[c % 2][:, 0:n], sg[c % 2][:, 0:n], gu_[c % 2][:, 0:n], ALU.mult)
                                K.dma('sp', S(ys_d[1, :, c, t0:t0 + n], (1, c, si_)), gb_[c % 2][:, 0:n])
                    K.barrier()

                if 'L' in phases:
                    with contextlib.ExitStack() as st:
                        gw32 = sb(st, "gw32", [128, 16, 128])
                        gwb = sb(st, "gwb", [128, 16, 128], BF16)
                        K.dma('sp', gw32[:], lrugw_d[l])
                        K.copy(gwb[:], gw32[:])
                        cd = sb(st, "cd", [128, 2, 4])
                        cd2 = sb(st, "cd2", [128, 2, 4])
                        for d in range(2):
                            K.act(cd[:, d, :], col(f"lru_lam{l}{d}", 0, 4), AF.Exp, scale=-1.0)
                        K.act(cd[:], cd[:], AF.Ln, bias=1.0)
                        K.ts(cd2[:], cd[:], -16.0, ALU.mult)
                        K.ts(cd[:], cd[:], -8.0, ALU.mult)
                        xraw = sb(st, "xraw", [128, T])
                        xcv = sb(st, "xcv", [128, T])
                        xcb = sb(st, "xcb", [128, T], BF16)
                        hsum = sb(st, "hsum", [128, T])

                        def tl(name, dt=F32, n=2, shape=(128, 512)):
                            return [sb(st, "%s%d" % (name, i), list(shape), dt) for i in range(n)]
                        rg, ig, aa, a2, bb, hb, zz, yo = tl("rg"), tl("ig"), tl("aa"), tl("a2"), tl("bb"), tl("hb"), tl("zz"), tl("yo", BF16)
                        for j in range(4):
                            load_w([(C_LX + j * 128, 128), (C_LZ + j * 128, 128)])
                            for si_, (t0, n) in enumerate(SLABS):
                                ps = PS[si_ % 2]
                                proj_fm(ps, 0, 128, t0, n)
                                K.copy(xraw[:, t0:t0 + n], ps[:, 0:n], e='act')
                            w = [col(f"lru_cw{l}{k}", j) for k in range(4)]
                            cb = col(f"lru_cb{l}", j)
                            K.ts(xcv[:], xraw[:], w[1], ALU.mult, cb, ALU.add)
                            for (a0, nr, rl) in ((0, 1, 256), (256, 64, 64)):
                                xv = xraw[:, a0:a0 + nr * rl].rearrange("p (r t) -> p r t", t=rl)
                                ov = xcv[:, a0:a0 + nr * rl].rearrange("p (r t) -> p r t", t=rl)
                                K.stt(ov[:, :, 1:rl], xv[:, :, 0:rl - 1], w[0], ov[:, :, 1:rl], ALU.mult, ALU.add)
                                K.stt(ov[:, :, 0:rl - 1], xv[:, :, 1:rl], w[2], ov[:, :, 0:rl - 1], ALU.mult, ALU.add)
                                K.stt(ov[:, :, 0:rl - 2], xv[:, :, 2:rl], w[3], ov[:, :, 0:rl - 2], ALU.mult, ALU.add)
                            K.copy(xcb[:], xcv[:], e='dve')
                            for d in range(2):
                                order = list(range(9)) if d == 0 else [0] + list(range(8, 0, -1))
                                prev = None
                                for pi in range(0, len(order), 2):
                                    grp = [(it, order[it]) for it in range(pi, min(pi + 2, len(order)))]
                                    for it, si_ in grp:
                                        t0, n = SLABS[si_]
                                        K.mm(PS[2 + it % 2][:, 0:n], gwb[:, (d * 2 + 0) * 4 + j, :], xcb[:, t0:t0 + n])
                                        K.mm(PS[4 + it % 2][:, 0:n], gwb[:, (d * 2 + 1) * 4 + j, :], xcb[:, t0:t0 + n])
                                    for it, si_ in grp:
                                        t0, n = SLABS[si_]
                                        b_ = it % 2
                                        K.act(rg[b_][:, 0:n], PS[2 + it % 2][:, 0:n], AF.Sigmoid, bias=col(f"lru_gb{l}{d}0", j))
                                        K.act(ig[b_][:, 0:n], PS[4 + it % 2][:, 0:n], AF.Sigmoid, bias=col(f"lru_gb{l}{d}1", j))
                                    for it, si_ in grp:
                                        t0, n = SLABS[si_]
                                        b_ = it % 2
                                        K.act(aa[b_][:, 0:n], rg[b_][:, 0:n], AF.Exp, scale=cd[:, d, j:j + 1])
                                        K.act(a2[b_][:, 0:n], rg[b_][:, 0:n], AF.Exp, scale=cd2[:, d, j:j + 1])
                                    for it, si_ in grp:
                                        t0, n = SLABS[si_]
                                        b_ = it % 2
                                        K.ts(a2[b_][:, 0:n], a2[b_][:, 0:n], -1.0, ALU.mult, 1.0, ALU.add)
                                        K.ts(a2[b_][:, 0:n], a2[b_][:, 0:n], 0.0, ALU.max)
                                        K.tt(bb[b_][:, 0:n], ig[b_][:, 0:n], xcv[:, t0:t0 + n], ALU.mult, e='pool')
                                    for it, si_ in grp:
                                        t0, n = SLABS[si_]
                                        b_ = it % 2
                                        K.act(a2[b_][:, 0:n], a2[b_][:, 0:n], AF.Sqrt)
                                    for it, si_ in grp:
                                        t0, n = SLABS[si_]
                                        b_ = it % 2
                                        K.tt(bb[b_][:, 0:n], bb[b_][:, 0:n], a2[b_][:, 0:n], ALU.mult, e='pool')
                                        init = 0.0 if prev is None else prev
                                        if d == 0:
                                            K.scan(hsum[:, t0:t0 + n], aa[b_][:, 0:n], bb[b_][:, 0:n], init)
                                            prev = hsum[:, t0 + n - 1:t0 + n]
                                        else:
                                            K.scan(rev(hb[b_][:, 0:n]), rev(aa[b_][:, 0:n]), rev(bb[b_][:, 0:n]), init)
                                            prev = hb[b_][:, 0:1]
                                            K.tt(hsum[:, t0:t0 + n], hsum[:, t0:t0 + n], hb[b_][:, 0:n], ALU.add, e='pool')
                            for si_, (t0, n) in enumerate(SLABS):
                                b_ = si_ % 2
                                ps = PS[si_ % 2]
                                proj_fm(ps, 128, 128, t0, n)
                                K.act(zz[b_][:, 0:n], ps[:, 0:n], AF.Silu)
                                K.tt(yo[b_][:, 0:n], zz[b_][:, 0:n], hsum[:, t0:t0 + n], ALU.mult)
                                K.dma('sp', S(ys_d[2, :, j, t0:t0 + n], (2, j, si_)), yo[b_][:, 0:n])
                    K.barrier()

                if 'M' in phases:
                    with contextlib.ExitStack() as st:
                        xbc = sb(st, "xbc", [128, 6, T], BF16)
                        st2 = contextlib.ExitStack()
                        xraw = sb(st2, "mxraw", [128, T])
                        xcv = sb(st2, "mxcv", [128, T])
                        for j in range(6):
                            load_w([(C_XBC + j * 128, 128)])
                            for si_, (t0, n) in enumerate(SLABS):
                                ps = PS[si_ % 2]
                                proj_fm(ps, 0, 128, t0, n)
                                K.copy(xraw[:, t0:t0 + n], ps[:, 0:n], e='act')
                            w = [col(f"m2_cw{l}{k}", j) for k in range(4)]
                            cb = col(f"m2_cb{l}", j)
                            K.ts(xcv[:], xraw[:], w[1], ALU.mult, cb, ALU.add)
                            for (a0, nr, rl) in ((0, 1, 256), (256, 64, 64)):
                                xv = xraw[:, a0:a0 + nr * rl].rearrange("p (r t) -> p r t", t=rl)
                                ov = xcv[:, a0:a0 + nr * rl].rearrange("p (r t) -> p r t", t=rl)
                                K.stt(ov[:, :, 1:rl], xv[:, :, 0:rl - 1], w[0], ov[:, :, 1:rl], ALU.mult, ALU.add)
                                K.stt(ov[:, :, 0:rl - 1], xv[:, :, 1:rl], w[2], ov[:, :, 0:rl - 1], ALU.mult, ALU.add)
                                K.stt(ov[:, :, 0:rl - 2], xv[:, :, 2:rl], w[3], ov[:, :, 0:rl - 2], ALU.mult, ALU.add)
                            K.act(S(xbc[:, j, :], j), xcv[:], AF.Silu)
                        K.barrier()
                        st2.close()
                        XBC = [S(xbc[:, j, :], j) for j in range(6)]
                        load_w([(C_DT, 16), (C_MZ, 512)])
                        dtb = [rowb(st, f"m2_dtb{l}{d}", "dtb%d" % d) for d in range(2)]
                        alg = [rowb(st, f"m2_alog{l}{d}", "alg%d" % d) for d in range(2)]
                        m2d = rowb(st, f"m2_d{l}", "m2d")
                        m2n = rowb(st, f"m2_norm{l}", "m2n")
                        for d in range(2):
                            K.act(alg[d][:], alg[d][:], AF.Exp)
                            K.ts(alg[d][:], alg[d][:], -1.0, ALU.mult)

                        def tl(name, shape, dt=F32, n=2):
                            return [sb(st, "%s%d" % (name, i), list(shape), dt) for i in range(n)]
                        dt_, dta, ecum, wdt = tl("dt", [128, 8]), tl("dta", [128, 8]), tl("ecum", [128, 8]), tl("wdt", [128, 8])
                        Rm = tl("Rm", [128, 8, 128])
                        exs = tl("exs", [128, 8, 128])
                        mdt = tl("mdt", [128, 8, 128])
                        Wt = tl("Wt", [128, 8, 128], BF16)
                        xtm = tl("xtm", [128, 640], BF16)
                        xw = tl("xw", [128, 512], BF16)
                        edec = tl("edec", [128, 4], F32, 4)
                        Sst = sb(st, "mS", [128, 256])
                        Sb = tl("mSb", [128, 256], BF16)
                        ya = tl("ya", [128, 512])
                        yb = tl("yb", [128, 512])
                        zt = tl("zt", [128, 512])
                        y2 = tl("y2", [128, 512])
                        ybf = tl("ybf", [128, 512], BF16)
                        yfm = tl("yfm", [128, 4, 128], BF16)
                        ssq = tl("ssq", [128, 1])
                        for d in range(2):
                            order = list(range(NT)) if d == 0 else [1, 0] + list(range(NT - 1, 1, -1))
                            K.memset(Sst[:], 0.0)
                            K.memset(Sb[0][:], 0.0)
                            si_m = [0]

                            def emit_MX(it, d=d, order=order):
                                p = order[it]
                                b_ = it % 2
                                t0 = p * 128
                                psA = PS[it % 2]
                                psS = [PS[2], PS[3]]
                                psC = PS[4 + it % 2]
                                psY = PS[6]
                                psZ = PS[7]
                                proj_tm(psA, d * 8, 8, t0, pc0=0)
                                K.tt(dt_[b_][:], psA[:, 0:8], dtb[d][:], ALU.add)
                                K.act(dt_[b_][:], dt_[b_][:], AF.Exp)
                                K.act(dt_[b_][:], dt_[b_][:], AF.Ln, bias=1.0)
                                K.tt(dta[b_][:], dt_[b_][:], alg[d][:], ALU.mult)
                                psT = psA[:, 128:512].bitcast(BF16)
                                for j in range(5):
                                    K.op('pe', [psA], [XBC[j], identb[:]], lambda g, j=j, t0=t0, psT=psT: g.transpose(psT[:, j * 128:(j + 1) * 128], xbc[:, j, t0:t0 + 128], identb[:]))
                                K.op('act', [xtm[b_][:]], [psA], lambda g, b_=b_, psT=psT: g.activation(out=xtm[b_][:], in_=psT[:, 0:640], func=AF.Copy))
                                K.tt(Rm[b_][:], TRI(d).rearrange("p (o i) -> p o i", o=1).to_broadcast([128, 8, 128]),
                                     dta[b_][:].rearrange("p (h o) -> p h o", o=1).to_broadcast([128, 8, 128]), ALU.mult, e='pool')
                                for hh in range(2):
                                    K.mm(psS[hh][:], SU(d), Rm[b_][:, hh * 4:(hh + 1) * 4, :].rearrange("p h i -> p (h i)"))
                                    K.act(exs[b_][:, hh * 4:(hh + 1) * 4, :].rearrange("p h i -> p (h i)"), psS[hh][:], AF.Exp)
                                K.mm(psA[:, 8:16], TRI(d), dta[b_][:])
                                K.act(ecum[b_][:], psA[:, 8:16], AF.Exp)
                                for g in range(2):
                                    K.op('pe', [S(psC[:, g * 128:(g + 1) * 128], 'sc')], [XBC[4], XBC[5]],
                                         lambda e_, g=g, t0=t0, psC=psC: e_.matmul(psC[:, g * 128:(g + 1) * 128], lhsT=xbc[64 * g:64 * g + 64, 4, t0:t0 + 128],
                                                                              rhs=xbc[64 * g:64 * g + 64, 5, t0:t0 + 128], start=True, stop=True), pesync=True)
                                K.tt(mdt[b_][:], TRI(d).rearrange("p (o i) -> p o i", o=1).to_broadcast([128, 8, 128]),
                                     dt_[b_][:].rearrange("p (h o) -> p h o", o=1).to_broadcast([128, 8, 128]), ALU.mult, e='pool')
                                K.tt(mdt[b_][:], mdt[b_][:], exs[b_][:], ALU.mult, e='pool')
                                for g in range(2):
                                    K.tt(Wt[b_][:, g * 4:(g + 1) * 4, :], mdt[b_][:, g * 4:(g + 1) * 4, :],
                                         S(psC[:, g * 128:(g + 1) * 128].rearrange("p (o i) -> p o i", o=1).to_broadcast([128, 4, 128]), 'sc'), ALU.mult)
                                chunks = [(0, 63), (64, 127)] if d == 0 else [(64, 64), (0, 0)]
                                for (c0, lastcol) in chunks:
                                    K.tt(wdt[b_][c0:c0 + 64, :], exs[b_][c0:c0 + 64, :, lastcol], dt_[b_][c0:c0 + 64, :], ALU.mult)
                                K.tt(xw[b_][:].rearrange("p (h q) -> p h q", q=64), xtm[b_][:, 0:512].rearrange("p (h q) -> p h q", q=64),
                                     wdt[b_][:].rearrange("p (h o) -> p h o", o=1).to_broadcast([128, 8, 64]), ALU.mult)

                            def emit_MY(it, d=d, order=order):
                                p = order[it]
                                b_ = it % 2
                                t0 = p * 128
                                psA = PS[it % 2]
                                psC = PS[4 + it % 2]
                                psY = PS[6]
                                psZ = PS[7]
                                si = si_m[0]
                                chunks = [(0, 63), (64, 127)] if d == 0 else [(64, 64), (0, 0)]
                                for hd in range(8):
                                    K.mm(psY[:, hd * 64:(hd + 1) * 64], Wt[b_][:, hd, :], xtm[b_][:, hd * 64:(hd + 1) * 64])
                                for ci, (c0, lastcol) in enumerate(chunks):
                                    for g in range(2):
                                        K.op('pe', [S(psZ[c0:c0 + 64, g * 256:(g + 1) * 256], 'z')], [XBC[5], Sb[si % 2][:]],
                                             lambda e_, g=g, c0=c0, t0=t0, s_=Sb[si % 2]: e_.matmul(psZ[c0:c0 + 64, g * 256:(g + 1) * 256], lhsT=xbc[64 * g:64 * g + 64, 5, t0 + c0:t0 + c0 + 64],
                                                                                                 rhs=s_[64 * g:64 * g + 64, :], start=True, stop=True), pesync=True)
                                    for g in range(2):
                                        K.op('pe', [S(psC[64 * g:64 * g + 64, 256:512], 'cs')], [xtm[b_][:], xw[b_][:]],
                                             lambda e_, g=g, c0=c0, b_=b_, psC=psC: e_.matmul(psC[64 * g:64 * g + 64, 256:512], lhsT=xtm[b_][c0:c0 + 64, 512 + 64 * g:512 + 64 * g + 64],
                                                                                            rhs=xw[b_][c0:c0 + 64, g * 256:(g + 1) * 256], start=True, stop=True), pesync=True)
                                        K.op('pe', [S(psA[64 * g:64 * g + 64, 16 + 4 * ci:20 + 4 * ci], 'ed')], [dta[b_][:]],
                                             lambda e_, g=g, c0=c0, ci=ci, b_=b_, psA=psA: e_.matmul(psA[64 * g:64 * g + 64, 16 + 4 * ci:20 + 4 * ci], lhsT=cst[c0:c0 + 64, CI_ALL, 0:64],
                                                                                                   rhs=dta[b_][c0:c0 + 64, 4 * g:4 * g + 4], start=True, stop=True), pesync=True)
                                    ed = edec[(it * 2 + ci) % 4]
                                    K.act(ed[:], S(psA[:, 16 + 4 * ci:20 + 4 * ci], 'ed'), AF.Exp)
                                    K.tt(Sst[:].rearrange("p (r q) -> p r q", q=64), Sst[:].rearrange("p (r q) -> p r q", q=64),
                                         ed[:].rearrange("p (r o) -> p r o", o=1).to_broadcast([128, 4, 64]), ALU.mult)
                                    K.tt(Sst[:], Sst[:], S(psC[:, 256:512], 'cs'), ALU.add)
                                    si += 1
                                    si_m[0] = si
                                    K.copy(Sb[si % 2][:], Sst[:], e='act')
                                K.tt(ya[b_][:].rearrange("p (h q) -> p h q", q=64), S(psZ[:].rearrange("p (h q) -> p h q", q=64), 'z'),
                                     ecum[b_][:].rearrange("p (h o) -> p h o", o=1).to_broadcast([128, 8, 64]), ALU.mult)
                                K.tt(ya[b_][:], ya[b_][:], psY[:], ALU.add)
                                if d == 0:
                                    K.tt(yb[b_][:].rearrange("p (h q) -> p h q", q=64), xtm[b_][:, 0:512].rearrange("p (h q) -> p h q", q=64),
                                         m2d[:].rearrange("p (h o) -> p h o", o=1).to_broadcast([128, 8, 64]), ALU.mult, e='pool')
                                    K.tt(ya[b_][:], ya[b_][:], yb[b_][:], ALU.add, e='pool')
                                    K.dma('sp', S(ym_d[t0:t0 + 128, :], p), ya[b_][:])
                                else:
                                    K.dma('sp', yb[b_][:], S(ym_d[t0:t0 + 128, :], p))
                                    K.tt(ya[b_][:], ya[b_][:], yb[b_][:], ALU.add, e='pool')
                                    proj_tm(psY, 16, 512, t0, pc0=0)
                                    K.act(zt[b_][:], psY[:], AF.Silu)
                                    K.tt(ya[b_][:], ya[b_][:], zt[b_][:], ALU.mult)
                                    K.act(y2[b_][:], ya[b_][:], AF.Square, accum=ssq[b_][:])
                                    K.rsqrt(ssq[b_][:], ssq[b_][:], 1.0 / 512)
                                    K.stt(ybf[b_][:], ya[b_][:], ssq[b_][:, 0:1], m2n[:], ALU.mult, ALU.mult)
                                    psT2 = psZ[:, 0:256].bitcast(BF16)
                                    for j in range(4):
                                        K.op('pe', [S(psZ[:], 'z')], [ybf[b_][:], identb[:]],
                                             lambda g, j=j, b_=b_, psT2=psT2: g.transpose(psT2[:, j * 128:(j + 1) * 128], ybf[b_][:, j * 128:(j + 1) * 128], identb[:]))
                                    K.op('act', [yfm[b_][:]], [S(psZ[:], 'z')], lambda g, b_=b_, psT2=psT2: g.activation(out=yfm[b_][:].rearrange("p j t -> p (j t)"), in_=psT2, func=AF.Copy))
                                    K.dma('sp', S(ys_d[3, :, :, t0:t0 + 128], (3, p)), yfm[b_][:])

                            emit_MX(0)
                            for it in range(NT):
                                if it + 1 < NT:
                                    emit_MX(it + 1)
                                emit_MY(it)
                    K.barrier()
                p15.close()
                K.barrier()

                if 'G' in phases:
                    with contextlib.ExitStack() as st:
                        wg = sb(st, "wg", [128, 4, 8, D], BF16)
                        wb = sb(st, "wb", [128, 4, 4, D], BF16)
                        wo = sb(st, "wo", [128, 8, D], BF16)
                        NSTG = 3
                        stg = [sb(st, "stg%d" % i, [128, 1024]) for i in range(NSTG)]
                        dq = ['sp', 'act', 'pool']
                        ce = ['dve', 'act', 'dve']
                        n_ = 0
                        for k in range(4):
                            for kt in range(8):
                                s_ = stg[n_ % NSTG]; q_ = dq[n_ % NSTG]; c_ = ce[n_ % NSTG]; n_ += 1
                                K.dma(q_, s_[:], wgate_d[l, k, kt * 128:(kt + 1) * 128, :])
                                K.copy(wg[:, k, kt, :], s_[:], e=c_)
                            for kt in range(4):
                                s_ = stg[n_ % NSTG]; q_ = dq[n_ % NSTG]; c_ = ce[n_ % NSTG]; n_ += 1
                                K.dma(q_, s_[:], wbr_d[l, k, kt * 128:(kt + 1) * 128, :])
                                K.copy(wb[:, k, kt, :], s_[:], e=c_)
                        for kt in range(8):
                            s_ = stg[n_ % NSTG]; q_ = dq[n_ % NSTG]; c_ = ce[n_ % NSTG]; n_ += 1
                            K.dma(q_, s_[:], wout_d[l, kt * 128:(kt + 1) * 128, :])
                            K.copy(wo[:, kt, :], s_[:], e=c_)
                        gtrow = sb(st, "gtrow", [128, 2, D])
                        gtb = rowb(st, f"gtb{l}", "gtbrow")
                        lb_ = sb(st, "lbc_", [128, 8, 128])
                        fnr = rowb(st, "final_norm", "fnr") if last else None
                        for cc in range(2):
                            for kt in range(8):
                                K.copy(lb_[:, kt, :], silc[:, kt, cc:cc + 1].to_broadcast([128, 128]))
                            for hf in range(2):
                                ps = PS[hf]
                                for kt in range(8):
                                    s_ = stg[n_ % NSTG]; q_ = dq[n_ % NSTG]; c_ = ce[n_ % NSTG]; n_ += 1
                                    K.dma(q_, s_[:, 0:512], wmod_d[l, kt * 128:(kt + 1) * 128, 2048 + hf * 512:2048 + (hf + 1) * 512])
                                    K.mm(ps[:], lb_[:, kt, :], s_[:, 0:512], start=(kt == 0), stop=(kt == 7))
                                K.tt(gtrow[:, cc, hf * 512:(hf + 1) * 512], ps[:], gtb[:, hf * 512:(hf + 1) * 512], ALU.add)

                        def tl(name, shape, dt=F32, n=2):
                            return [sb(st, "%s%d" % (name, i), list(shape), dt) for i in range(n)]
                        hsl = tl("hsl", [128, 8, 256], BF16)
                        ysl = tl("ysl", [128, 4, 4, 256], BF16)
                        macc = tl("macc", [128, 256], F32, 2)
                        gsg = tl("gsg", [128, 256], F32, 2)
                        mT = tl("mT", [128, 8, 256], BF16)
                        xrow = tl("xrow", [128, D], F32, 2)
                        orow = tl("orow", [128, D], F32, 2)
                        jk = stg[0]
                        ssq = tl("fss", [128, 1])
                        for si_, (t0, n) in enumerate(SLABS256):
                            if last and si_ == 0:
                                continue
                            b_ = si_ % 2
                            cc = 1 if si_ == 0 else 0
                            K.dma('sp', hsl[b_][:], hs_d[:, :, t0:t0 + n])
                            for k in range(4):
                                K.dma('pool', ysl[b_][:, k], ys_d[k, :, :, t0:t0 + n])
                            for oc in range(8):
                                for k in range(4):
                                    psg = PS[(oc * 4 + k) % 2]
                                    psb = PS[2 + (oc * 4 + k) % 2]
                                    for kt in range(8):
                                        K.mm(psg[:, 0:n], wg[:, k, kt, oc * 128:(oc + 1) * 128], hsl[b_][:, kt, :], start=(kt == 0), stop=(kt == 7))
                                    for kt in range(4):
                                        K.mm(psb[:, 0:n], wb[:, k, kt, oc * 128:(oc + 1) * 128], ysl[b_][:, k, kt, :], start=(kt == 0), stop=(kt == 3))
                                    g_ = gsg[(oc * 4 + k) % 2]
                                    K.act(g_[:], psg[:, 0:n], AF.Sigmoid, bias=col(f"b_gate{l}{k}", oc))
                                    m_ = macc[oc % 2]
                                    if k == 0:
                                        K.tt(m_[:], g_[:], psb[:, 0:n], ALU.mult)
                                    else:
                                        K.tt(g_[:], g_[:], psb[:, 0:n], ALU.mult)
                                        K.tt(m_[:], m_[:], g_[:], ALU.add, e='pool')
                                K.copy(mT[b_][:, oc, :], macc[oc % 2][:], e='act')
                            for tt_ in range(2):
                                r0 = t0 + tt_ * 128
                                xr_ = xrow[tt_]
                                K.dma('pool', xr_[:], xin_d[r0:r0 + 128, :])
                                for hf in range(2):
                                    ps = PS[4 + hf]
                                    for kt in range(8):
                                        K.mm(ps[:], mT[b_][:, kt, tt_ * 128:(tt_ + 1) * 128], wo[:, kt, hf * 512:(hf + 1) * 512], start=(kt == 0), stop=(kt == 7))
                                    K.tt(orow[tt_][:, hf * 512:(hf + 1) * 512], ps[:], gtrow[:, cc, hf * 512:(hf + 1) * 512], ALU.mult)
                                K.tt(orow[tt_][:], orow[tt_][:], xr_[:], ALU.add, e='pool')
                                if not last:
                                    K.dma('sp', x1_d[r0:r0 + 128, :], orow[tt_][:])
                                else:
                                    s2 = ssq[tt_]
                                    K.act(jk[:], orow[tt_][:], AF.Square, accum=s2[:])
                                    K.rsqrt(s2[:], s2[:], 1.0 / D)
                                    K.stt(orow[tt_][:], orow[tt_][:], s2[:, 0:1], fnr[:], ALU.mult, ALU.mult)
                                    K.dma('sp', out_d[r0 - NCTX:r0 - NCTX + 128, :], orow[tt_][:])
                    K.barrier()
        K.barrier()
        print("instructions:", K.ninst, {e: K.cnt[e] for e in K.cnt}, "sems", K.nsem)
    return nc


def make_inputs(inp):
    shared, coff, roff = prep_shared(inp)
    maps = []
    for b in range(8):
        m = dict(shared)
        m["xc"] = np.ascontiguousarray(np.concatenate([inp["ctx"][b], inp["x"][b]], axis=0), np.float32)
        cc = np.stack([np.asarray(inp["c"][b], np.float32).reshape(8, 128).T, np.asarray(inp["c_ctx"], np.float32).reshape(8, 128).T], axis=-1)
        m["ccol"] = np.ascontiguousarray(cc, np.float32)
        maps.append(m)
    return shared, coff, roff, maps


def kernel(**inputs):
    inp = {k: np.asarray(v) for k, v in inputs.items()}
    shared, coff, roff, maps = make_inputs(inp)
    nc = build(shared, coff, roff)
    res = run_bass_kernel_spmd(nc, maps, core_ids=list(range(8)))
    out = np.stack([np.asarray(r["out"], np.float32) for r in res.results], axis=0)
    return out
```

```python
import os
import contextlib
import numpy as np
import concourse.bass as bass
import concourse.mybir as mybir
from concourse.bass_utils import run_bass_kernel_spmd
from concourse.ap import AP

F32 = mybir.dt.float32
BF16 = mybir.dt.bfloat16
AF = mybir.ActivationFunctionType
ALU = mybir.AluOpType

T = 4352
NCTX = 256
NLAT = 4096
D = 1024
NT = 34
EPS = 1e-6
PI = float(np.pi)
C_Q, C_I, C_FF, C_FB, C_HZ, C_SU, C_SZ, C_LX, C_LZ, C_XBC, C_DT, C_MZ = (
    0, 512, 1024, 1536, 2048, 2560, 3072, 3584, 4096, 4608, 5376, 5392)
IN_COLS = 5904
SL = 256
SLABS = [(0, 256)] + [(256 + 512 * i, 512) for i in range(8)]
SLABS256 = [(256 * i, 256) for i in range(17)]


def rev(ap):
    pat = [list(p) for p in ap.ap]
    step, cnt = pat[-1]
    off = ap.offset + step * (cnt - 1)
    pat[-1] = [-step, cnt]
    return AP(ap.tensor, off, pat)


class S:
    def __init__(self, ap, sub):
        self.ap = ap
        self.sub = sub


def _ap(x):
    return x.ap if isinstance(x, S) else x


def _key(x):
    if isinstance(x, S):
        nm = getattr(x.ap, 'tensor', x.ap).name
        if nm.startswith('ps'):
            return (nm, None)
        return (nm, x.sub)
    return (getattr(x, 'tensor', x).name, None)


class Sched:
    def __init__(self, nc, stack):
        self.nc = nc
        self.eng = {'pe': nc.tensor, 'dve': nc.vector, 'act': nc.scalar, 'pool': nc.gpsimd, 'sp': nc.sync}
        self.cnt = {e: 0 for e in self.eng}
        self.EPOCH = 20000
        self.sems = {e: [] for e in self.eng}
        self.stack = stack
        self.seen = {e: {} for e in self.eng}
        self.lastw = {}
        self.readers = {}
        self.nsem = 0
        self.dma_slots = {}
        self.dma_next = {}
        for q in ('sp', 'pool', 'act'):
            n = 12 if q == 'sp' else 6
            self.dma_slots[q] = [[self._newsem(), 0] for _ in range(n)]
            self.dma_next[q] = 0
        self.ninst = 0
        self.nopool = False

    def _newsem(self):
        s = self.stack.enter_context(self.nc.semaphore("s%d" % self.nsem))
        self.nsem += 1
        return (self.nsem, s)

    def _token(self, e):
        n = self.cnt[e]
        ep = n // self.EPOCH
        while len(self.sems[e]) <= ep:
            self.sems[e].append(self._newsem())
        sid, sem = self.sems[e][ep]
        self.cnt[e] = n + 1
        return (sid, sem, n - ep * self.EPOCH + 1, e)

    def _wait(self, e, toks, pesync=False):
        best = {}
        for t in toks:
            if t is None:
                continue
            if e == 'pe' and t[3] == 'pe' and not pesync:
                continue
            if t[0] not in best or best[t[0]][2] < t[2]:
                best[t[0]] = t
        for sid, t in best.items():
            if self.seen[e].get(sid, 0) < t[2]:
                self.eng[e].wait_ge(t[1], t[2])
                self.seen[e][sid] = t[2]

    @staticmethod
    def _psplit(outs, ins):
        o2 = list(outs)
        i2 = []
        for x in ins:
            if _key(x)[0].startswith('ps'):
                o2.append(x)
            else:
                i2.append(x)
        return o2, i2

    def _deps(self, outs, ins):
        outs, ins = self._psplit(outs, ins)
        toks = []
        for x in ins:
            toks.append(self.lastw.get(_key(x)))
        for x in outs:
            k = _key(x)
            toks.append(self.lastw.get(k))
            toks.extend(self.readers.get(k, {}).values())
        return toks

    def _record(self, tok, outs, ins):
        outs, ins = self._psplit(outs, ins)
        for x in ins:
            self.readers.setdefault(_key(x), {})[tok[0]] = tok
        for x in outs:
            k = _key(x)
            self.lastw[k] = tok
            self.readers[k] = {}

    def op(self, e, outs, ins, fn, pesync=False):
        self._wait(e, self._deps(outs, ins), pesync)
        inst = fn(self.eng[e])
        tok = self._token(e)
        inst.then_inc(tok[1], 1)
        self._record(tok, outs, ins)
        self.ninst += 1
        return inst

    def dma(self, q, out, in_, **kw):
        slots = self.dma_slots[q]
        i = self.dma_next[q]
        self.dma_next[q] = (i + 1) % len(slots)
        (sid, sem), val = slots[i]
        toks = self._deps([out], [in_])
        if val > 0:
            toks.append((sid, sem, val, 'dma'))
        self._wait(q, toks)
        self.eng[q].dma_start(out=_ap(out), in_=_ap(in_), **kw).then_inc(sem, 16)
        slots[i][1] = val + 16
        tok = (sid, sem, val + 16, 'dma')
        self._record(tok, [out], [in_])
        self.ninst += 1

    def barrier(self):
        toks = []
        for e in self.eng:
            n = self.cnt[e]
            if n == 0:
                continue
            ep = (n - 1) // self.EPOCH
            sid, sem = self.sems[e][ep]
            toks.append((sid, sem, n - ep * self.EPOCH, e + '_b'))
        for q, slots in self.dma_slots.items():
            for (sid, sem), val in slots:
                if val > 0:
                    toks.append((sid, sem, val, 'dma'))
        for e in self.eng:
            self._wait(e, toks)
        self.lastw = {}
        self.readers = {}

    def tt(self, out, in0, in1, op, e='dve'):
        if e == 'pool' and self.nopool:
            e = 'dve'
        return self.op(e, [out], [in0, in1], lambda g: g.tensor_tensor(out=_ap(out), in0=_ap(in0), in1=_ap(in1), op=op))

    def ts(self, out, in0, s1, op0, s2=None, op1=None, e='dve'):
        ins = [in0] + [s for s in (s1, s2) if isinstance(s, (AP, S))]
        if op1 is None:
            return self.op(e, [out], ins, lambda g: g.tensor_scalar(out=_ap(out), in0=_ap(in0), scalar1=_ap(s1) if isinstance(s1, (AP, S)) else s1, scalar2=None, op0=op0))
        return self.op(e, [out], ins, lambda g: g.tensor_scalar(out=_ap(out), in0=_ap(in0), scalar1=_ap(s1) if isinstance(s1, (AP, S)) else s1,
                                                                 scalar2=_ap(s2) if isinstance(s2, (AP, S)) else s2, op0=op0, op1=op1))

    def stt(self, out, in0, sc, in1, op0, op1, e='dve'):
        ins = [in0, in1] + ([sc] if isinstance(sc, (AP, S)) else [])
        return self.op(e, [out], ins, lambda g: g.scalar_tensor_tensor(out=_ap(out), in0=_ap(in0), scalar=_ap(sc) if isinstance(sc, (AP, S)) else sc,
                                                                        in1=_ap(in1), op0=op0, op1=op1))

    def act(self, out, in_, func, bias=None, scale=None, accum=None):
        ins = [in_] + [s for s in (bias, scale) if isinstance(s, (AP, S))]
        outs = [out] + ([accum] if accum is not None else [])
        kw = {}
        if bias is not None:
            kw['bias'] = _ap(bias) if isinstance(bias, (AP, S)) else bias
        if scale is not None:
            kw['scale'] = _ap(scale) if isinstance(scale, (AP, S)) else scale
        if accum is not None:
            kw['accum_out'] = _ap(accum)
        return self.op('act', outs, ins, lambda g: g.activation(out=_ap(out), in_=_ap(in_), func=func, **kw))

    def rsqrt(self, out, in_, scale):
        self.act(out, in_, AF.Ln, bias=self.epsc, scale=scale)
        return self.act(out, out, AF.Exp, scale=-0.5)

    def sinred(self, out, x, shift, tmp, negpi):
        MAGIC = 12582912.0
        C1 = 6.28125
        C2 = float(2 * np.pi - 6.28125)
        xs = x
        if shift != 0.0:
            self.ts(out, x, float(shift), ALU.add)
            xs = out
        self.ts(tmp, xs, 1.0 / (2 * np.pi), ALU.mult, MAGIC, ALU.add)
        self.ts(tmp, tmp, -MAGIC, ALU.add)
        self.stt(out, tmp, -C1, xs, ALU.mult, ALU.add)
        self.stt(out, tmp, -C2, out, ALU.mult, ALU.add)
        self.ts(out, out, PI, ALU.min, -PI, ALU.max)
        return self.act(out, out, AF.Sin)

    def recip(self, out, in_):
        return self.op('dve', [out], [in_], lambda g: g.reciprocal(out=_ap(out), in_=_ap(in_)))

    def copy(self, out, in_, e='dve'):
        if e == 'act':
            return self.act(out, in_, AF.Copy)
        return self.op(e, [out], [in_], lambda g: g.tensor_copy(out=_ap(out), in_=_ap(in_)))

    def memset(self, out, val, e='dve'):
        return self.op(e, [out], [], lambda g: g.memset(_ap(out), val))

    def mm(self, out, lhsT, rhs, start=True, stop=True):
        return self.op('pe', [out], [lhsT, rhs], lambda g: g.matmul(_ap(out), lhsT=_ap(lhsT), rhs=_ap(rhs), start=start, stop=stop))

    def tr(self, out, in_, ident):
        return self.op('pe', [out], [in_, ident], lambda g: g.transpose(_ap(out), _ap(in_), _ap(ident)))

    def scan(self, out, d0, d1, init, op0=ALU.mult, op1=ALU.add):
        ins = [d0, d1] + ([init] if isinstance(init, (AP, S)) else [])
        return self.op('dve', [out], ins, lambda g: g.tensor_tensor_scan(_ap(out), _ap(d0), _ap(d1), _ap(init) if isinstance(init, (AP, S)) else init, op0, op1))


class ColPack:
    def __init__(self):
        self.parts = []
        self.off = {}
        self.n = 0

    def add(self, name, vec):
        v = np.asarray(vec, np.float32).reshape(-1, 128).T
        self.off[name] = (self.n, v.shape[1])
        self.parts.append(v)
        self.n += v.shape[1]

    def add_raw(self, name, arr):
        arr = np.asarray(arr, np.float32)
        self.off[name] = (self.n, arr.shape[1])
        self.parts.append(arr)
        self.n += arr.shape[1]

    def array(self):
        return np.ascontiguousarray(np.concatenate(self.parts, axis=1))


class RowPack:
    def __init__(self):
        self.parts = []
        self.off = {}
        self.n = 0

    def add(self, name, vec):
        v = np.asarray(vec, np.float32).reshape(-1)
        self.off[name] = (self.n, v.size)
        self.parts.append(v)
        self.n += v.size

    def array(self):
        v = np.concatenate(self.parts)
        return np.ascontiguousarray(np.broadcast_to(v[None, :], (128, v.size)))


def const_mats():
    j = np.arange(128)[:, None]
    i = np.arange(128)[None, :]
    same = (j // 64) == (i // 64)
    ident = (j == i).astype(np.float32)
    tri_f = ((j <= i) & same).astype(np.float32)
    tri_b = ((j >= i) & same).astype(np.float32)
    su_f = ((j > i) & same).astype(np.float32)
    su_b = ((j < i) & same).astype(np.float32)
    ones = same.astype(np.float32)
    e2m_f = tri_f - 0.5 * ones
    e2m_b = tri_b - 0.5 * ones
    allone = np.ones((128, 128), np.float32)
    iota = np.broadcast_to(np.arange(1, 129, dtype=np.float32)[None, :], (128, 128))
    iota2 = np.broadcast_to(np.arange(129, 257, dtype=np.float32)[None, :], (128, 128))
    m = np.stack([ident, tri_f, tri_b, su_f, su_b, e2m_f, e2m_b, ones, allone, iota, iota2], axis=1)
    return np.ascontiguousarray(m.astype(np.float32))


CI_ID, CI_TRI, CI_SU, CI_E2M, CI_ONES, CI_ALL, CI_IOTA = 0, 1, 3, 5, 7, 8, 9


def prep_shared(inp):
    cp = ColPack()
    rp = RowPack()
    for l in range(2):
        cp.add(f"norm_w{l}", inp["norm_w"][l])
        cp.add(f"b_mod{l}", inp["b_mod"][l])
        cp.add(f"hg_norm{l}", inp["hg_norm"][l])
        for d in range(2):
            cp.add(f"s5_are{l}{d}", inp["s5_a_re"][l, d])
            cp.add(f"s5_aim{l}{d}", inp["s5_a_im"][l, d])
            cp.add(f"s5_ls{l}{d}", np.repeat(inp["s5_log_step"][l, d], 64))
            cp.add(f"lru_lam{l}{d}", inp["lru_lam"][l, d])
            for g in range(2):
                cp.add(f"lru_gb{l}{d}{g}", inp["lru_gate_b"][l, d, g])
        cp.add(f"s5_d{l}", inp["s5_d"][l])
        cp.add(f"s5_bglu{l}", inp["s5_b_glu"][l])
        for k in range(4):
            cp.add(f"lru_cw{l}{k}", inp["lru_conv_w"][l, k])
            cp.add(f"m2_cw{l}{k}", inp["m2_conv_w"][l, k])
            cp.add(f"b_gate{l}{k}", inp["b_gate"][l, k])
        cp.add(f"lru_cb{l}", inp["lru_conv_b"][l])
        cp.add(f"m2_cb{l}", inp["m2_conv_b"][l])
        rp.add(f"gtb{l}", inp["b_mod"][l][2048:3072])
        rp.add(f"m2_norm{l}", inp["m2_norm"][l])
        rp.add(f"m2_d{l}", inp["m2_d"][l])
        for d in range(2):
            rp.add(f"m2_dtb{l}{d}", inp["m2_dt_bias"][l, d])
            rp.add(f"m2_alog{l}{d}", inp["m2_a_log"][l, d])
    for l in range(3):
        for d in range(2):
            cp.add(f"hg_lg{l}{d}", inp["hg_lb_logits"][l, d])
            rp.add(f"hg_lg{l}{d}", inp["hg_lb_logits"][l, d])
    rp.add("final_norm", inp["final_norm"])
    bq = np.zeros((2, 2, 128, 16, 128), np.float32)
    cq = np.zeros((2, 2, 128, 16, 64), np.float32)
    for l in range(2):
        for ri, (bn, cn) in enumerate((("s5_b_re", "s5_c_re"), ("s5_b_im", "s5_c_im"))):
            B = inp[bn][l]
            C = inp[cn][l]
            for g in range(32):
                k = g // 2
                gl = g % 2
                c = k // 4
                kq = k % 4
                bq[l, ri, 32 * kq + 16 * gl:32 * kq + 16 * gl + 16, k, 64 * gl:64 * gl + 64] = B[g].T
                cq[l, ri, 64 * gl:64 * gl + 64, k, 32 * (kq % 2) + 16 * gl:32 * (kq % 2) + 16 * gl + 16] = C[g].T
    gw = np.zeros((2, 128, 16, 128), np.float32)
    for l in range(2):
        for d in range(2):
            for g in range(2):
                for j in range(4):
                    for bl in range(2):
                        gw[l, 64 * bl:64 * bl + 64, (d * 2 + g) * 4 + j, 64 * bl:64 * bl + 64] = inp["lru_gate_w"][l, d, g, 2 * j + bl]
    shared = {
        "cols": cp.array(), "rows": rp.array(), "cst": const_mats(),
        "s5bq": np.ascontiguousarray(bq), "s5cq": np.ascontiguousarray(cq), "lrugw": np.ascontiguousarray(gw),
        "w_mod": np.ascontiguousarray(inp["w_mod"], np.float32), "w_in": np.ascontiguousarray(inp["w_in"], np.float32),
        "w_gate": np.ascontiguousarray(inp["w_gate"], np.float32), "w_branch": np.ascontiguousarray(inp["w_branch"], np.float32),
        "w_out": np.ascontiguousarray(inp["w_out"], np.float32), "w_glu": np.ascontiguousarray(inp["s5_w_glu"], np.float32),
    }
    return shared, cp.off, rp.off


def build(shared, coff, roff, dbg=None):
    dbg = dbg or {}
    layers = dbg.get("layers", [0, 1])
    phases = dbg.get("phases", "HSLMG")
    nc = bass.Bass("TRN2", target_bir_lowering=False)
    okind = "ExternalOutput" if dbg.get("dump") else "Internal"

    def din(name, shape, dt=F32):
        return nc.dram_tensor(name, list(shape), dt, kind="ExternalInput").ap()

    xc_d = din("xc", [T, D])
    ccol_d = din("ccol", [128, 8, 2])
    cols_d = din("cols", list(shared["cols"].shape))
    rows_d = din("rows", list(shared["rows"].shape))
    cst_d = din("cst", [128, 11, 128])
    s5bq_d = din("s5bq", [2, 2, 128, 16, 128])
    s5cq_d = din("s5cq", [2, 2, 128, 16, 64])
    lrugw_d = din("lrugw", [2, 128, 16, 128])
    wmod_d = din("w_mod", [2, D, 3072])
    win_d = din("w_in", [2, D, IN_COLS])
    wgate_d = din("w_gate", [2, 4, D, D])
    wbr_d = din("w_branch", [2, 4, 512, D])
    wout_d = din("w_out", [2, D, D])
    wglu_d = din("w_glu", [2, 512, 512])
    out_d = nc.dram_tensor("out", [NLAT, D], F32, kind="ExternalOutput").ap()
    x1_d = nc.dram_tensor("x1s", [T, D], F32, kind=okind).ap()
    hs_d = nc.dram_tensor("hs", [128, 8, T], BF16, kind="Internal").ap()
    ys_d = nc.dram_tensor("ys", [4, 128, 4, T], BF16, kind=okind).ap()
    gs_d = nc.dram_tensor("gs", [128, 4, T], BF16, kind=okind).ap()
    ym_d = nc.dram_tensor("ym", [T, 512], F32, kind="Internal").ap()
    ysd_d = nc.dram_tensor("ysd", [128, 4, T], F32, kind=okind).ap() if dbg.get("dump") else None

    NCOLS = shared["cols"].shape[1]

    with contextlib.ExitStack() as gstack:
        K = Sched(nc, gstack)
        K.nopool = dbg.get('nopool', True)

        uniq = [0]

        def sb(stack, name, shape, dt=F32):
            uniq[0] += 1
            return stack.enter_context(nc.sbuf_tensor("%s_u%d" % (name, uniq[0]), list(shape), dt))

        cst = sb(gstack, "cst", [128, 11, 128])
        cols = sb(gstack, "cols", [128, NCOLS])
        identb = sb(gstack, "identb", [128, 128], BF16)
        ccol = sb(gstack, "ccol", [128, 8, 2])
        PS = [gstack.enter_context(nc.psum_tensor("ps%d" % i, [128, 512], F32)) for i in range(8)]
        K.dma('sp', cst[:], cst_d)
        K.dma('sp', cols[:], cols_d)
        K.dma('sp', ccol[:], ccol_d)
        K.copy(identb[:], cst[:, CI_ID, :])
        epsc = sb(gstack, "epsc", [128, 1])
        K.memset(epsc[:], EPS)
        K.epsc = epsc[:, 0:1]

        def col(name, j=0, n=1):
            o, k = coff[name]
            return cols[:, o + j:o + j + n]

        def rowb(stack, name, tname):
            o, n = roff[name]
            t = sb(stack, tname, [128, n])
            K.dma('sp', t[:], rows_d[:, o:o + n])
            return t

        ident = cst[:, CI_ID, :]
        ONESBD = cst[:, CI_ONES, :]
        ALLONE = cst[:, CI_ALL, :]
        IOTA = cst[:, CI_IOTA, :]

        def TRI(d):
            return cst[:, CI_TRI + d, :]

        def SU(d):
            return cst[:, CI_SU + d, :]

        def E2M(d):
            return cst[:, CI_E2M + d, :]

        for l in layers:
            xin_d = xc_d if l == 0 else x1_d
            last = (l == 1)
            with contextlib.ExitStack() as lstack:
                modcol = sb(lstack, "modcol", [128, 24, 2])
                g1col = sb(lstack, "g1col", [128, 8, 2])
                silc = sb(lstack, "silc", [128, 8, 2])
                K.act(silc[:], ccol[:], AF.Silu)
                with contextlib.ExitStack() as st:
                    wm = [sb(st, "wm%d" % i, [128, 8, 512]) for i in range(2)]
                    for jb in range(6):
                        w = wm[jb % 2]
                        K.dma('sp', w[:], wmod_d[l, :, jb * 512:(jb + 1) * 512].rearrange("(k p) n -> p k n", p=128))
                        for jj in range(4):
                            j = jb * 4 + jj
                            for kt in range(8):
                                K.mm(PS[0][:, j * 2:j * 2 + 2], w[:, kt, jj * 128:(jj + 1) * 128], silc[:, kt, :], start=(kt == 0), stop=(kt == 7))
                    o, _ = coff[f"b_mod{l}"]
                    K.tt(modcol[:], PS[0][:, 0:48].rearrange("p (j c) -> p j c", c=2),
                         cols[:, o:o + 24].rearrange("p (j c) -> p j c", c=1).to_broadcast([128, 24, 2]), ALU.add)
                    o, _ = coff[f"norm_w{l}"]
                    K.ts(g1col[:], modcol[:, 8:16, :], 1.0, ALU.add)
                    K.tt(g1col[:], g1col[:], cols[:, o:o + 8].rearrange("p (j c) -> p j c", c=1).to_broadcast([128, 8, 2]), ALU.mult)
                shcol = modcol[:, 0:8, :]

                p15 = contextlib.ExitStack()
                hT = sb(p15, "hT", [128, 8, T], BF16)
                wst = [sb(p15, "wst%d" % i, [128, 8, 128]) for i in range(2)]
                wblk = sb(p15, "wblk", [128, 8, 768], BF16)
                wst_i = [0]

                def load_w(colranges):
                    pos = 0
                    for (c0, n) in colranges:
                        for s0 in range(0, n, 128):
                            m = min(128, n - s0)
                            w = wst[wst_i[0] % 2]
                            wst_i[0] += 1
                            K.dma('sp' if wst_i[0] % 2 else 'pool', w[:, :, 0:m], win_d[l, :, c0 + s0:c0 + s0 + m].rearrange("(k p) n -> p k n", p=128))
                            K.copy(wblk[:, :, pos:pos + m], w[:, :, 0:m], e='dve')
                            pos += m
                    return pos

                def proj_fm(ps, wc0, ncols, t0, n):
                    for kt in range(8):
                        K.mm(ps[0:ncols, 0:n], wblk[:, kt, wc0:wc0 + ncols], hT[:, kt, t0:t0 + n], start=(kt == 0), stop=(kt == 7))

                def proj_tm(ps, wc0, ncols, t0, pc0=0):
                    for kt in range(8):
                        K.mm(ps[:, pc0:pc0 + ncols], hT[:, kt, t0:t0 + 128], wblk[:, kt, wc0:wc0 + ncols], start=(kt == 0), stop=(kt == 7))

                with contextlib.ExitStack() as st:
                    xr = [sb(st, "xr%d" % i, [128, D]) for i in range(2)]
                    xn = [sb(st, "xn%d" % i, [128, D]) for i in range(2)]
                    junk = sb(st, "junk", [128, D])
                    ss = [sb(st, "ss%d" % i, [128, 1]) for i in range(2)]
                    for p in range(NT):
                        cc = 1 if p < 2 else 0
                        a, b, s_ = xr[p % 2], xn[p % 2], ss[p % 2]
                        K.dma('sp' if p % 2 == 0 else 'pool', a[:], xin_d[p * 128:(p + 1) * 128, :])
                        K.act(junk[:], a[:], AF.Square, accum=s_[:])
                        K.rsqrt(s_[:], s_[:], 1.0 / D)
                        K.ts(b[:], a[:], s_[:, 0:1], ALU.mult)
                        for half in range(2):
                            ps = PS[(p * 2 + half) % 4]
                            for q in range(4):
                                kt = half * 4 + q
                                K.tr(ps[:, q * 128:(q + 1) * 128], b[:, kt * 128:(kt + 1) * 128], ident)
                            for q in range(4):
                                kt = half * 4 + q
                                K.act(hT[:, kt, p * 128:(p + 1) * 128], ps[:, q * 128:(q + 1) * 128], AF.Identity,
                                      bias=shcol[:, kt, cc:cc + 1], scale=g1col[:, kt, cc:cc + 1])
                    K.dma('sp', hs_d, hT[:])

                if 'H' in phases:
                    with contextlib.ExitStack() as st:
                        stl = contextlib.ExitStack()
                        lbc = sb(st, "lbc", [128, 2, 4])
                        lbr = sb(st, "lbr", [128, 2, 512])
                        lgc = sb(stl, "lgc", [128, 3, 2, 4])
                        lgr = sb(stl, "lgr", [128, 3, 2, 512])
                        tmpc = sb(stl, "tmpc", [128, 2, 4])
                        tmpr = sb(stl, "tmpr", [128, 2, 512])
                        for ll in (range(3) if 'B' in dbg.get("hs", "BAF") else []):
                            for d in range(2):
                                o, _ = coff[f"hg_lg{ll}{d}"]
                                K.act(lgc[:, ll, d, :], cols[:, o:o + 4], AF.Exp)
                                o, n = roff[f"hg_lg{ll}{d}"]
                                K.dma('sp', lgr[:, ll, d, :], rows_d[:, o:o + n])
                        if 'B' in dbg.get("hs", "BAF"):
                            K.act(lgr[:], lgr[:], AF.Exp)
                        for (lg, lb_, tmp) in (((lgc, lbc, tmpc), (lgr, lbr, tmpr)) if 'B' in dbg.get("hs", "BAF") else []):
                            K.tt(tmp[:], lg[:, 0], lg[:, 1], ALU.add)
                            K.tt(tmp[:], tmp[:], lg[:, 2], ALU.add)
                            if l == 0:
                                K.tt(lb_[:], lg[:, 1], lg[:, 2], ALU.add)
                            else:
                                K.copy(lb_[:], lg[:, 2])
                            K.recip(tmp[:], tmp[:])
                            K.tt(lb_[:], lb_[:], tmp[:], ALU.mult)
                        K.barrier()
                        stl.close()
                        qs = sb(st, "qs", [128, T])
                        kkTs = [sb(st, "kkTs%d" % i, [128, T]) for i in range(2)]
                        vt = sb(st, "vt", [128, NT, 128], BF16)
                        osum = sb(st, "osum", [128, T])
                        SstD = [sb(st, "Sst%d" % i, [128, 128]) for i in range(2)]
                        SbfD = [[sb(st, "Sbf%d_%d" % (d_, i), [128, 128], BF16) for i in range(2)] for d_ in range(2)]

                        def tl(name, dt=F32, n=2, shape=(128, 128)):
                            return [sb(st, "%s%d" % (name, i), list(shape), dt) for i in range(n)]
                        def tl2(name, dt=F32):
                            return [tl(name + "d%d_" % d_, dt) for d_ in range(2)]
                        kkM, lfM = tl2("kkM"), tl2("lfM")
                        E1, E2, E3, EU = tl2("E1"), tl2("E2"), tl2("E3"), tl2("EU")
                        qb, qt, ktl, kh, attm = tl2("qb", BF16), tl2("qt", BF16), tl2("ktl", BF16), tl2("kh", BF16), tl2("attm", BF16)
                        o2 = tl("o2", F32, 1, (128, 512)) * 2
                        rs = tl("rs", F32, 2, (128, 512))
                        zz = tl("zz", F32, 2, (128, 512))
                        yo = tl("yo", BF16, 2, (128, 512))
                        for h in range(dbg.get("nheads", 4)):
                            hc = h * 128
                            load_w([(C_Q + hc, 128), (C_I + hc, 128), (C_FF + hc, 128), (C_FB + hc, 128), (C_HZ + hc, 128)])
                            for si_, (t0, n) in enumerate(SLABS):
                                ps = PS[si_ % 2]
                                proj_fm(ps, 0, 128, t0, n)
                                K.act(qs[:, t0:t0 + n], ps[:, 0:n], AF.Silu)
                            for p in range(NT):
                                ps = PS[2 + p % 2]
                                proj_tm(ps, 128, 128, p * 128, pc0=0)
                                K.copy(vt[:, p, :], ps[:, 0:128])
                            for d in range(2):
                                for si_, (t0, n) in enumerate(SLABS):
                                    ps = PS[4 + si_ % 2]
                                    proj_fm(ps, 256 + d * 128, 128, t0, n)
                                    kv_ = kkTs[d][:, t0:t0 + n]
                                    K.act(kv_, ps[:, 0:n], AF.Exp)
                                    K.act(kv_, kv_, AF.Ln, bias=1.0)
                                    K.act(kv_, kv_, AF.Exp, scale=-1.0)
                                    K.ts(kv_, kv_, lbc[:, d, h:h + 1], ALU.mult)
                            orders = [list(range(NT)), [1, 0] + list(range(NT - 1, 1, -1))]
                            sidx = [0, 0]
                            for d in range(2):
                                K.memset(SstD[d][:], 0.0)
                                K.memset(SbfD[d][0][:], 0.0)

                            def emit_X(it):
                                b_ = it % 2
                                dd = (0, 1)
                                p_ = [orders[d][it] for d in dd]
                                for d in dd:
                                    K.tr(PS[d][:, 0:128], kkTs[d][:, p_[d] * 128:(p_[d] + 1) * 128], ident)
                                for d in dd:
                                    K.copy(kkM[d][b_][:], PS[d][:, 0:128], e='act')
                                for d in dd:
                                    K.act(lfM[d][b_][:], kkM[d][b_][:], AF.Ln, bias=1.0, scale=-1.0)
                                for d in dd:
                                    psB = PS[2 + d]
                                    K.mm(psB[:, 0:128], lfM[d][b_][:], TRI(d))
                                    K.mm(psB[:, 128:256], lfM[d][b_][:], E2M(d))
                                    K.mm(psB[:, 256:384], SU(d), lfM[d][b_][:])
                                for d in dd:
                                    psB = PS[2 + d]
                                    K.act(E1[d][b_][:], psB[:, 0:128], AF.Exp)
                                    K.act(E2[d][b_][:], psB[:, 128:256], AF.Exp)
                                    K.act(E3[d][b_][:], psB[:, 128:256], AF.Exp, scale=-1.0)
                                    K.act(EU[d][b_][:], psB[:, 256:384], AF.Exp)
                                for d in dd:
                                    t0 = p_[d] * 128
                                    K.tt(qt[d][b_][:], qs[:, t0:t0 + 128], E2[d][b_][:], ALU.mult, e='pool')
                                    K.tt(ktl[d][b_][:], kkTs[d][:, t0:t0 + 128], E3[d][b_][:], ALU.mult)
                                for d in dd:
                                    t0 = p_[d] * 128
                                    K.tt(qb[d][b_][:], qs[:, t0:t0 + 128], E1[d][b_][:], ALU.mult, e='pool')
                                    K.tt(kh[d][b_][:], kkM[d][b_][:], EU[d][b_][:], ALU.mult)
                                for d in dd:
                                    K.mm(PS[d][:, 128:256], ktl[d][b_][:], qt[d][b_][:])
                                for d in dd:
                                    K.tt(attm[d][b_][:], PS[d][:, 128:256], TRI(d), ALU.mult)

                            ntl = min(NT, dbg.get("htiles", 99))
                            emit_X(0)
                            for it in range(ntl):
                                b_ = it % 2
                                if it + 1 < ntl:
                                    emit_X(it + 1)
                                for d in range(2):
                                    p = orders[d][it]
                                    K.mm(PS[4 + d][:, 0:128], vt[:, p, :], attm[d][b_][:], start=True, stop=False)
                                for ci in range(2):
                                    for d in range(2):
                                        p = orders[d][it]
                                        c0, lastcol = ([(0, 63), (64, 127)] if d == 0 else [(64, 64), (0, 0)])[ci]
                                        psO, psk = PS[4 + d], PS[6 + d]
                                        K.mm(psO[:, c0:c0 + 64], SbfD[d][sidx[d] % 2][:], qb[d][b_][:, c0:c0 + 64], start=False, stop=(ci == 1))
                                        K.mm(psk[:, 0:128], kh[d][b_][c0:c0 + 64, :], vt[c0:c0 + 64, p, :])
                                        K.stt(SstD[d][:], SstD[d][:], E1[d][b_][:, lastcol:lastcol + 1], psk[:, 0:128], ALU.mult, ALU.add)
                                        sidx[d] += 1
                                        K.copy(SbfD[d][sidx[d] % 2][:], SstD[d][:], e='act')
                                for d in range(2):
                                    p = orders[d][it]
                                    t0 = p * 128
                                    key = S(osum[:, t0:t0 + 128], p)
                                    first = (d == 0 and p >= 2) or (d == 1 and p < 2)
                                    stepf = p
                                    stepb = (1 - p) if p < 2 else (NT + 1 - p)
                                    is_first = (stepf < stepb) if d == 0 else (stepb < stepf)
                                    if stepf == stepb:
                                        is_first = (d == 0)
                                    if is_first:
                                        K.copy(key, PS[4 + d][:, 0:128])
                                    else:
                                        K.tt(key, key, PS[4 + d][:, 0:128], ALU.add)
                            o, _ = coff[f"hg_norm{l}"]
                            for si_, (t0, n) in (enumerate(SLABS) if 'F' in dbg.get("hs", "BAF") else []):
                                b_ = si_ % 2
                                ps = PS[si_ % 2]
                                ps2 = PS[2 + si_ % 2]
                                okeys = [S(osum[:, tt0:tt0 + 128], tt0 // 128) for tt0 in range(t0, t0 + n, 128)]
                                K.op('dve', [o2[b_][:]], okeys, lambda g, b_=b_, t0=t0, n=n: g.tensor_tensor(out=o2[b_][:, 0:n], in0=osum[:, t0:t0 + n], in1=osum[:, t0:t0 + n], op=ALU.mult))
                                K.mm(ps[:, 0:n], ALLONE, o2[b_][:, 0:n])
                                K.rsqrt(rs[b_][:, 0:n], ps[:, 0:n], 1.0 / 128)
                                proj_fm(ps2, 512, 128, t0, n)
                                K.act(zz[b_][:, 0:n], ps2[:, 0:n], AF.Silu)
                                K.op('dve', [rs[b_][:]], okeys + [rs[b_][:], cols[:, o + h:o + h + 1]],
                                     lambda g, b_=b_, t0=t0, n=n, o=o, h=h: g.scalar_tensor_tensor(out=rs[b_][:, 0:n], in0=rs[b_][:, 0:n], scalar=cols[:, o + h:o + h + 1], in1=osum[:, t0:t0 + n], op0=ALU.mult, op1=ALU.mult))
                                K.tt(yo[b_][:, 0:n], rs[b_][:, 0:n], zz[b_][:, 0:n], ALU.mult)
                                K.dma('sp', S(ys_d[0, :, h, t0:t0 + n], (0, h, si_)), yo[b_][:, 0:n])
                    K.barrier()

                if 'S' in phases:
                    with contextlib.ExitStack() as st:
                        bqb = sb(st, "bqb", [128, 2, 16, 128], BF16)
                        cqb = sb(st, "cqb", [128, 2, 16, 64], BF16)
                        with contextlib.ExitStack() as st2:
                            bq32 = sb(st2, "bq32", [128, 2, 16, 128])
                            cq32 = sb(st2, "cq32", [128, 2, 16, 64])
                            for ri in range(2):
                                K.dma('sp', bq32[:, ri], s5bq_d[l, ri])
                                K.dma('sp', cq32[:, ri], s5cq_d[l, ri])
                            K.copy(bqb[:], bq32[:])
                            K.copy(cqb[:], cq32[:])
                            K.barrier()
                        def cst16(name):
                            return [sb(st, "%s%d" % (name, d), [128, 16]) for d in range(2)]
                        stp, rr, th, cr_, ci_, t_a, t_b, t_c = (cst16("stp"), cst16("rr"), cst16("th"), cst16("cr"), cst16("ci"),
                                                                cst16("ta"), cst16("tb"), cst16("tc"))
                        for d in range(2):
                            are = col(f"s5_are{l}{d}", 0, 16)
                            aim = col(f"s5_aim{l}{d}", 0, 16)
                            K.act(stp[d][:], col(f"s5_ls{l}{d}", 0, 16), AF.Exp)
                            K.tt(th[d][:], aim, stp[d][:], ALU.mult)
                            K.tt(t_a[d][:], are, stp[d][:], ALU.mult)
                            K.act(rr[d][:], t_a[d][:], AF.Exp)
                        negpi = sb(st, "negpi", [128, 1])
                        K.memset(negpi[:], -PI)
                        for d in range(2):
                            are = col(f"s5_are{l}{d}", 0, 16)
                            aim = col(f"s5_aim{l}{d}", 0, 16)
                            sn, cs = t_b[d], t_c[d]
                            K.sinred(sn[:], th[d][:], 0.0, t_a[d][:], None)
                            K.sinred(cs[:], th[d][:], 0.5 * PI, t_a[d][:], None)
                            K.tt(cs[:], cs[:], rr[d][:], ALU.mult)
                            K.ts(cs[:], cs[:], -1.0, ALU.add)
                            K.tt(sn[:], sn[:], rr[d][:], ALU.mult)
                            K.tt(t_a[d][:], are, are, ALU.mult)
                            K.tt(cr_[d][:], aim, aim, ALU.mult)
                            K.tt(t_a[d][:], t_a[d][:], cr_[d][:], ALU.add)
                            K.recip(t_a[d][:], t_a[d][:])
                            K.tt(cr_[d][:], cs[:], are, ALU.mult)
                            K.tt(ci_[d][:], sn[:], aim, ALU.mult)
                            K.tt(cr_[d][:], cr_[d][:], ci_[d][:], ALU.add)
                            K.tt(cr_[d][:], cr_[d][:], t_a[d][:], ALU.mult)
                            K.tt(ci_[d][:], sn[:], are, ALU.mult)
                            K.tt(cs[:], cs[:], aim, ALU.mult)
                            K.tt(ci_[d][:], ci_[d][:], cs[:], ALU.subtract)
                            K.tt(ci_[d][:], ci_[d][:], t_a[d][:], ALU.mult)
                        ub = sb(st, "ub", [128, T], BF16)
                        ysum = sb(st, "ysum", [128, T])
                        gt_ = [sb(st, "gt%d" % i, [128, 512]) for i in range(1)] * 2
                        gu_ = [sb(st, "gu%d" % i, [128, 512]) for i in range(2)]
                        gb_ = [sb(st, "gb%d" % i, [128, 512], BF16) for i in range(2)]
                        st3 = contextlib.ExitStack()
                        tabc = sb(st3, "tabc", [128, 4, SL])
                        tabm = sb(st3, "tabm", [128, 4, SL])
                        tabB = sb(st3, "tabB", [128, 4, 2, SL])
                        tabD = sb(st3, "tabD", [128, 4, 2, SL])
                        NB = 4

                        def tl(name, dt=F32, n=NB, shape=(128, 2, SL)):
                            return [sb(st3, "%s%d" % (name, i), list(shape), dt) for i in range(n)]
                        A_, B_, W_ = tl("sA"), tl("sB"), tl("sW")
                        D_, V_, U_ = A_, A_, B_
                        ph, tq1, tq2 = B_[0][:, 0, :], B_[1][:, 0, :], B_[2][:, 0, :]
                        slt = tl("slt", F32, NB, (128, 2))
                        Ub = [[sb(st3, "Ub%d_%d" % (k, i), [128, 2, SL], BF16) for i in range(2)] for k in range(4)]
                        Vb = [[sb(st3, "Vb%d_%d" % (k, i), [128, 2, SL], BF16) for i in range(2)] for k in range(4)]
                        sl_ = [[sb(st3, "sl%d_%d" % (k, i), [128, 2]) for i in range(2)] for k in range(4)]
                        IOTAL = cst[:, CI_IOTA:CI_IOTA + SL // 128, :].rearrange("p a b -> p (a b)")

                        def view(ap, swap=False, rv=False):
                            pat = [list(p) for p in ap.ap]
                            off = ap.offset
                            if swap:
                                st_, cn = pat[-2]
                                off = off + st_ * (cn - 1)
                                pat[-2] = [-st_, cn]
                            if rv:
                                st_, cn = pat[-1]
                                off = off + st_ * (cn - 1)
                                pat[-1] = [-st_, cn]
                            return AP(ap.tensor, off, pat)
                        nchunk = T // SL
                        S5E = dbg.get('s5e', 'dve')
                        for c in range(4):
                            load_w([(C_SU + c * 128, 128)])
                            for si_, (t0, n) in enumerate(SLABS):
                                ps = PS[si_ % 2]
                                proj_fm(ps, 0, 128, t0, n)
                                K.copy(ub[:, t0:t0 + n], ps[:, 0:n], e='act')
                            for d in dbg.get("s5dirs", [0, 1]):
                                for kq in range(4):
                                    k = 4 * c + kq
                                    K.ts(ph[:], IOTAL, th[d][:, k:k + 1], ALU.mult)
                                    K.sinred(tabD[:, kq, 0, :], ph[:], 0.0, tq1[:], None)
                                    K.sinred(tabc[:, kq, :], ph[:], 0.5 * PI, tq1[:], None)
                                    K.ts(tq2[:], tabD[:, kq, 0, :], ci_[d][:, k:k + 1], ALU.mult)
                                    K.stt(tabm[:, kq, :], tabc[:, kq, :], cr_[d][:, k:k + 1], tq2[:], ALU.mult, ALU.add)
                                    K.ts(tq2[:], tabD[:, kq, 0, :], cr_[d][:, k:k + 1], ALU.mult)
                                    K.stt(tabB[:, kq, 1, :], tabc[:, kq, :], ci_[d][:, k:k + 1], tq2[:], ALU.mult, ALU.subtract)
                                    K.ts(tabB[:, kq, 0, :], tabB[:, kq, 1, :], -1.0, ALU.mult)
                                    K.ts(tabD[:, kq, 1, :], tabD[:, kq, 0, :], -1.0, ALU.mult)
                                if d == 0:
                                    corder = list(range(nchunk))
                                else:
                                    nc_ctx = NCTX // SL
                                    corder = list(range(nc_ctx - 1, -1, -1)) + list(range(nchunk - 1, nc_ctx - 1, -1))
                                def emit_P(it, ch, kq):
                                    k = 4 * c + kq
                                    t0 = ch * SL
                                    psP = PS[4 + kq]
                                    pb = 64 * (kq // 2)
                                    K.mm(psP[:, 0:SL], bqb[pb:pb + 64, 0, k, :], ub[pb:pb + 64, t0:t0 + SL])
                                    K.mm(psP[:, SL:2 * SL], bqb[pb:pb + 64, 1, k, :], ub[pb:pb + 64, t0:t0 + SL])

                                for kq in range(4):
                                    emit_P(0, corder[0], kq)
                                for it, ch in enumerate(corder):
                                    t0 = ch * SL
                                    par = it % 2
                                    psY = PS[2 + it % 2]
                                    rv = (d == 1)
                                    for kq in range(4):
                                        P3 = PS[4 + kq][:, 0:2 * SL].rearrange("p (a t) -> p a t", a=2)
                                        K.tt(A_[kq][:], view(P3, False, rv), tabm[:, kq:kq + 1, :].to_broadcast([128, 2, SL]), ALU.mult)
                                        K.tt(B_[kq][:], view(P3, True, rv), tabB[:, kq], ALU.mult)
                                    if it + 1 < len(corder):
                                        for kq in range(4):
                                            emit_P(it + 1, corder[it + 1], kq)
                                    for kq in range(4):
                                        K.tt(D_[kq][:], A_[kq][:], B_[kq][:], ALU.add, e=S5E)
                                    for hf in range(2):
                                        for kq in range(4):
                                            k = 4 * c + kq
                                            rk = rr[d][:, k:k + 1].to_broadcast([128, SL])
                                            init = 0.0 if it == 0 else sl_[kq][1 - par][:, hf:hf + 1]
                                            K.scan(W_[kq][:, hf, :], rk, D_[kq][:, hf, :], init, ALU.mult, ALU.add if hf == 0 else ALU.subtract)
                                    for kq in range(4):
                                        K.ts(sl_[kq][par][:], W_[kq][:, :, SL - 1], tabc[:, kq, SL - 1:SL], ALU.mult)
                                    for kq in range(4):
                                        K.tt(slt[kq][:], view(W_[kq][:], True, False)[:, :, SL - 1], tabD[:, kq, :, SL - 1], ALU.mult)
                                    for kq in range(4):
                                        K.tt(sl_[kq][par][:], sl_[kq][par][:], slt[kq][:], ALU.add)
                                    for kq in range(4):
                                        K.tt(view(Ub[kq][par][:], False, rv), W_[kq][:], tabc[:, kq:kq + 1, :].to_broadcast([128, 2, SL]), ALU.mult, e=S5E)
                                    for kq in range(4):
                                        K.tt(view(Vb[kq][par][:], False, rv), view(W_[kq][:], True, False), tabD[:, kq], ALU.mult)
                                    for kq in range(4):
                                        k = 4 * c + kq
                                        pb = 64 * (kq // 2)
                                        u_, v_ = Ub[kq][par], Vb[kq][par]
                                        K.mm(psY[pb:pb + 64, 0:SL], cqb[:, 0, k, :], u_[:, 0, :], start=(kq % 2 == 0), stop=False)
                                        K.mm(psY[pb:pb + 64, 0:SL], cqb[:, 0, k, :], v_[:, 0, :], start=False, stop=False)
                                        K.mm(psY[pb:pb + 64, 0:SL], cqb[:, 1, k, :], u_[:, 1, :], start=False, stop=False)
                                        K.mm(psY[pb:pb + 64, 0:SL], cqb[:, 1, k, :], v_[:, 1, :], start=False, stop=(kq % 2 == 1))
                                    if d == 0:
                                        K.stt(S(ysum[:, t0:t0 + SL], ch), ub[:, t0:t0 + SL], col(f"s5_d{l}", c), psY[:, 0:SL], ALU.mult, ALU.add)
                                    else:
                                        K.tt(S(ysum[:, t0:t0 + SL], ch), S(ysum[:, t0:t0 + SL], ch), psY[:, 0:SL], ALU.add)
                            if dbg.get("dump"):
                                K.barrier()
                                K.dma('sp', ysd_d[:, c, :], ysum[:])
                                K.barrier()
                            for si_, (t0, n) in enumerate(SLABS):
                                b_ = si_ % 2
                                ysl = [S(ysum[:, tt0:tt0 + SL], tt0 // SL) for tt0 in range(t0, t0 + n, SL)]
                                yv = ysum[:, t0:t0 + n]
                                K.op('dve', [gt_[b_][:]], ysl, lambda g, b_=b_, yv=yv, n=n: g.tensor_tensor(out=gt_[b_][:, 0:n], in0=yv, in1=yv, op=ALU.mult))
                                K.ts(gt_[b_][:, 0:n], gt_[b_][:, 0:n], 0.044715, ALU.mult, 1.0, ALU.add)
                                K.op('dve', [gu_[b_][:]], ysl + [gt_[b_][:]], lambda g, b_=b_, yv=yv, n=n: g.tensor_tensor(out=gu_[b_][:, 0:n], in0=gt_[b_][:, 0:n], in1=yv, op=ALU.mult))
                                K.act(gu_[b_][:, 0:n], gu_[b_][:, 0:n], AF.Sigmoid, scale=1.5957691216)
                                K.op('dve', [gb_[b_][:]], ysl + [gu_[b_][:]], lambda g, b_=b_, yv=yv, n=n: g.tensor_tensor(out=gb_[b_][:, 0:n], in0=gu_[b_][:, 0:n], in1=yv, op=ALU.mult))
                                K.dma('sp', S(gs_d[:, c, t0:t0 + n], (c, si_)), gb_[b_][:, 0:n])
                        K.barrier()
                        st3.close()
                        wg32 = sb(st, "wg32", [128, 4, 512])
                        wgb = sb(st, "wgb", [128, 4, 512], BF16)
                        K.dma('sp', wg32[:], wglu_d[l].rearrange("(k p) n -> p k n", p=128))
                        K.copy(wgb[:], wg32[:], e='pool')
                        load_w([(C_SZ, 512)])
                        gin = [sb(st, "gin%d" % i, [128, 4, 512], BF16) for i in range(2)]
                        sg = [sb(st, "sg%d" % i, [128, 512]) for i in range(2)]
                        for si_, (t0, n) in enumerate(SLABS):
                            b_ = si_ % 2
                            for c in range(4):
                                K.dma('sp', gin[b_][:, c, 0:n], S(gs_d[:, c, t0:t0 + n], (c, si_)))
                            for c in range(4):
                                ps = PS[c % 2]
                                ps2 = PS[2 + c % 2]
                                for kc in range(4):
                                    K.mm(ps[:, 0:n], wgb[:, kc, c * 128:(c + 1) * 128], gin[b_][:, kc, 0:n], start=(kc == 0), stop=(kc == 3))
                                K.act(sg[c % 2][:, 0:n], ps[:, 0:n], AF.Sigmoid, bias=col(f"s5_bglu{l}", c))
                                proj_fm(ps2, c * 128, 128, t0, n)
                                K.act(gu_[c % 2][:, 0:n], ps2[:, 0:n], AF.Sigmoid)
                                K.tt(gu_[c % 2][:, 0:n], gu_[c % 2][:, 0:n], ps2[:, 0:n], ALU.mult)
                                K.tt(sg[c % 2][:, 0:n], sg[c % 2][:, 0:n], gin[b_][:, c, 0:n], ALU.mult)
                                K.tt(gb_[c % 2][:, 0:n], sg[c % 2][:, 0:n], gu_[c % 2][:, 0:n], ALU.mult)
                                K.dma('sp', S(ys_d[1, :, c, t0:t0 + n], (1, c, si_)), gb_[c % 2][:, 0:n])
                    K.barrier()

                if 'L' in phases:
                    with contextlib.ExitStack() as st:
                        gw32 = sb(st, "gw32", [128, 16, 128])
                        gwb = sb(st, "gwb", [128, 16, 128], BF16)
                        K.dma('sp', gw32[:], lrugw_d[l])
                        K.copy(gwb[:], gw32[:])
                        cd = sb(st, "cd", [128, 2, 4])
                        cd2 = sb(st, "cd2", [128, 2, 4])
                        for d in range(2):
                            K.act(cd[:, d, :], col(f"lru_lam{l}{d}", 0, 4), AF.Exp, scale=-1.0)
                        K.act(cd[:], cd[:], AF.Ln, bias=1.0)
                        K.ts(cd2[:], cd[:], -16.0, ALU.mult)
                        K.ts(cd[:], cd[:], -8.0, ALU.mult)
                        xraw = sb(st, "xraw", [128, T])
                        xcv = sb(st, "xcv", [128, T])
                        xcb = sb(st, "xcb", [128, T], BF16)
                        hsum = sb(st, "hsum", [128, T])

                        def tl(name, dt=F32, n=2, shape=(128, 512)):
                            return [sb(st, "%s%d" % (name, i), list(shape), dt) for i in range(n)]
                        rg, ig, aa, a2, bb, hb, zz, yo = tl("rg"), tl("ig"), tl("aa"), tl("a2"), tl("bb"), tl("hb"), tl("zz"), tl("yo", BF16)
                        for j in range(4):
                            load_w([(C_LX + j * 128, 128), (C_LZ + j * 128, 128)])
                            for si_, (t0, n) in enumerate(SLABS):
                                ps = PS[si_ % 2]
                                proj_fm(ps, 0, 128, t0, n)
                                K.copy(xraw[:, t0:t0 + n], ps[:, 0:n], e='act')
                            w = [col(f"lru_cw{l}{k}", j) for k in range(4)]
                            cb = col(f"lru_cb{l}", j)
                            K.ts(xcv[:], xraw[:], w[1], ALU.mult, cb, ALU.add)
                            for (a0, nr, rl) in ((0, 1, 256), (256, 64, 64)):
                                xv = xraw[:, a0:a0 + nr * rl].rearrange("p (r t) -> p r t", t=rl)
                                ov = xcv[:, a0:a0 + nr * rl].rearrange("p (r t) -> p r t", t=rl)
                                K.stt(ov[:, :, 1:rl], xv[:, :, 0:rl - 1], w[0], ov[:, :, 1:rl], ALU.mult, ALU.add)
                                K.stt(ov[:, :, 0:rl - 1], xv[:, :, 1:rl], w[2], ov[:, :, 0:rl - 1], ALU.mult, ALU.add)
                                K.stt(ov[:, :, 0:rl - 2], xv[:, :, 2:rl], w[3], ov[:, :, 0:rl - 2], ALU.mult, ALU.add)
                            K.copy(xcb[:], xcv[:], e='dve')
                            for d in range(2):
                                order = list(range(9)) if d == 0 else [0] + list(range(8, 0, -1))
                                prev = None
                                for pi in range(0, len(order), 2):
                                    grp = [(it, order[it]) for it in range(pi, min(pi + 2, len(order)))]
                                    for it, si_ in grp:
                                        t0, n = SLABS[si_]
                                        K.mm(PS[2 + it % 2][:, 0:n], gwb[:, (d * 2 + 0) * 4 + j, :], xcb[:, t0:t0 + n])
                                        K.mm(PS[4 + it % 2][:, 0:n], gwb[:, (d * 2 + 1) * 4 + j, :], xcb[:, t0:t0 + n])
                                    for it, si_ in grp:
                                        t0, n = SLABS[si_]
                                        b_ = it % 2
                                        K.act(rg[b_][:, 0:n], PS[2 + it % 2][:, 0:n], AF.Sigmoid, bias=col(f"lru_gb{l}{d}0", j))
                                        K.act(ig[b_][:, 0:n], PS[4 + it % 2][:, 0:n], AF.Sigmoid, bias=col(f"lru_gb{l}{d}1", j))
                                    for it, si_ in grp:
                                        t0, n = SLABS[si_]
                                        b_ = it % 2
                                        K.act(aa[b_][:, 0:n], rg[b_][:, 0:n], AF.Exp, scale=cd[:, d, j:j + 1])
                                        K.act(a2[b_][:, 0:n], rg[b_][:, 0:n], AF.Exp, scale=cd2[:, d, j:j + 1])
                                    for it, si_ in grp:
                                        t0, n = SLABS[si_]
                                        b_ = it % 2
                                        K.ts(a2[b_][:, 0:n], a2[b_][:, 0:n], -1.0, ALU.mult, 1.0, ALU.add)
                                        K.ts(a2[b_][:, 0:n], a2[b_][:, 0:n], 0.0, ALU.max)
                                        K.tt(bb[b_][:, 0:n], ig[b_][:, 0:n], xcv[:, t0:t0 + n], ALU.mult, e='pool')
                                    for it, si_ in grp:
                                        t0, n = SLABS[si_]
                                        b_ = it % 2
                                        K.act(a2[b_][:, 0:n], a2[b_][:, 0:n], AF.Sqrt)
                                    for it, si_ in grp:
                                        t0, n = SLABS[si_]
                                        b_ = it % 2
                                        K.tt(bb[b_][:, 0:n], bb[b_][:, 0:n], a2[b_][:, 0:n], ALU.mult, e='pool')
                                        init = 0.0 if prev is None else prev
                                        if d == 0:
                                            K.scan(hsum[:, t0:t0 + n], aa[b_][:, 0:n], bb[b_][:, 0:n], init)
                                            prev = hsum[:, t0 + n - 1:t0 + n]
                                        else:
                                            K.scan(rev(hb[b_][:, 0:n]), rev(aa[b_][:, 0:n]), rev(bb[b_][:, 0:n]), init)
                                            prev = hb[b_][:, 0:1]
                                            K.tt(hsum[:, t0:t0 + n], hsum[:, t0:t0 + n], hb[b_][:, 0:n], ALU.add, e='pool')
                            for si_, (t0, n) in enumerate(SLABS):
                                b_ = si_ % 2
                                ps = PS[si_ % 2]
                                proj_fm(ps, 128, 128, t0, n)
                                K.act(zz[b_][:, 0:n], ps[:, 0:n], AF.Silu)
                                K.tt(yo[b_][:, 0:n], zz[b_][:, 0:n], hsum[:, t0:t0 + n], ALU.mult)
                                K.dma('sp', S(ys_d[2, :, j, t0:t0 + n], (2, j, si_)), yo[b_][:, 0:n])
                    K.barrier()

                if 'M' in phases:
                    with contextlib.ExitStack() as st:
                        xbc = sb(st, "xbc", [128, 6, T], BF16)
                        st2 = contextlib.ExitStack()
                        xraw = sb(st2, "mxraw", [128, T])
                        xcv = sb(st2, "mxcv", [128, T])
                        for j in range(6):
                            load_w([(C_XBC + j * 128, 128)])
                            for si_, (t0, n) in enumerate(SLABS):
                                ps = PS[si_ % 2]
                                proj_fm(ps, 0, 128, t0, n)
                                K.copy(xraw[:, t0:t0 + n], ps[:, 0:n], e='act')
                            w = [col(f"m2_cw{l}{k}", j) for k in range(4)]
                            cb = col(f"m2_cb{l}", j)
                            K.ts(xcv[:], xraw[:], w[1], ALU.mult, cb, ALU.add)
                            for (a0, nr, rl) in ((0, 1, 256), (256, 64, 64)):
                                xv = xraw[:, a0:a0 + nr * rl].rearrange("p (r t) -> p r t", t=rl)
                                ov = xcv[:, a0:a0 + nr * rl].rearrange("p (r t) -> p r t", t=rl)
                                K.stt(ov[:, :, 1:rl], xv[:, :, 0:rl - 1], w[0], ov[:, :, 1:rl], ALU.mult, ALU.add)
                                K.stt(ov[:, :, 0:rl - 1], xv[:, :, 1:rl], w[2], ov[:, :, 0:rl - 1], ALU.mult, ALU.add)
                                K.stt(ov[:, :, 0:rl - 2], xv[:, :, 2:rl], w[3], ov[:, :, 0:rl - 2], ALU.mult, ALU.add)
                            K.act(S(xbc[:, j, :], j), xcv[:], AF.Silu)
                        K.barrier()
                        st2.close()
                        XBC = [S(xbc[:, j, :], j) for j in range(6)]
                        load_w([(C_DT, 16), (C_MZ, 512)])
                        dtb = [rowb(st, f"m2_dtb{l}{d}", "dtb%d" % d) for d in range(2)]
                        alg = [rowb(st, f"m2_alog{l}{d}", "alg%d" % d) for d in range(2)]
                        m2d = rowb(st, f"m2_d{l}", "m2d")
                        m2n = rowb(st, f"m2_norm{l}", "m2n")
                        for d in range(2):
                            K.act(alg[d][:], alg[d][:], AF.Exp)
                            K.ts(alg[d][:], alg[d][:], -1.0, ALU.mult)

                        def tl(name, shape, dt=F32, n=2):
                            return [sb(st, "%s%d" % (name, i), list(shape), dt) for i in range(n)]
                        dt_, dta, ecum, wdt = tl("dt", [128, 8]), tl("dta", [128, 8]), tl("ecum", [128, 8]), tl("wdt", [128, 8])
                        Rm = tl("Rm", [128, 8, 128])
                        exs = tl("exs", [128, 8, 128])
                        mdt = tl("mdt", [128, 8, 128])
                        Wt = tl("Wt", [128, 8, 128], BF16)
                        xtm = tl("xtm", [128, 640], BF16)
                        xw = tl("xw", [128, 512], BF16)
                        edec = tl("edec", [128, 4], F32, 4)
                        Sst = sb(st, "mS", [128, 256])
                        Sb = tl("mSb", [128, 256], BF16)
                        ya = tl("ya", [128, 512])
                        yb = tl("yb", [128, 512])
                        zt = tl("zt", [128, 512])
                        y2 = tl("y2", [128, 512])
                        ybf = tl("ybf", [128, 512], BF16)
                        yfm = tl("yfm", [128, 4, 128], BF16)
                        ssq = tl("ssq", [128, 1])
                        for d in range(2):
                            order = list(range(NT)) if d == 0 else [1, 0] + list(range(NT - 1, 1, -1))
                            K.memset(Sst[:], 0.0)
                            K.memset(Sb[0][:], 0.0)
                            si_m = [0]

                            def emit_MX(it, d=d, order=order):
                                p = order[it]
                                b_ = it % 2
                                t0 = p * 128
                                psA = PS[it % 2]
                                psS = [PS[2], PS[3]]
                                psC = PS[4 + it % 2]
                                psY = PS[6]
                                psZ = PS[7]
                                proj_tm(psA, d * 8, 8, t0, pc0=0)
                                K.tt(dt_[b_][:], psA[:, 0:8], dtb[d][:], ALU.add)
                                K.act(dt_[b_][:], dt_[b_][:], AF.Exp)
                                K.act(dt_[b_][:], dt_[b_][:], AF.Ln, bias=1.0)
                                K.tt(dta[b_][:], dt_[b_][:], alg[d][:], ALU.mult)
                                psT = psA[:, 128:512].bitcast(BF16)
                                for j in range(5):
                                    K.op('pe', [psA], [XBC[j], identb[:]], lambda g, j=j, t0=t0, psT=psT: g.transpose(psT[:, j * 128:(j + 1) * 128], xbc[:, j, t0:t0 + 128], identb[:]))
                                K.op('act', [xtm[b_][:]], [psA], lambda g, b_=b_, psT=psT: g.activation(out=xtm[b_][:], in_=psT[:, 0:640], func=AF.Copy))
                                K.tt(Rm[b_][:], TRI(d).rearrange("p (o i) -> p o i", o=1).to_broadcast([128, 8, 128]),
                                     dta[b_][:].rearrange("p (h o) -> p h o", o=1).to_broadcast([128, 8, 128]), ALU.mult, e='pool')
                                for hh in range(2):
                                    K.mm(psS[hh][:], SU(d), Rm[b_][:, hh * 4:(hh + 1) * 4, :].rearrange("p h i -> p (h i)"))
                                    K.act(exs[b_][:, hh * 4:(hh + 1) * 4, :].rearrange("p h i -> p (h i)"), psS[hh][:], AF.Exp)
                                K.mm(psA[:, 8:16], TRI(d), dta[b_][:])
                                K.act(ecum[b_][:], psA[:, 8:16], AF.Exp)
                                for g in range(2):
                                    K.op('pe', [S(psC[:, g * 128:(g + 1) * 128], 'sc')], [XBC[4], XBC[5]],
                                         lambda e_, g=g, t0=t0, psC=psC: e_.matmul(psC[:, g * 128:(g + 1) * 128], lhsT=xbc[64 * g:64 * g + 64, 4, t0:t0 + 128],
                                                                              rhs=xbc[64 * g:64 * g + 64, 5, t0:t0 + 128], start=True, stop=True), pesync=True)
                                K.tt(mdt[b_][:], TRI(d).rearrange("p (o i) -> p o i", o=1).to_broadcast([128, 8, 128]),
                                     dt_[b_][:].rearrange("p (h o) -> p h o", o=1).to_broadcast([128, 8, 128]), ALU.mult, e='pool')
                                K.tt(mdt[b_][:], mdt[b_][:], exs[b_][:], ALU.mult, e='pool')
                                for g in range(2):
                                    K.tt(Wt[b_][:, g * 4:(g + 1) * 4, :], mdt[b_][:, g * 4:(g + 1) * 4, :],
                                         S(psC[:, g * 128:(g + 1) * 128].rearrange("p (o i) -> p o i", o=1).to_broadcast([128, 4, 128]), 'sc'), ALU.mult)
                                chunks = [(0, 63), (64, 127)] if d == 0 else [(64, 64), (0, 0)]
                                for (c0, lastcol) in chunks:
                                    K.tt(wdt[b_][c0:c0 + 64, :], exs[b_][c0:c0 + 64, :, lastcol], dt_[b_][c0:c0 + 64, :], ALU.mult)
                                K.tt(xw[b_][:].rearrange("p (h q) -> p h q", q=64), xtm[b_][:, 0:512].rearrange("p (h q) -> p h q", q=64),
                                     wdt[b_][:].rearrange("p (h o) -> p h o", o=1).to_broadcast([128, 8, 64]), ALU.mult)

                            def emit_MY(it, d=d, order=order):
                                p = order[it]
                                b_ = it % 2
                                t0 = p * 128
                                psA = PS[it % 2]
                                psC = PS[4 + it % 2]
                                psY = PS[6]
                                psZ = PS[7]
                                si = si_m[0]
                                chunks = [(0, 63), (64, 127)] if d == 0 else [(64, 64), (0, 0)]
                                for hd in range(8):
                                    K.mm(psY[:, hd * 64:(hd + 1) * 64], Wt[b_][:, hd, :], xtm[b_][:, hd * 64:(hd + 1) * 64])
                                for ci, (c0, lastcol) in enumerate(chunks):
                                    for g in range(2):
                                        K.op('pe', [S(psZ[c0:c0 + 64, g * 256:(g + 1) * 256], 'z')], [XBC[5], Sb[si % 2][:]],
                                             lambda e_, g=g, c0=c0, t0=t0, s_=Sb[si % 2]: e_.matmul(psZ[c0:c0 + 64, g * 256:(g + 1) * 256], lhsT=xbc[64 * g:64 * g + 64, 5, t0 + c0:t0 + c0 + 64],
                                                                                                 rhs=s_[64 * g:64 * g + 64, :], start=True, stop=True), pesync=True)
                                    for g in range(2):
                                        K.op('pe', [S(psC[64 * g:64 * g + 64, 256:512], 'cs')], [xtm[b_][:], xw[b_][:]],
                                             lambda e_, g=g, c0=c0, b_=b_, psC=psC: e_.matmul(psC[64 * g:64 * g + 64, 256:512], lhsT=xtm[b_][c0:c0 + 64, 512 + 64 * g:512 + 64 * g + 64],
                                                                                            rhs=xw[b_][c0:c0 + 64, g * 256:(g + 1) * 256], start=True, stop=True), pesync=True)
                                        K.op('pe', [S(psA[64 * g:64 * g + 64, 16 + 4 * ci:20 + 4 * ci], 'ed')], [dta[b_][:]],
                                             lambda e_, g=g, c0=c0, ci=ci, b_=b_, psA=psA: e_.matmul(psA[64 * g:64 * g + 64, 16 + 4 * ci:20 + 4 * ci], lhsT=cst[c0:c0 + 64, CI_ALL, 0:64],
                                                                                                   rhs=dta[b_][c0:c0 + 64, 4 * g:4 * g + 4], start=True, stop=True), pesync=True)
                                    ed = edec[(it * 2 + ci) % 4]
                                    K.act(ed[:], S(psA[:, 16 + 4 * ci:20 + 4 * ci], 'ed'), AF.Exp)
                                    K.tt(Sst[:].rearrange("p (r q) -> p r q", q=64), Sst[:].rearrange("p (r q) -> p r q", q=64),
                                         ed[:].rearrange("p (r o) -> p r o", o=1).to_broadcast([128, 4, 64]), ALU.mult)
                                    K.tt(Sst[:], Sst[:], S(psC[:, 256:512], 'cs'), ALU.add)
                                    si += 1
                                    si_m[0] = si
                                    K.copy(Sb[si % 2][:], Sst[:], e='act')
                                K.tt(ya[b_][:].rearrange("p (h q) -> p h q", q=64), S(psZ[:].rearrange("p (h q) -> p h q", q=64), 'z'),
                                     ecum[b_][:].rearrange("p (h o) -> p h o", o=1).to_broadcast([128, 8, 64]), ALU.mult)
                                K.tt(ya[b_][:], ya[b_][:], psY[:], ALU.add)
                                if d == 0:
                                    K.tt(yb[b_][:].rearrange("p (h q) -> p h q", q=64), xtm[b_][:, 0:512].rearrange("p (h q) -> p h q", q=64),
                                         m2d[:].rearrange("p (h o) -> p h o", o=1).to_broadcast([128, 8, 64]), ALU.mult, e='pool')
                                    K.tt(ya[b_][:], ya[b_][:], yb[b_][:], ALU.add, e='pool')
                                    K.dma('sp', S(ym_d[t0:t0 + 128, :], p), ya[b_][:])
                                else:
                                    K.dma('sp', yb[b_][:], S(ym_d[t0:t0 + 128, :], p))
                                    K.tt(ya[b_][:], ya[b_][:], yb[b_][:], ALU.add, e='pool')
                                    proj_tm(psY, 16, 512, t0, pc0=0)
                                    K.act(zt[b_][:], psY[:], AF.Silu)
                                    K.tt(ya[b_][:], ya[b_][:], zt[b_][:], ALU.mult)
                                    K.act(y2[b_][:], ya[b_][:], AF.Square, accum=ssq[b_][:])
                                    K.rsqrt(ssq[b_][:], ssq[b_][:], 1.0 / 512)
                                    K.stt(ybf[b_][:], ya[b_][:], ssq[b_][:, 0:1], m2n[:], ALU.mult, ALU.mult)
                                    psT2 = psZ[:, 0:256].bitcast(BF16)
                                    for j in range(4):
                                        K.op('pe', [S(psZ[:], 'z')], [ybf[b_][:], identb[:]],
                                             lambda g, j=j, b_=b_, psT2=psT2: g.transpose(psT2[:, j * 128:(j + 1) * 128], ybf[b_][:, j * 128:(j + 1) * 128], identb[:]))
                                    K.op('act', [yfm[b_][:]], [S(psZ[:], 'z')], lambda g, b_=b_, psT2=psT2: g.activation(out=yfm[b_][:].rearrange("p j t -> p (j t)"), in_=psT2, func=AF.Copy))
                                    K.dma('sp', S(ys_d[3, :, :, t0:t0 + 128], (3, p)), yfm[b_][:])

                            emit_MX(0)
                            for it in range(NT):
                                if it + 1 < NT:
                                    emit_MX(it + 1)
                                emit_MY(it)
                    K.barrier()
                p15.close()
                K.barrier()

                if 'G' in phases:
                    with contextlib.ExitStack() as st:
                        wg = sb(st, "wg", [128, 4, 8, D], BF16)
                        wb = sb(st, "wb", [128, 4, 4, D], BF16)
                        wo = sb(st, "wo", [128, 8, D], BF16)
                        NSTG = 3
                        stg = [sb(st, "stg%d" % i, [128, 1024]) for i in range(NSTG)]
                        dq = ['sp', 'act', 'pool']
                        ce = ['dve', 'act', 'dve']
                        n_ = 0
                        for k in range(4):
                            for kt in range(8):
                                s_ = stg[n_ % NSTG]; q_ = dq[n_ % NSTG]; c_ = ce[n_ % NSTG]; n_ += 1
                                K.dma(q_, s_[:], wgate_d[l, k, kt * 128:(kt + 1) * 128, :])
                                K.copy(wg[:, k, kt, :], s_[:], e=c_)
                            for kt in range(4):
                                s_ = stg[n_ % NSTG]; q_ = dq[n_ % NSTG]; c_ = ce[n_ % NSTG]; n_ += 1
                                K.dma(q_, s_[:], wbr_d[l, k, kt * 128:(kt + 1) * 128, :])
                                K.copy(wb[:, k, kt, :], s_[:], e=c_)
                        for kt in range(8):
                            s_ = stg[n_ % NSTG]; q_ = dq[n_ % NSTG]; c_ = ce[n_ % NSTG]; n_ += 1
                            K.dma(q_, s_[:], wout_d[l, kt * 128:(kt + 1) * 128, :])
                            K.copy(wo[:, kt, :], s_[:], e=c_)
                        gtrow = sb(st, "gtrow", [128, 2, D])
                        gtb = rowb(st, f"gtb{l}", "gtbrow")
                        lb_ = sb(st, "lbc_", [128, 8, 128])
                        fnr = rowb(st, "final_norm", "fnr") if last else None
                        for cc in range(2):
                            for kt in range(8):
                                K.copy(lb_[:, kt, :], silc[:, kt, cc:cc + 1].to_broadcast([128, 128]))
                            for hf in range(2):
                                ps = PS[hf]
                                for kt in range(8):
                                    s_ = stg[n_ % NSTG]; q_ = dq[n_ % NSTG]; c_ = ce[n_ % NSTG]; n_ += 1
                                    K.dma(q_, s_[:, 0:512], wmod_d[l, kt * 128:(kt + 1) * 128, 2048 + hf * 512:2048 + (hf + 1) * 512])
                                    K.mm(ps[:], lb_[:, kt, :], s_[:, 0:512], start=(kt == 0), stop=(kt == 7))
                                K.tt(gtrow[:, cc, hf * 512:(hf + 1) * 512], ps[:], gtb[:, hf * 512:(hf + 1) * 512], ALU.add)

                        def tl(name, shape, dt=F32, n=2):
                            return [sb(st, "%s%d" % (name, i), list(shape), dt) for i in range(n)]
                        hsl = tl("hsl", [128, 8, 256], BF16)
                        ysl = tl("ysl", [128, 4, 4, 256], BF16)
                        macc = tl("macc", [128, 256], F32, 2)
                        gsg = tl("gsg", [128, 256], F32, 2)
                        mT = tl("mT", [128, 8, 256], BF16)
                        xrow = tl("xrow", [128, D], F32, 2)
                        orow = tl("orow", [128, D], F32, 2)
                        jk = stg[0]
                        ssq = tl("fss", [128, 1])
                        for si_, (t0, n) in enumerate(SLABS256):
                            if last and si_ == 0:
                                continue
                            b_ = si_ % 2
                            cc = 1 if si_ == 0 else 0
                            K.dma('sp', hsl[b_][:], hs_d[:, :, t0:t0 + n])
                            for k in range(4):
                                K.dma('pool', ysl[b_][:, k], ys_d[k, :, :, t0:t0 + n])
                            for oc in range(8):
                                for k in range(4):
                                    psg = PS[(oc * 4 + k) % 2]
                                    psb = PS[2 + (oc * 4 + k) % 2]
                                    for kt in range(8):
                                        K.mm(psg[:, 0:n], wg[:, k, kt, oc * 128:(oc + 1) * 128], hsl[b_][:, kt, :], start=(kt == 0), stop=(kt == 7))
                                    for kt in range(4):
                                        K.mm(psb[:, 0:n], wb[:, k, kt, oc * 128:(oc + 1) * 128], ysl[b_][:, k, kt, :], start=(kt == 0), stop=(kt == 3))
                                    g_ = gsg[(oc * 4 + k) % 2]
                                    K.act(g_[:], psg[:, 0:n], AF.Sigmoid, bias=col(f"b_gate{l}{k}", oc))
                                    m_ = macc[oc % 2]
                                    if k == 0:
                                        K.tt(m_[:], g_[:], psb[:, 0:n], ALU.mult)
                                    else:
                                        K.tt(g_[:], g_[:], psb[:, 0:n], ALU.mult)
                                        K.tt(m_[:], m_[:], g_[:], ALU.add, e='pool')
                                K.copy(mT[b_][:, oc, :], macc[oc % 2][:], e='act')
                            for tt_ in range(2):
                                r0 = t0 + tt_ * 128
                                xr_ = xrow[tt_]
                                K.dma('pool', xr_[:], xin_d[r0:r0 + 128, :])
                                for hf in range(2):
                                    ps = PS[4 + hf]
                                    for kt in range(8):
                                        K.mm(ps[:], mT[b_][:, kt, tt_ * 128:(tt_ + 1) * 128], wo[:, kt, hf * 512:(hf + 1) * 512], start=(kt == 0), stop=(kt == 7))
                                    K.tt(orow[tt_][:, hf * 512:(hf + 1) * 512], ps[:], gtrow[:, cc, hf * 512:(hf + 1) * 512], ALU.mult)
                                K.tt(orow[tt_][:], orow[tt_][:], xr_[:], ALU.add, e='pool')
                                if not last:
                                    K.dma('sp', x1_d[r0:r0 + 128, :], orow[tt_][:])
                                else:
                                    s2 = ssq[tt_]
                                    K.act(jk[:], orow[tt_][:], AF.Square, accum=s2[:])
                                    K.rsqrt(s2[:], s2[:], 1.0 / D)
                                    K.stt(orow[tt_][:], orow[tt_][:], s2[:, 0:1], fnr[:], ALU.mult, ALU.mult)
                                    K.dma('sp', out_d[r0 - NCTX:r0 - NCTX + 128, :], orow[tt_][:])
                    K.barrier()
        K.barrier()
        print("instructions:", K.ninst, {e: K.cnt[e] for e in K.cnt}, "sems", K.nsem)
    return nc


def make_inputs(inp):
    shared, coff, roff = prep_shared(inp)
    maps = []
    for b in range(8):
        m = dict(shared)
        m["xc"] = np.ascontiguousarray(np.concatenate([inp["ctx"][b], inp["x"][b]], axis=0), np.float32)
        cc = np.stack([np.asarray(inp["c"][b], np.float32).reshape(8, 128).T, np.asarray(inp["c_ctx"], np.float32).reshape(8, 128).T], axis=-1)
        m["ccol"] = np.ascontiguousarray(cc, np.float32)
        maps.append(m)
    return shared, coff, roff, maps


def kernel(**inputs):
    inp = {k: np.asarray(v) for k, v in inputs.items()}
    shared, coff, roff, maps = make_inputs(inp)
    nc = build(shared, coff, roff)
    res = run_bass_kernel_spmd(nc, maps, core_ids=list(range(8)))
    out = np.stack([np.asarray(r["out"], np.float32) for r in res.results], axis=0)
    return out
```
